# Optimizing a Trainium2 kernel written in Bass

```python
import math
import jax
import jax.numpy as jnp
from jax import lax
import numpy as np

D_MODEL = 1024
BATCH = 16
SEQ = 4096
DEPTH = 4

HEAD_DIM = 128
MIX_WIDTH = D_MODEL
HALF_WIDTH = MIX_WIDTH // 2
RET_HEADS = HALF_WIDTH // HEAD_DIM
RET_WIDTH = RET_HEADS * HEAD_DIM
RET_CHUNK = 128
POOL_WINDOWS = (2, 4, 8, 16)
POOL_WIDTH = HALF_WIDTH
POOL_GROUP = POOL_WIDTH // len(POOL_WINDOWS)
LRU_WIDTH = HALF_WIDTH
LRU_BLOCKS = 8
LRU_C = 8.0
CONV_WIDTH = 4
ATT_HEADS = HALF_WIDTH // HEAD_DIM
ATT_WIDTH = ATT_HEADS * HEAD_DIM
DIL_PATTERNS = ((128, 1), (512, 4), (2048, 16))
ATT_BLOCK = 128
ROPE_THETA = 10000.0
EVEN_IN = 4 * RET_WIDTH + POOL_WIDTH
ODD_IN = 2 * LRU_WIDTH + 3 * ATT_WIDTH
D_FF = -(-8 * D_MODEL // (3 * 256)) * 256
DEEPNORM_ALPHA = (2 * DEPTH) ** 0.25
DEEPNORM_BETA = (8 * DEPTH) ** -0.25
LN_EPS = 1e-5

kernel_name = "retnet_pool_griffin_longnet_hybrid"

F32 = jnp.float32


def layer_norm(x, g, b):
    xf = x.astype(F32)
    mu = jnp.mean(xf, -1, keepdims=True)
    var = jnp.mean(jnp.square(xf - mu), -1, keepdims=True)
    return ((xf - mu) * lax.rsqrt(var + LN_EPS) * g + b).astype(x.dtype)


def rope_tables(positions, dim):
    inv = ROPE_THETA ** (-jnp.arange(0, dim, 2, dtype=F32) / dim)
    ang = positions.astype(F32)[..., None] * inv
    return jnp.cos(ang), jnp.sin(ang)


def apply_rope(t, cos, sin):
    t1, t2 = jnp.split(t.astype(F32), 2, axis=-1)
    c = cos[:, :, None, :]
    s = sin[:, :, None, :]
    return jnp.concatenate([t1 * c - t2 * s, t2 * c + t1 * s], axis=-1).astype(t.dtype)


def retention(q, k, v):
    B, S, H, dh = q.shape
    C = RET_CHUNK
    nc = S // C
    lg = jnp.log1p(-(2.0 ** (-5.0 - jnp.arange(H, dtype=F32))))
    idx = jnp.arange(C, dtype=F32)
    qc = q.astype(F32).reshape(B, nc, C, H, dh)
    kc = (k.astype(F32) * dh ** -0.5).reshape(B, nc, C, H, dh)
    vc = v.astype(F32).reshape(B, nc, C, H, dh)
    rel = idx[:, None] - idx[None, :]
    decay = jnp.where(rel[None] >= 0, jnp.exp(jnp.maximum(rel, 0.0)[None] * lg[:, None, None]), 0.0)
    scores = jnp.einsum('bnihd,bnjhd->bnhij', qc, kc) * decay
    inner = jnp.einsum('bnhij,bnjhe->bnihe', scores, vc)
    k_decay = jnp.exp((C - 1 - idx)[None, :] * lg[:, None])
    kv = jnp.einsum('bnjhd,hj,bnjhe->nbhde', kc, k_decay, vc)
    chunk_decay = jnp.exp(C * lg)[:, None, None]

    def step(state, kv_n):
        return state * chunk_decay + kv_n, state

    _, prev = lax.scan(step, jnp.zeros((B, H, dh, dh), F32), kv)
    q_decay = jnp.exp((idx + 1.0)[None, :] * lg[:, None])
    cross = jnp.einsum('bnihd,nbhde,hi->bnihe', qc, prev, q_decay)
    return (inner + cross).reshape(B, S, H, dh)


def head_group_norm(y, g):
    mu = jnp.mean(y, -1, keepdims=True)
    var = jnp.mean(jnp.square(y - mu), -1, keepdims=True)
    yn = (y - mu) * lax.rsqrt(var + LN_EPS)
    B, S, H, dh = y.shape
    return yn.reshape(B, S, H * dh) * g


def multiscale_pool(p, pool_w, pool_scale):
    B, S, _ = p.shape
    pg = p.astype(F32).reshape(B, S, len(POOL_WINDOWS), POOL_GROUP)
    cs = jnp.cumsum(pg, axis=1)
    pos1 = jnp.arange(1, S + 1)
    outs = []
    for gi, w in enumerate(POOL_WINDOWS):
        c = cs[:, :, gi]
        c_prev = jnp.pad(c, ((0, 0), (w, 0), (0, 0)))[:, :S]
        cnt = jnp.minimum(pos1, w).astype(F32)[None, :, None]
        outs.append((c - c_prev) / cnt - pg[:, :, gi])
    pooled = jnp.stack(outs, axis=2)
    mixed = jnp.einsum('bsgc,gcd->bsgd', pooled, pool_w)
    return mixed.reshape(B, S, POOL_WIDTH) * pool_scale


def causal_depthwise_conv(u, w, b):
    y = lax.conv_general_dilated(
        u, w.astype(u.dtype)[:, None, :], window_strides=(1,),
        padding=[(CONV_WIDTH - 1, 0)], dimension_numbers=('NWC', 'WIO', 'NWC'),
        feature_group_count=u.shape[-1])
    return y + b


def rg_lru(u, w_a, b_a, w_x, b_x, lam):
    B, S, R = u.shape
    uf = u.astype(F32)
    ub = uf.reshape(B, S, LRU_BLOCKS, R // LRU_BLOCKS)
    r = jax.nn.sigmoid(jnp.einsum('bsnc,ncd->bsnd', ub, w_a).reshape(B, S, R) + b_a)
    i = jax.nn.sigmoid(jnp.einsum('bsnc,ncd->bsnd', ub, w_x).reshape(B, S, R) + b_x)
    log_a = -LRU_C * r * jax.nn.softplus(-lam.astype(F32))
    a = jnp.exp(log_a)
    bseq = jnp.sqrt(-jnp.expm1(2.0 * log_a)) * (i * uf)

    def combine(left, right):
        a1, b1 = left
        a2, b2 = right
        return a1 * a2, a2 * b1 + b2

    _, h = lax.associative_scan(combine, (a, bseq), axis=1)
    return h


def banded_window_attn(q, k, v, n_back):
    N, L, H, dh = q.shape
    QB = ATT_BLOCK
    nb = -(-L // QB)
    Lp = nb * QB
    q = jnp.pad(q.astype(F32), ((0, 0), (0, Lp - L), (0, 0), (0, 0)))
    kp = jnp.pad(k.astype(F32), ((0, 0), (QB, Lp - L), (0, 0), (0, 0)))
    vp = jnp.pad(v.astype(F32), ((0, 0), (QB, Lp - L), (0, 0), (0, 0)))
    qb = q.reshape(N, nb, QB, H, dh)
    kb = jnp.concatenate([kp[:, :Lp].reshape(N, nb, QB, H, dh), kp[:, QB:].reshape(N, nb, QB, H, dh)], axis=2)
    vb = jnp.concatenate([vp[:, :Lp].reshape(N, nb, QB, H, dh), vp[:, QB:].reshape(N, nb, QB, H, dh)], axis=2)
    s = jnp.einsum('nbqhd,nbkhd->nbhqk', qb, kb)
    qpos = jnp.arange(QB)[:, None] + QB
    kpos = jnp.arange(2 * QB)[None, :]
    rel = qpos - kpos
    band = (rel >= 0) & (rel <= n_back)
    valid = band[None] & ((jnp.arange(nb)[:, None, None] > 0) | (kpos >= QB)[None])
    s = jnp.where(valid[None, :, None], s, -jnp.inf)
    m = jnp.max(s, axis=-1, keepdims=True)
    p = jnp.exp(s - m)
    den = jnp.sum(p, axis=-1, keepdims=True)
    o = jnp.einsum('nbhqk,nbkhd->nbqhd', p / den, vb).reshape(N, Lp, H, dh)[:, :L]
    lse = (m + jnp.log(den))[..., 0]
    lse = jnp.transpose(lse, (0, 1, 3, 2)).reshape(N, Lp, H)[:, :L]
    return o, lse


def dilated_attention(q, k, v):
    B, S, H, dh = q.shape
    q = q * dh ** -0.5
    outs, lses = [], []
    for window, dil in DIL_PATTERNS:
        Ld = S // dil

        def to_strided(t):
            return t.reshape(B, Ld, dil, H, dh).transpose(0, 2, 1, 3, 4).reshape(B * dil, Ld, H, dh)

        o, lse = banded_window_attn(to_strided(q), to_strided(k), to_strided(v), window // dil)
        outs.append(o.reshape(B, dil, Ld, H, dh).transpose(0, 2, 1, 3, 4).reshape(B, S, H, dh))
        lses.append(lse.reshape(B, dil, Ld, H).transpose(0, 2, 1, 3).reshape(B, S, H))
    wts = jax.nn.softmax(jnp.stack(lses, axis=0), axis=0)
    return jnp.sum(wts[..., None] * jnp.stack(outs, axis=0), axis=0)


def even_mixer(x, cos, sin, w_in, ret_norm_g, pool_w, pool_scale, w_out):
    B, S, _ = x.shape
    z = x @ w_in
    q, k, v, g, p = jnp.split(z, [RET_WIDTH, 2 * RET_WIDTH, 3 * RET_WIDTH, 4 * RET_WIDTH], axis=-1)
    heads = lambda t: t.reshape(B, S, RET_HEADS, HEAD_DIM)
    ret = retention(apply_rope(heads(q), cos, sin), apply_rope(heads(k), cos, sin), heads(v))
    ret = head_group_norm(ret, ret_norm_g) * jax.nn.silu(g.astype(F32))
    pool = multiscale_pool(p, pool_w, pool_scale)
    cat = jnp.concatenate([ret, pool], axis=-1).astype(x.dtype)
    return cat @ w_out


def odd_mixer(x, cos, sin, w_in, conv_w, conv_b, gate_a_w, gate_a_b, gate_x_w, gate_x_b, lru_lambda, w_out):
    B, S, _ = x.shape
    z = x @ w_in
    gate_in, u, q, k, v = jnp.split(
        z, [LRU_WIDTH, 2 * LRU_WIDTH, 2 * LRU_WIDTH + ATT_WIDTH, 2 * LRU_WIDTH + 2 * ATT_WIDTH], axis=-1)
    u = causal_depthwise_conv(u, conv_w, conv_b)
    y_lru = rg_lru(u, gate_a_w, gate_a_b, gate_x_w, gate_x_b, lru_lambda) * jax.nn.gelu(gate_in.astype(F32))
    heads = lambda t: t.reshape(B, S, ATT_HEADS, HEAD_DIM)
    y_att = dilated_attention(apply_rope(heads(q), cos, sin), apply_rope(heads(k), cos, sin), heads(v))
    cat = jnp.concatenate([y_lru, y_att.reshape(B, S, ATT_WIDTH)], axis=-1).astype(x.dtype)
    return cat @ w_out


def swiglu(x, w_in, w_out):
    gate, up = jnp.split(x @ w_in, 2, axis=-1)
    return (jax.nn.silu(gate) * up) @ w_out


def setup_inputs(seed: int = 0) -> dict:
    key = jax.random.key(seed)
    ks = jax.random.split(key, 24)
    D = D_MODEL
    ne = (DEPTH + 1) // 2
    no = DEPTH // 2
    bw = LRU_WIDTH // LRU_BLOCKS

    def nrm(k, shape, scale):
        return jax.random.normal(k, shape, F32) * scale

    lam_u = jax.random.uniform(ks[14], (no, LRU_WIDTH), F32, minval=0.9, maxval=0.999)
    s = lam_u ** (1.0 / LRU_C)
    lru_lambda = jnp.log(s) - jnp.log1p(-s)
    positions = jnp.broadcast_to(jnp.arange(SEQ, dtype=jnp.int32)[None, :], (BATCH, SEQ))
    return {
        "x": nrm(ks[0], (BATCH, SEQ, D), 1.0),
        "positions": positions,
        "ev_w_in": nrm(ks[1], (ne, D, EVEN_IN), D ** -0.5),
        "ev_ret_norm_g": 1.0 + nrm(ks[2], (ne, RET_WIDTH), 0.02),
        "ev_pool_w": nrm(ks[3], (ne, len(POOL_WINDOWS), POOL_GROUP, POOL_GROUP), POOL_GROUP ** -0.5),
        "ev_pool_scale": 1.0 + nrm(ks[4], (ne, POOL_WIDTH), 0.02),
        "ev_w_out": nrm(ks[5], (ne, MIX_WIDTH, D), MIX_WIDTH ** -0.5 * DEEPNORM_BETA),
        "od_w_in": nrm(ks[6], (no, D, ODD_IN), D ** -0.5),
        "od_conv_w": nrm(ks[7], (no, CONV_WIDTH, LRU_WIDTH), CONV_WIDTH ** -0.5),
        "od_conv_b": nrm(ks[8], (no, LRU_WIDTH), 0.01),
        "od_gate_a_w": nrm(ks[9], (no, LRU_BLOCKS, bw, bw), bw ** -0.5),
        "od_gate_a_b": nrm(ks[10], (no, LRU_WIDTH), 0.01),
        "od_gate_x_w": nrm(ks[11], (no, LRU_BLOCKS, bw, bw), bw ** -0.5),
        "od_gate_x_b": nrm(ks[12], (no, LRU_WIDTH), 0.01),
        "od_lru_lambda": lru_lambda,
        "od_w_out": nrm(ks[13], (no, MIX_WIDTH, D), MIX_WIDTH ** -0.5 * DEEPNORM_BETA),
        "ffn_w_in": nrm(ks[15], (DEPTH, D, 2 * D_FF), D ** -0.5),
        "ffn_w_out": nrm(ks[16], (DEPTH, D_FF, D), D_FF ** -0.5 * DEEPNORM_BETA),
        "ln_g": 1.0 + nrm(ks[17], (DEPTH, 2, D), 0.02),
        "ln_b": nrm(ks[18], (DEPTH, 2, D), 0.02),
    }


def reference(x, positions, ev_w_in, ev_ret_norm_g, ev_pool_w, ev_pool_scale, ev_w_out,
              od_w_in, od_conv_w, od_conv_b, od_gate_a_w, od_gate_a_b, od_gate_x_w, od_gate_x_b,
              od_lru_lambda, od_w_out, ffn_w_in, ffn_w_out, ln_g, ln_b):
    cos, sin = rope_tables(positions, HEAD_DIM)
    h = x
    for layer in range(DEPTH):
        j = layer // 2
        if layer % 2 == 0:
            mix = even_mixer(h, cos, sin, ev_w_in[j], ev_ret_norm_g[j], ev_pool_w[j], ev_pool_scale[j], ev_w_out[j])
        else:
            mix = odd_mixer(h, cos, sin, od_w_in[j], od_conv_w[j], od_conv_b[j], od_gate_a_w[j], od_gate_a_b[j],
                            od_gate_x_w[j], od_gate_x_b[j], od_lru_lambda[j], od_w_out[j])
        h = layer_norm(DEEPNORM_ALPHA * h + mix.astype(h.dtype), ln_g[layer, 0], ln_b[layer, 0])
        h = layer_norm(DEEPNORM_ALPHA * h + swiglu(h, ffn_w_in[layer], ffn_w_out[layer]), ln_g[layer, 1], ln_b[layer, 1])
    return h
```

```python
import contextlib
import math
import numpy as np
import concourse.bass as bass
import concourse.mybir as mybir
from concourse.bass_utils import run_bass_kernel_spmd

F32 = mybir.dt.float32
BF16 = mybir.dt.bfloat16
I32 = mybir.dt.int32
AF = mybir.ActivationFunctionType
ALU = mybir.AluOpType
AX = mybir.AxisListType

D = 1024
DFF = 2816
NH = 4
DH = 128
TB = 512
NJ = TB // 128
DEPTH = 4
ALPHA = float((2 * DEPTH) ** 0.25)
EPS = 1e-5
POOL_W = (2, 4, 8, 16)
DIL_PATTERNS = ((128, 1), (512, 4), (2048, 16))
EPOCH = 30000

C_INVF, C_SGN, C_ID, C_DEC, C_KDEC, C_CDEC, C_QDEC, C_MASK, C_PCNT, C_END = (
    0, 1, 2, 130, 642, 1154, 1666, 3714, 3970, 4034)


class Buf:
    __slots__ = ("name", "t", "excl", "lw", "rd", "rd_dma")

    def __init__(self, name, t=None, excl=False):
        self.name = name
        self.t = t
        self.excl = excl
        self.lw = None
        self.rd = {}
        self.rd_dma = []

    def __getitem__(self, k):
        return self.t[k]


class DSem:
    __slots__ = ("sem", "cnt")

    def __init__(self, sem):
        self.sem = sem
        self.cnt = 0


class Op:
    __slots__ = ("eng", "fn", "deps", "sig", "sem", "val", "isdma", "dsem")

    def __init__(self, eng, fn, isdma=False, dsem=None):
        self.eng = eng
        self.fn = fn
        self.deps = ()
        self.sig = False
        self.sem = None
        self.val = 0
        self.isdma = isdma
        self.dsem = dsem


class Prog:
    ENGS = ("pe", "act", "dve", "pool", "sp")

    def __init__(self, nc):
        self.nc = nc
        self.es = contextlib.ExitStack()
        self.scope = self.es
        self.ops = []
        self.eng = {"pe": nc.tensor, "act": nc.scalar, "dve": nc.vector, "pool": nc.gpsimd, "sp": nc.sync}
        self.nsem = 0
        self.nbuf = 0
        self.cnt = {}
        self.cursem = {}
        self.waited = {e: {} for e in self.ENGS}
        self.pending_dma = []
        self.nwait = 0
        self.ninst = 0
        self.dsem_pool = []
        self.phase_dsems = []
        self.bar_t = self.es.enter_context(self.nc.sbuf_tensor("bar_scr", [128, 8], F32))

    def sbuf(self, name, shape, dt):
        self.nbuf += 1
        t = self.scope.enter_context(self.nc.sbuf_tensor(f"{name}_{self.nbuf}", list(shape), dt))
        return Buf(name, t)

    def psum(self, name, shape, dt):
        self.nbuf += 1
        t = self.scope.enter_context(self.nc.psum_tensor(f"{name}_{self.nbuf}", list(shape), dt))
        return Buf(name, t, excl=True)

    def dram(self, name, shape, dt, kind="Internal"):
        t = self.nc.dram_tensor(name, list(shape), dt, kind=kind)
        return Buf(name, t.ap())

    def new_sem(self, name=None):
        self.nsem += 1
        return self.es.enter_context(self.nc.semaphore(name or f"s{self.nsem}"))

    def dsem(self):
        d = self.dsem_pool.pop() if self.dsem_pool else DSem(self.new_sem())
        self.phase_dsems.append(d)
        return d

    def end_phase(self):
        self.barrier()
        self.flush()
        self.dsem_pool.extend(self.phase_dsems)
        self.phase_dsems = []

    def _add(self, op, reads, writes):
        eng = op.eng
        deps = set()
        for b in reads:
            if b.lw is not None:
                deps.add(b.lw)
            if b.excl:
                for e, r in b.rd.items():
                    if e != eng:
                        deps.add(r)
        for b in writes:
            if b.lw is not None:
                deps.add(b.lw)
            for r in b.rd.values():
                deps.add(r)
            for r in b.rd_dma:
                deps.add(r)
        if eng == "pe" and not op.isdma:
            deps = {d for d in deps if d.isdma or d.eng != "pe"}
        op.deps = tuple(deps)
        for d in deps:
            d.sig = True
        for b in reads:
            if op.isdma:
                b.rd_dma.append(op)
            else:
                b.rd[eng] = op
        for b in writes:
            b.lw = op
            b.rd = {}
            b.rd_dma = []
        self.ops.append(op)
        return op

    def op(self, eng, fn, reads=(), writes=()):
        return self._add(Op(eng, fn), reads, writes)

    def dma(self, q, out, in_, dsem, reads=(), writes=(), **kw):
        def fn(e, out=out, in_=in_, kw=kw):
            return e.dma_start(out=out, in_=in_, **kw)
        o = Op(q, fn, isdma=True, dsem=dsem)
        o.sig = True
        self.pending_dma.append(o)
        return self._add(o, reads, writes)

    def barrier(self):
        if self.bar_t is None:
            t = self.es.enter_context(self.nc.sbuf_tensor("bar_scr", [128, 8], F32))
            self.bar_t = t
        t = self.bar_t
        first = []
        for i, e in enumerate(("act", "dve", "pool")):
            if e == "act":
                o = Op(e, (lambda en, i=i: en.memzero(t[:, i:i + 1])))
            else:
                o = Op(e, (lambda en, i=i: en.memset(t[:, i:i + 1], 0.0)))
            o.sig = True
            self.ops.append(o)
            first.append(o)
        last_pe = None
        for o in reversed(self.ops):
            if o.eng == "pe" and not o.isdma:
                last_pe = o
                break
        if last_pe is not None:
            last_pe.sig = True
            first.append(last_pe)
        deps = tuple(first) + tuple(self.pending_dma)
        self.pending_dma = []
        self.join_deps = deps

    def flush(self, with_join=True):
        self._emit()
        jd = getattr(self, "join_deps", None)
        if jd:
            for en in self.ENGS:
                self._waits(en, jd)
            self.join_deps = None

    def _assign(self, op):
        if op.isdma:
            op.dsem.cnt += 16
            op.sem = op.dsem.sem
            op.val = op.dsem.cnt
        elif op.sig:
            e = op.eng
            if e not in self.cursem or self.cnt[e] >= EPOCH:
                self.cursem[e] = self.new_sem(f"e_{e}_{self.nsem}")
                self.cnt[e] = 0
            self.cnt[e] += 1
            op.sem = self.cursem[e]
            op.val = self.cnt[e]

    def _waits(self, en, deps):
        e = self.eng[en]
        w = self.waited[en]
        need = {}
        for d in deps:
            k = id(d.sem)
            if w.get(k, 0) >= d.val:
                continue
            if k not in need or need[k][1] < d.val:
                need[k] = (d.sem, d.val)
        for k, (s, v) in need.items():
            e.wait_ge(s, v)
            w[k] = v
            self.nwait += 1

    def _emit(self):
        for op in self.ops:
            self._assign(op)
        for op in self.ops:
            self._waits(op.eng, op.deps)
            ins = op.fn(self.eng[op.eng])
            if op.isdma:
                ins.then_inc(op.sem, 16)
            elif op.sig:
                ins.then_inc(op.sem, 1)
            op.fn = None
            self.ninst += 1
        self.ops = []

    def finish(self, final_ops=()):
        self._emit()
        e = self.eng["sp"]
        for d in final_ops:
            e.wait_ge(d.sem, d.val)
        self.es.close()


def make_consts():
    c = np.zeros((128, C_END), np.float64)
    inv = 10000.0 ** (-(np.arange(0, DH, 2, dtype=np.float32) / np.float32(DH)).astype(np.float32))
    inv = inv.astype(np.float32)
    p = np.arange(128)
    c[:, C_INVF] = inv[p % 64]
    c[:, C_SGN] = np.where(p < 64, -1.0, 1.0)
    c[:, C_ID:C_ID + 128] = np.eye(128)
    lg = np.log1p(-(2.0 ** (-5.0 - np.arange(NH, dtype=np.float64))))
    i = np.arange(128)
    sc = DH ** -0.5
    for h in range(NH):
        rel = i[None, :] - i[:, None]
        c[:, C_DEC + h * 128:C_DEC + (h + 1) * 128] = np.where(rel >= 0, sc * np.exp(np.maximum(rel, 0) * lg[h]), 0.0)
        c[:, C_KDEC + h * 128:C_KDEC + (h + 1) * 128] = (sc * np.exp((127 - i) * lg[h]))[:, None]
        c[:, C_CDEC + h * 128:C_CDEC + (h + 1) * 128] = np.exp(128 * lg[h])
        t = np.arange(TB)
        c[:, C_QDEC + h * TB:C_QDEC + (h + 1) * TB] = np.exp(((t % 128) + 1) * lg[h])[None, :]
    cc = np.arange(256)
    c[:, C_MASK:C_MASK + 256] = ((cc[None, :] >= p[:, None]) & (cc[None, :] <= p[:, None] + 128)).astype(np.float64)
    for g, w in enumerate(POOL_W):
        t = np.arange(16)
        c[:, C_PCNT + g * 16:C_PCNT + (g + 1) * 16] = (1.0 / np.minimum(t + 1, w))[None, :]
    return c.astype(np.float32)


class Ctx:
    pass


def load_weight(P, stages, sidx, src_ap, srcBuf, dst_ap, dstBuf, ncols):
    st, ds = stages[sidx[0] % len(stages)]
    ce = ("act", "pool", "dve")[sidx[0] % 3]
    sidx[0] += 1
    P.dma("sp", st[:, 0:ncols], src_ap, ds, reads=[srcBuf], writes=[st])
    if ce == "act":
        P.op("act", lambda e: e.activation(out=dst_ap, in_=st[:, 0:ncols], func=AF.Copy), reads=[st], writes=[dstBuf])
    else:
        P.op(ce, lambda e: e.tensor_copy(out=dst_ap, in_=st[:, 0:ncols]), reads=[st], writes=[dstBuf])


def layer_norm_tile(P, r_ap, rBuf, o_ap, oBuf, g_tab, b_tab, gbBuf, sm, k):
    st, mv, ve, rs, nb, nh = sm["st"][k % 2], sm["mv"][k % 2], sm["ve"][k % 2], sm["rs"][k % 2], sm["nb"][k % 2], sm["nh"]
    for i in range(2):
        P.op("dve", lambda e, i=i: e.bn_stats(out=st[:, i, :], in_=r_ap[:, i * 512:(i + 1) * 512]), reads=[rBuf], writes=[st])
    P.op("dve", lambda e: e.bn_aggr(out=mv[:], in_=st[:].rearrange("p a b -> p (a b)")), reads=[st], writes=[mv])
    P.op("dve", lambda e: e.tensor_scalar_add(out=ve[:], in0=mv[:, 1:2], scalar1=EPS), reads=[mv], writes=[ve])
    P.op("pool", lambda e: e.tensor_tensor(out=rs[:], in0=ve[:], in1=nh[:, 0:1], op=ALU.pow), reads=[ve, nh], writes=[rs])
    P.op("dve", lambda e: e.scalar_tensor_tensor(out=nb[:], in0=mv[:, 0:1], scalar=-1.0, in1=rs[:], op0=ALU.mult, op1=ALU.mult),
         reads=[mv, rs], writes=[nb])
    P.op("act", lambda e: e.activation(out=r_ap, in_=r_ap, func=AF.Identity, bias=nb[:], scale=rs[:]), reads=[rBuf, nb, rs], writes=[rBuf])
    P.op("pool", lambda e: e.tensor_tensor(out=r_ap, in0=r_ap, in1=g_tab, op=ALU.mult), reads=[rBuf, gbBuf], writes=[rBuf])
    P.op("pool", lambda e: e.tensor_tensor(out=o_ap, in0=r_ap, in1=b_tab, op=ALU.add), reads=[rBuf, gbBuf], writes=[oBuf])


def ln_smalls(P, tag):
    sm = {}
    sm["st"] = [P.sbuf(f"{tag}_st{i}", [128, 2, 6], F32) for i in range(2)]
    for n in ("mv",):
        sm[n] = [P.sbuf(f"{tag}_{n}{i}", [128, 2], F32) for i in range(2)]
    for n in ("ve", "rs", "nb"):
        sm[n] = [P.sbuf(f"{tag}_{n}{i}", [128, 1], F32) for i in range(2)]
    sm["nh"] = P.sbuf(f"{tag}_nh", [128, 16], F32)
    P.op("pool", lambda e: e.memset(sm["nh"][:], -0.5), writes=[sm["nh"]])
    return sm


def blk_bufs(name, ap, nblk):
    return [Buf(f"{name}{b}", ap) for b in range(nblk)]


def phase_P(P, G, w_out_ap, wBuf, lng_ap, lnb_ap, lnBuf, hin, hin_b, hout, hout_b):
    NT, NB = G.NT, G.NB
    with contextlib.ExitStack() as sc:
        P.scope = sc
        wo = P.sbuf("P_wo", [128, 8, D], BF16)
        wo_k = [Buf(f"P_wo{k}", wo.t) for k in range(8)]
        stages = [(P.sbuf(f"P_stg{i}", [128, 1024], F32), P.dsem()) for i in range(3)]
        sidx = [0]
        for k in range(8):
            load_weight(P, stages, sidx, w_out_ap[k * 128:(k + 1) * 128, :], wBuf, wo.t[:, k, :], wo_k[k], 1024)
        gb = P.sbuf("P_gb", [128, 2, D], F32)
        gds = P.dsem()
        P.dma("sp", gb[:, 0, :], lng_ap.partition_broadcast(128), gds, reads=[lnBuf], writes=[gb])
        P.dma("sp", gb[:, 1, :], lnb_ap.partition_broadcast(128), gds, reads=[lnBuf], writes=[gb])
        sm = ln_smalls(P, "P")
        ct = [P.sbuf(f"P_ct{i}", [128, 8, TB], BF16) for i in range(2)]
        ctd = [P.dsem() for _ in range(2)]
        hb = [[P.sbuf(f"P_h{i}_{j}", [128, D], F32) for j in range(NJ)] for i in range(2)]
        hd_ = [[P.dsem() for j in range(NJ)] for i in range(2)]
        ob = [P.sbuf(f"P_o{i}", [128, D], F32) for i in range(2)]
        od = [P.dsem() for _ in range(2)]
        ps = [P.psum(f"P_ps{i}", [128, 512], F32) for i in range(4)]

        def loads(b):
            i = b % 2
            t0 = b * TB
            P.dma("sp", ct[i][:], G.catT.t[:, :, t0:t0 + TB].rearrange("c p t -> p c t"), ctd[i], reads=[G.catT_b[b]], writes=[ct[i]])
            for j in range(NJ):
                P.dma("sp", hb[i][j][:], hin.t[t0 + j * 128:t0 + (j + 1) * 128, :], hd_[i][j], reads=[hin_b[b]], writes=[hb[i][j]])

        loads(0)
        kk = 0
        for b in range(NB):
            i = b % 2
            t0 = b * TB
            if b + 1 < NB:
                loads(b + 1)
            for j in range(NJ):
                for n in range(2):
                    pb = ps[(2 * j + n) % 4]
                    for k in range(8):
                        P.op("pe", lambda e, pb=pb, k=k, j=j, n=n, i=i: e.matmul(
                            pb[:, :], lhsT=ct[i][:, k, j * 128:(j + 1) * 128], rhs=wo.t[:, k, n * 512:(n + 1) * 512],
                            start=(k == 0), stop=(k == 7)), reads=[ct[i], wo_k[k]], writes=[pb])
                    P.op("dve", lambda e, pb=pb, j=j, n=n, i=i: e.scalar_tensor_tensor(
                        out=hb[i][j][:, n * 512:(n + 1) * 512], in0=hb[i][j][:, n * 512:(n + 1) * 512], scalar=ALPHA,
                        in1=pb[:, :], op0=ALU.mult, op1=ALU.add), reads=[hb[i][j], pb], writes=[hb[i][j]])
                o = ob[kk % 2]
                layer_norm_tile(P, hb[i][j][:], hb[i][j], o[:], o, gb[:, 0, :], gb[:, 1, :], gb, sm, kk)
                P.dma("sp", hout.t[t0 + j * 128:t0 + (j + 1) * 128, :], o[:], od[kk % 2], reads=[o], writes=[hout_b[b]])
                kk += 1
        P.end_phase()
    P.scope = P.es


def phase_F(P, G, wi_ap, wiBuf, wf_ap, wfBuf, lng_ap, lnb_ap, lnBuf, hin, hin_b, hout, hout_b):
    NT, NB = G.NT, G.NB
    NC = DFF // 128
    fin = []
    with contextlib.ExitStack() as sc:
        P.scope = sc
        wi = P.sbuf("F_wi", [128, 8, 2 * DFF], BF16)
        wi_k = [Buf(f"F_wi{k}", wi.t) for k in range(8)]
        wf = P.sbuf("F_wf", [128, NC, D], BF16)
        wf_k = [Buf(f"F_wf{k}", wf.t) for k in range(NC)]
        stages = [(P.sbuf(f"F_stg{i}", [128, 1024], F32), P.dsem()) for i in range(2)]
        sidx = [0]
        for k in range(8):
            for q in range(8):
                load_weight(P, stages, sidx, wi_ap[k * 128:(k + 1) * 128, q * 704:(q + 1) * 704], wiBuf,
                            wi.t[:, k, q * 704:(q + 1) * 704], wi_k[k], 704)
        for k in range(NC):
            load_weight(P, stages, sidx, wf_ap[k * 128:(k + 1) * 128, :], wfBuf, wf.t[:, k, :], wf_k[k], 1024)
        gb = P.sbuf("F_gb", [128, 2, D], F32)
        gds = P.dsem()
        P.dma("sp", gb[:, 0, :], lng_ap.partition_broadcast(128), gds, reads=[lnBuf], writes=[gb])
        P.dma("sp", gb[:, 1, :], lnb_ap.partition_broadcast(128), gds, reads=[lnBuf], writes=[gb])
        ident = P.sbuf("F_id", [128, 128], F32)
        P.dma("sp", ident[:], G.consts.t[:, C_ID:C_ID + 128], P.dsem(), reads=[G.consts], writes=[ident])
        sm = ln_smalls(P, "F")
        hb = [P.sbuf(f"F_h{j}", [128, D], F32) for j in range(NJ)]
        hds = [P.dsem() for j in range(NJ)]
        ob = [P.sbuf(f"F_o{i}", [128, D], F32) for i in range(2)]
        od = [P.dsem() for _ in range(2)]
        hT = P.sbuf("F_hT", [128, 8, TB], BF16)
        hT_k = [Buf(f"F_hT{k}", hT.t) for k in range(8)]
        act = P.sbuf("F_act", [128, NC, TB], BF16)
        act_k = [Buf(f"F_act{k}", act.t) for k in range(NC)]
        sg = [P.sbuf(f"F_sg{i}", [128, TB], F32) for i in range(2)]
        psO = [P.psum(f"F_psO{i}", [128, 512], F32) for i in range(2)]
        psG = [P.psum(f"F_psG{i}", [128, 512], F32) for i in range(2)]
        psU = [P.psum(f"F_psU{i}", [128, 512], F32) for i in range(2)]
        psT = [P.psum(f"F_psT{i}", [128, 512], F32) for i in range(2)]

        def loads(b):
            t0 = b * TB
            for j in range(NJ):
                P.dma("sp", hb[j][:], hin.t[t0 + j * 128:t0 + (j + 1) * 128, :], hds[j], reads=[hin_b[b]], writes=[hb[j]])

        loads(0)
        kk = 0
        for b in range(NB):
            t0 = b * TB
            for c in range(8):
                pt = psT[c % 2]
                for j in range(NJ):
                    P.op("pe", lambda e, pt=pt, j=j, c=c: e.transpose(out=pt[:, j * 128:(j + 1) * 128],
                                                                      in_=hb[j][:, c * 128:(c + 1) * 128], identity=ident[:]),
                         reads=[hb[j], ident], writes=[pt])
                if c % 2 == 0:
                    P.op("dve", lambda e, pt=pt, c=c: e.tensor_copy(out=hT.t[:, c, :], in_=pt[:, :]), reads=[pt], writes=[hT_k[c]])
                else:
                    P.op("act", lambda e, pt=pt, c=c: e.activation(out=hT.t[:, c, :], in_=pt[:, :], func=AF.Copy), reads=[pt], writes=[hT_k[c]])
            for c in range(NC):
                pg, pu, s = psG[c % 2], psU[c % 2], sg[c % 2]
                for k in range(8):
                    P.op("pe", lambda e, pg=pg, k=k, c=c: e.matmul(pg[:, :], lhsT=wi.t[:, k, c * 128:(c + 1) * 128], rhs=hT.t[:, k, :],
                                                                   start=(k == 0), stop=(k == 7)), reads=[wi_k[k], hT_k[k]], writes=[pg])
                for k in range(8):
                    P.op("pe", lambda e, pu=pu, k=k, c=c: e.matmul(pu[:, :], lhsT=wi.t[:, k, DFF + c * 128:DFF + (c + 1) * 128], rhs=hT.t[:, k, :],
                                                                   start=(k == 0), stop=(k == 7)), reads=[wi_k[k], hT_k[k]], writes=[pu])
                P.op("act", lambda e, pg=pg, s=s: e.activation(out=s[:], in_=pg[:, :], func=AF.Silu), reads=[pg], writes=[s])
                P.op("dve", lambda e, pu=pu, s=s, c=c: e.tensor_tensor(out=act.t[:, c, :], in0=s[:], in1=pu[:, :], op=ALU.mult),
                     reads=[s, pu], writes=[act_k[c]])
            for j in range(NJ):
                for n in range(2):
                    pb = psO[n]
                    for k in range(NC):
                        P.op("pe", lambda e, pb=pb, k=k, j=j, n=n: e.matmul(
                            pb[:, :], lhsT=act.t[:, k, j * 128:(j + 1) * 128], rhs=wf.t[:, k, n * 512:(n + 1) * 512],
                            start=(k == 0), stop=(k == NC - 1)), reads=[act_k[k], wf_k[k]], writes=[pb])
                    P.op("dve", lambda e, pb=pb, j=j, n=n: e.scalar_tensor_tensor(
                        out=hb[j][:, n * 512:(n + 1) * 512], in0=hb[j][:, n * 512:(n + 1) * 512], scalar=ALPHA,
                        in1=pb[:, :], op0=ALU.mult, op1=ALU.add), reads=[hb[j], pb], writes=[hb[j]])
                o = ob[kk % 2]
                layer_norm_tile(P, hb[j][:], hb[j], o[:], o, gb[:, 0, :], gb[:, 1, :], gb, sm, kk)
                if b + 1 < NB:
                    P.dma("sp", hb[j][:], hin.t[t0 + TB + j * 128:t0 + TB + (j + 1) * 128, :], hds[j], reads=[hin_b[b + 1]], writes=[hb[j]])
                fin.append(P.dma("sp", hout.t[t0 + j * 128:t0 + (j + 1) * 128, :], o[:], od[kk % 2], reads=[o], writes=[hout_b[b]]))
                kk += 1
        P.end_phase()
    P.scope = P.es
    return fin


def phase_R(P, G):
    S, NS = G.S, G.NS
    MAGIC = 12582912.0
    HI = 6.28125
    LO = 2.0 * math.pi - HI
    PIL = 3.1415925
    with contextlib.ExitStack() as sc:
        P.scope = sc
        cs = P.sbuf("R_cs", [128, 2], F32)
        P.dma("sp", cs[:], G.consts.t[:, 0:2], P.dsem(), reads=[G.consts], writes=[cs])
        pi = P.sbuf("R_pi", [128, S], I32)
        ang = P.sbuf("R_ang", [128, S], F32)
        kq = P.sbuf("R_k", [128, S], F32)
        r = P.sbuf("R_r", [128, S], F32)
        oc = P.sbuf("R_oc", [128, S], F32)
        os_ = P.sbuf("R_os", [128, S], F32)
        d1, d2, d3 = P.dsem(), P.dsem(), P.dsem()
        for s in range(NS):
            P.dma("sp", pi[:], G.pos.t[s:s + 1, :].partition_broadcast(128), d1, reads=[G.pos], writes=[pi])
            P.op("dve", lambda e: e.tensor_copy(out=ang[:], in_=pi[:]), reads=[pi], writes=[ang])
            P.op("dve", lambda e: e.tensor_scalar(out=ang[:], in0=ang[:], scalar1=cs[:, 0:1], scalar2=None, op0=ALU.mult),
                 reads=[ang, cs], writes=[ang])
            P.op("dve", lambda e: e.tensor_scalar(out=kq[:], in0=ang[:], scalar1=float(1.0 / (2.0 * math.pi)), scalar2=MAGIC,
                                                  op0=ALU.mult, op1=ALU.add), reads=[ang], writes=[kq])
            P.op("dve", lambda e: e.tensor_scalar_add(out=kq[:], in0=kq[:], scalar1=-MAGIC), reads=[kq], writes=[kq])
            P.op("dve", lambda e: e.scalar_tensor_tensor(out=r[:], in0=kq[:], scalar=-HI, in1=ang[:], op0=ALU.mult, op1=ALU.add),
                 reads=[kq, ang], writes=[r])
            P.op("dve", lambda e: e.scalar_tensor_tensor(out=r[:], in0=kq[:], scalar=-LO, in1=r[:], op0=ALU.mult, op1=ALU.add),
                 reads=[kq, r], writes=[r])
            P.op("dve", lambda e: e.tensor_scalar(out=r[:], in0=r[:], scalar1=-PIL, scalar2=PIL, op0=ALU.max, op1=ALU.min),
                 reads=[r], writes=[r])
            P.op("act", lambda e: e.activation(out=os_[:], in_=r[:], func=AF.Sin, scale=cs[:, 1:2]), reads=[r, cs], writes=[os_])
            P.op("dve", lambda e: e.scalar_tensor_tensor(out=kq[:], in0=r[:], scalar=-1.0, in1=r[:], op0=ALU.mult, op1=ALU.max), reads=[r], writes=[kq])
            P.op("act", lambda e: e.activation(out=oc[:], in_=kq[:], func=AF.Sin, scale=-1.0, bias=float(math.pi / 2)),
                 reads=[kq], writes=[oc])
            P.dma("sp", G.cosT.t[s, :, :], oc[:], d2, reads=[oc], writes=[G.cosT])
            P.dma("sp", G.sinT.t[s, :, :], os_[:], d3, reads=[os_], writes=[G.sinT])
        P.end_phase()
    P.scope = P.es


def transpose_block(P, hb, ident, psT, hT, hT_k):
    for c in range(8):
        pt = psT[c % len(psT)]
        for j in range(NJ):
            P.op("pe", lambda e, pt=pt, j=j, c=c: e.transpose(out=pt[:, j * 128:(j + 1) * 128],
                                                              in_=hb[j][:, c * 128:(c + 1) * 128], identity=ident[:]),
                 reads=[hb[j], ident], writes=[pt])
        if c % 2 == 0:
            P.op("dve", lambda e, pt=pt, c=c: e.tensor_copy(out=hT.t[:, c, :], in_=pt[:, :]), reads=[pt], writes=[hT_k[c]])
        else:
            P.op("act", lambda e, pt=pt, c=c: e.activation(out=hT.t[:, c, :], in_=pt[:, :], func=AF.Copy), reads=[pt], writes=[hT_k[c]])


def rope_head(P, X, Y, cs, sn, t1, t2):
    P.op("dve", lambda e: e.tensor_tensor(out=t1[:], in0=X[:, :], in1=cs[:], op=ALU.mult), reads=[X, cs], writes=[t1])
    P.op("dve", lambda e: e.tensor_tensor(out=t2[:], in0=Y[:, :], in1=sn[:], op=ALU.mult), reads=[Y, sn], writes=[t2])
    P.op("pool", lambda e: e.tensor_tensor(out=t1[:], in0=t1[:], in1=t2[:], op=ALU.add), reads=[t1, t2], writes=[t1])


def phase_E(P, G, l2, hin, hin_b):
    S, NS, NB = G.S, G.NS, G.NB
    BPS = S // TB
    WC = 3584
    with contextlib.ExitStack() as sc:
        P.scope = sc
        wi = P.sbuf("E_wi", [128, 8, WC], BF16)
        wi_k = [Buf(f"E_wi{k}", wi.t) for k in range(8)]
        stages = [(P.sbuf(f"E_stg{i}", [128, 512], F32), P.dsem()) for i in range(3)]
        sidx = [0]
        for k in range(8):
            for q in range(7):
                load_weight(P, stages, sidx, G.ev_w_in.t[l2, k * 128:(k + 1) * 128, q * 512:(q + 1) * 512], G.ev_w_in,
                            wi.t[:, k, q * 512:(q + 1) * 512], wi_k[k], 512)
        pw = P.sbuf("E_pw", [128, 4, 128], BF16)
        pst, pds = stages[2]
        P.dma("sp", pst[:, 0:512].rearrange("p (g d) -> p g d", g=4), G.ev_pool_w.t[l2].rearrange("g c d -> c g d"), pds,
              reads=[G.ev_pool_w], writes=[pst])
        P.op("dve", lambda e: e.tensor_copy(out=pw[:].rearrange("p g d -> p (g d)"), in_=pst[:, 0:512]), reads=[pst], writes=[pw])
        psc = P.sbuf("E_psc", [128, 4], F32)
        P.dma("sp", psc[:], G.ev_pool_scale.t[l2], P.dsem(), reads=[G.ev_pool_scale], writes=[psc])
        gain = P.sbuf("E_gain", [128, 512], F32)
        P.dma("sp", gain[:], G.ev_ret_norm_g.t[l2:l2 + 1, :].partition_broadcast(128), P.dsem(), reads=[G.ev_ret_norm_g], writes=[gain])
        ct = P.sbuf("E_ct", [128, C_END - C_ID], F32)
        P.dma("sp", ct[:], G.consts.t[:, C_ID:C_END], P.dsem(), reads=[G.consts], writes=[ct])
        o_ = lambda c: c - C_ID
        ident = Buf("E_ident", ct.t[:, o_(C_ID):o_(C_ID) + 128])
        dec = ct.t[:, o_(C_DEC):o_(C_DEC) + 512]
        kdec = ct.t[:, o_(C_KDEC):o_(C_KDEC) + 512]
        cdec = ct.t[:, o_(C_CDEC):o_(C_CDEC) + 512]
        qdec = ct.t[:, o_(C_QDEC):o_(C_QDEC) + 2048]
        pcnt = ct.t[:, o_(C_PCNT):o_(C_PCNT) + 64]
        ident_bf = P.sbuf("E_idbf", [128, 128], BF16)
        P.op("dve", lambda e: e.tensor_copy(out=ident_bf[:], in_=ct.t[:, 0:128]), reads=[ct], writes=[ident_bf])
        identF = Buf("E_identF", ct.t[:, 0:128])
        identF.lw = ct.lw
        nh = P.sbuf("E_nh", [128, 16], F32)
        P.op("pool", lambda e: e.memset(nh[:], -0.5), writes=[nh])

        cs = [P.sbuf(f"E_cs{i}", [128, TB], F32) for i in range(1)] * 2
        sn = [P.sbuf(f"E_sn{i}", [128, TB], F32) for i in range(1)] * 2
        csd = [P.dsem() for _ in range(1)] * 2
        hb = [P.sbuf(f"E_h{j}", [128, D], F32) for j in range(NJ)]
        hds = [P.dsem() for _ in range(NJ)]
        hT = P.sbuf("E_hT", [128, 8, TB], BF16)
        hT_k = [Buf(f"E_hT{k}", hT.t) for k in range(8)]
        qr = P.sbuf("E_qr", [128, NH, TB], BF16)
        kr = P.sbuf("E_kr", [128, NH, TB], BF16)
        qd = P.sbuf("E_qd", [128, NH, TB], BF16)
        t1 = [P.sbuf(f"E_t1{i}", [128, TB], F32) for i in range(2)]
        t2 = [P.sbuf(f"E_t2{i}", [128, TB], F32) for i in range(2)]
        vb = P.sbuf("E_vb", [128, NJ, 512], BF16)
        vdb = P.sbuf("E_vdb", [128, NJ, 512], BF16)
        gsg = P.sbuf("E_gsg", [128, NJ, 512], F32)
        xh = P.sbuf("E_xh", [128, 4, 16 + TB], F32)
        sa = P.sbuf("E_sa", [128, 16 + TB], F32)
        sb = P.sbuf("E_sb", [128, 16 + TB], F32)
        pooled = P.sbuf("E_pooled", [128, 4, TB], BF16)
        state = P.sbuf("E_state", [128, 512], F32)
        stmp = P.sbuf("E_stmp", [128, 512], F32)
        sbf = [P.sbuf(f"E_sbf{i}", [128, 512], BF16) for i in range(6)]
        PT = [P.sbuf(f"E_PT{i}", [128, 512], BF16) for i in range(2)]
        ktok = [P.sbuf(f"E_ktok{i}", [128, 512], BF16) for i in range(2)]
        osb = P.sbuf("E_osb", [128, NJ, 512], F32)
        sq = P.sbuf("E_sq", [128, NJ, 512], F32)
        s1 = P.sbuf("E_s1", [128, 16], F32)
        s2 = P.sbuf("E_s2", [128, 16], F32)
        mean = P.sbuf("E_mean", [128, 16], F32)
        msq = P.sbuf("E_msq", [128, 16], F32)
        var = P.sbuf("E_var", [128, 16], F32)
        rstd = P.sbuf("E_rstd", [128, 16], F32)
        nb = P.sbuf("E_nb", [128, 16], F32)
        cat = [P.sbuf(f"E_cat{i}", [128, 8, TB], BF16) for i in range(1)] * 2
        catd = [P.dsem() for _ in range(1)] * 2
        psA = [P.psum(f"E_psA{i}", [128, 512], F32) for i in range(4)]
        psS = [P.psum(f"E_psS{i}", [128, 512], F32) for i in range(2)]
        psTB = P.psum("E_psTB", [128, 1024], BF16)
        psKV = P.psum("E_psKV", [128, 512], F32)

        def loads(b):
            t0 = b * TB
            s = t0 // S
            ts = t0 - s * S
            i = b % 2
            P.dma("sp", cs[i][:], G.cosT.t[s, :, ts:ts + TB], csd[i], reads=[G.cosT], writes=[cs[i]])
            P.dma("sp", sn[i][:], G.sinT.t[s, :, ts:ts + TB], csd[i], reads=[G.sinT], writes=[sn[i]])
            for j in range(NJ):
                P.dma("sp", hb[j][:], hin.t[t0 + j * 128:t0 + (j + 1) * 128, :], hds[j], reads=[hin_b[b]], writes=[hb[j]])

        loads(0)
        gch = 0
        for b in range(NB):
            t0 = b * TB
            i = b % 2
            first = (t0 % S == 0)
            if first:
                P.op("pool", lambda e: e.memset(state[:], 0.0), writes=[state])
                sb0 = sbf[gch % 6]
                P.op("pool", lambda e, sb0=sb0: e.memset(sb0[:], 0.0), writes=[sb0])
                P.op("pool", lambda e: e.memset(xh[:, :, 0:16], 0.0), writes=[xh])
            transpose_block(P, hb, identF, psA, hT, hT_k)
            na = 0
            for which, col, colsw in (("q", 0, 2560), ("k", 512, 3072)):
                for hd in range(NH):
                    X, Y = psA[(2 * na) % 4], psA[(2 * na + 1) % 4]
                    for k in range(8):
                        P.op("pe", lambda e, X=X, k=k, c0=col + hd * 128: e.matmul(X[:, :], lhsT=wi.t[:, k, c0:c0 + 128], rhs=hT.t[:, k, :],
                                                                                  start=(k == 0), stop=(k == 7)), reads=[wi_k[k], hT_k[k]], writes=[X])
                    for k in range(8):
                        P.op("pe", lambda e, Y=Y, k=k, c0=colsw + hd * 128: e.matmul(Y[:, :], lhsT=wi.t[:, k, c0:c0 + 128], rhs=hT.t[:, k, :],
                                                                                    start=(k == 0), stop=(k == 7)), reads=[wi_k[k], hT_k[k]], writes=[Y])
                    a, bb = t1[na % 2], t2[na % 2]
                    rope_head(P, X, Y, cs[i], sn[i], a, bb)
                    if which == "q":
                        P.op("act", lambda e, a=a, hd=hd: e.activation(out=qr[:, hd, :], in_=a[:], func=AF.Copy), reads=[a], writes=[qr])
                        P.op("pool", lambda e, a=a, hd=hd: e.tensor_tensor(out=qd[:, hd, :], in0=a[:], in1=qdec[:, hd * TB:(hd + 1) * TB], op=ALU.mult),
                             reads=[a, ct], writes=[qd])
                    else:
                        P.op("act", lambda e, a=a, hd=hd: e.activation(out=kr[:, hd, :], in_=a[:], func=AF.Copy), reads=[a], writes=[kr])
                    na += 1
            for gi in range(4):
                X = psA[na % 4]
                na += 1
                for k in range(8):
                    P.op("pe", lambda e, X=X, k=k, c0=2048 + gi * 128: e.matmul(X[:, :], lhsT=wi.t[:, k, c0:c0 + 128], rhs=hT.t[:, k, :],
                                                                               start=(k == 0), stop=(k == 7)), reads=[wi_k[k], hT_k[k]], writes=[X])
                P.op("act", lambda e, X=X, gi=gi: e.activation(out=xh[:, gi, 16:16 + TB], in_=X[:, :], func=AF.Copy), reads=[X], writes=[xh])
            for j in range(NJ):
                X = psA[na % 4]
                na += 1
                for k in range(8):
                    P.op("pe", lambda e, X=X, k=k, j=j: e.matmul(X[:, :], lhsT=hT.t[:, k, j * 128:(j + 1) * 128], rhs=wi.t[:, k, 1024:1536],
                                                                 start=(k == 0), stop=(k == 7)), reads=[wi_k[k], hT_k[k]], writes=[X])
                P.op("act", lambda e, X=X, j=j: e.activation(out=vb[:, j, :], in_=X[:, :], func=AF.Copy), reads=[X], writes=[vb])
                P.op("dve", lambda e, X=X, j=j: e.tensor_tensor(out=vdb[:, j, :], in0=X[:, :], in1=kdec, op=ALU.mult), reads=[X, ct], writes=[vdb])
                X = psA[na % 4]
                na += 1
                for k in range(8):
                    P.op("pe", lambda e, X=X, k=k, j=j: e.matmul(X[:, :], lhsT=hT.t[:, k, j * 128:(j + 1) * 128], rhs=wi.t[:, k, 1536:2048],
                                                                 start=(k == 0), stop=(k == 7)), reads=[wi_k[k], hT_k[k]], writes=[X])
                P.op("act", lambda e, X=X, j=j: e.activation(out=gsg[:, j, :], in_=X[:, :], func=AF.Silu), reads=[X], writes=[gsg])
                P.op("pool", lambda e, j=j: e.tensor_tensor(out=gsg[:, j, :], in0=gsg[:, j, :], in1=gain[:], op=ALU.mult), reads=[gsg, gain], writes=[gsg])
            if b + 1 < NB:
                loads(b + 1)
            for gi, w in enumerate(POOL_W):
                src = xh.t[:, gi, :]
                cur, curBuf = src, xh
                sh = 1
                tgl = [sa, sb]
                ti = 0
                while sh < w:
                    dst = tgl[ti % 2]
                    lo = 2 * sh - 1
                    P.op("pool", lambda e, dst=dst, cur=cur, sh=sh, lo=lo: e.tensor_tensor(out=dst[:, lo:16 + TB], in0=cur[:, lo:16 + TB],
                                                                                         in1=cur[:, lo - sh:16 + TB - sh], op=ALU.add),
                         reads=[curBuf], writes=[dst])
                    cur, curBuf = dst.t, dst
                    sh *= 2
                    ti += 1
                P.op("dve", lambda e, cur=cur, gi=gi, w=w: e.scalar_tensor_tensor(out=pooled[:, gi, :], in0=cur[:, 16:16 + TB], scalar=float(1.0 / w),
                                                                                 in1=xh[:, gi, 16:16 + TB], op0=ALU.mult, op1=ALU.subtract),
                     reads=[curBuf, xh], writes=[pooled])
                if first:
                    tmp = tgl[ti % 2]
                    P.op("pool", lambda e, cur=cur, gi=gi, tmp=tmp: e.tensor_tensor(out=tmp[:, 0:16], in0=cur[:, 16:32], in1=pcnt[:, gi * 16:(gi + 1) * 16], op=ALU.mult),
                         reads=[curBuf, ct], writes=[tmp])
                    P.op("pool", lambda e, gi=gi, tmp=tmp: e.tensor_tensor(out=pooled[:, gi, 0:16], in0=tmp[:, 0:16], in1=xh[:, gi, 16:32], op=ALU.subtract),
                         reads=[tmp, xh, pooled], writes=[pooled])
                X = psA[na % 4]
                na += 1
                P.op("pe", lambda e, X=X, gi=gi: e.matmul(X[:, :], lhsT=pw[:, gi, :], rhs=pooled[:, gi, :], start=True, stop=True),
                     reads=[pw, pooled], writes=[X])
                P.op("act", lambda e, X=X, gi=gi, i=i: e.activation(out=cat[i][:, 4 + gi, :], in_=X[:, :], func=AF.Identity, scale=psc[:, gi:gi + 1]),
                     reads=[X, psc], writes=[cat[i]])
            P.op("act", lambda e: e.activation(out=xh[:, :, 0:16], in_=xh[:, :, TB:TB + 16], func=AF.Copy), reads=[xh], writes=[xh])
            for j in range(NJ):
                Sb = psS[j % 2]
                for hd in range(NH):
                    P.op("pe", lambda e, Sb=Sb, hd=hd, j=j: e.matmul(Sb[:, hd * 128:(hd + 1) * 128], lhsT=kr[:, hd, j * 128:(j + 1) * 128],
                                                                     rhs=qr[:, hd, j * 128:(j + 1) * 128], start=True, stop=True),
                         reads=[kr, qr], writes=[Sb])
                pt = PT[j % 2]
                P.op("dve", lambda e, Sb=Sb, pt=pt: e.tensor_tensor(out=pt[:], in0=Sb[:, :], in1=dec, op=ALU.mult), reads=[Sb, ct], writes=[pt])
                for hd in range(NH):
                    P.op("pe", lambda e, hd=hd, j=j: e.transpose(out=psTB[:, (j % 2) * 512 + hd * 128:(j % 2) * 512 + (hd + 1) * 128],
                                                                 in_=kr[:, hd, j * 128:(j + 1) * 128], identity=ident_bf[:]),
                         reads=[kr, ident_bf], writes=[psTB])
                kt = ktok[j % 2]
                P.op("act", lambda e, kt=kt, j=j: e.activation(out=kt[:], in_=psTB[:, (j % 2) * 512:(j % 2 + 1) * 512], func=AF.Copy),
                     reads=[psTB], writes=[kt])
                for hd in range(NH):
                    P.op("pe", lambda e, kt=kt, hd=hd, j=j: e.matmul(psKV[:, hd * 128:(hd + 1) * 128], lhsT=kt[:, hd * 128:(hd + 1) * 128],
                                                                     rhs=vdb[:, j, hd * 128:(hd + 1) * 128], start=True, stop=True),
                         reads=[kt, vdb], writes=[psKV])
                O = psA[j]
                sbc = sbf[gch % 6]
                for hd in range(NH):
                    P.op("pe", lambda e, O=O, pt=pt, hd=hd, j=j: e.matmul(O[:, hd * 128:(hd + 1) * 128], lhsT=pt[:, hd * 128:(hd + 1) * 128],
                                                                          rhs=vb[:, j, hd * 128:(hd + 1) * 128], start=True, stop=False),
                         reads=[pt, vb], writes=[O])
                    P.op("pe", lambda e, O=O, sbc=sbc, hd=hd, j=j: e.matmul(O[:, hd * 128:(hd + 1) * 128], lhsT=qd[:, hd, j * 128:(j + 1) * 128],
                                                                            rhs=sbc[:, hd * 128:(hd + 1) * 128], start=False, stop=True),
                         reads=[qd, sbc], writes=[O])
                P.op("act", lambda e, O=O, j=j: e.activation(out=osb[:, j, :], in_=O[:, :], func=AF.Copy), reads=[O], writes=[osb])
                sbn = sbf[(gch + 1) % 6]
                P.op("pool", lambda e: e.tensor_tensor(out=stmp[:], in0=state[:], in1=cdec, op=ALU.mult), reads=[state, ct], writes=[stmp])
                P.op("dve", lambda e: e.tensor_tensor(out=state[:], in0=stmp[:], in1=psKV[:, :], op=ALU.add), reads=[stmp, psKV], writes=[state])
                P.op("act", lambda e, sbn=sbn: e.activation(out=sbn[:], in_=state[:], func=AF.Copy), reads=[state], writes=[sbn])
                gch += 1
            o3 = osb.t[:].rearrange("p j (h e) -> p (j h) e", h=NH)
            q3 = sq.t[:].rearrange("p j (h e) -> p (j h) e", h=NH)
            P.op("dve", lambda e: e.tensor_reduce(out=s1[:], in_=o3, axis=AX.X, op=ALU.add), reads=[osb], writes=[s1])
            P.op("pool", lambda e: e.tensor_tensor(out=sq[:], in0=osb[:], in1=osb[:], op=ALU.mult), reads=[osb], writes=[sq])
            P.op("dve", lambda e: e.tensor_reduce(out=s2[:], in_=q3, axis=AX.X, op=ALU.add), reads=[sq], writes=[s2])
            P.op("dve", lambda e: e.tensor_scalar(out=mean[:], in0=s1[:], scalar1=1.0 / DH, scalar2=None, op0=ALU.mult), reads=[s1], writes=[mean])
            P.op("dve", lambda e: e.tensor_tensor(out=msq[:], in0=mean[:], in1=mean[:], op=ALU.mult), reads=[mean], writes=[msq])
            P.op("dve", lambda e: e.scalar_tensor_tensor(out=var[:], in0=s2[:], scalar=1.0 / DH, in1=msq[:], op0=ALU.mult, op1=ALU.subtract),
                 reads=[s2, msq], writes=[var])
            P.op("dve", lambda e: e.tensor_scalar_add(out=var[:], in0=var[:], scalar1=EPS), reads=[var], writes=[var])
            P.op("pool", lambda e: e.tensor_tensor(out=rstd[:], in0=var[:], in1=nh[:], op=ALU.pow), reads=[var, nh], writes=[rstd])
            P.op("dve", lambda e: e.scalar_tensor_tensor(out=nb[:], in0=mean[:], scalar=-1.0, in1=rstd[:], op0=ALU.mult, op1=ALU.mult),
                 reads=[mean, rstd], writes=[nb])
            P.op("pool", lambda e: e.tensor_tensor(out=q3, in0=o3, in1=rstd[:].unsqueeze(2).to_broadcast([128, 16, DH]), op=ALU.mult),
                 reads=[osb, rstd], writes=[sq])
            P.op("pool", lambda e: e.tensor_tensor(out=q3, in0=q3, in1=nb[:].unsqueeze(2).to_broadcast([128, 16, DH]), op=ALU.add),
                 reads=[sq, nb], writes=[sq])
            P.op("pool", lambda e: e.tensor_tensor(out=sq[:], in0=sq[:], in1=gsg[:], op=ALU.mult), reads=[sq, gsg], writes=[sq])
            for j in range(NJ):
                X = psS[j % 2]
                for hd in range(NH):
                    P.op("pe", lambda e, X=X, hd=hd, j=j: e.transpose(out=X[:, hd * 128:(hd + 1) * 128], in_=sq[:, j, hd * 128:(hd + 1) * 128],
                                                                      identity=identF[:]), reads=[sq, identF], writes=[X])
                eng = "dve" if j % 2 == 0 else "act"
                outap = cat[i][:, 0:4, j * 128:(j + 1) * 128]
                inap = X[:, :].rearrange("p (h t) -> p h t", h=NH)
                if eng == "dve":
                    P.op("dve", lambda e, outap=outap, inap=inap: e.tensor_copy(out=outap, in_=inap), reads=[X], writes=[cat[i]])
                else:
                    P.op("act", lambda e, outap=outap, inap=inap: e.activation(out=outap, in_=inap, func=AF.Copy), reads=[X], writes=[cat[i]])
            P.dma("sp", G.catT.t[:, :, t0:t0 + TB].rearrange("c p t -> p c t"), cat[i][:], catd[i], reads=[cat[i]], writes=[G.catT_b[b]])
        P.end_phase()
    P.scope = P.es


def build_program(NS, S, layers, debug=False):
    nc = bass.Bass("TRN2", target_bir_lowering=False)
    P = Prog(nc)
    G = Ctx()
    G.NS, G.S = NS, S
    G.NT = NS * S
    G.NB = G.NT // TB
    NT, NB = G.NT, G.NB
    ext = lambda name, shape, dt=F32: P.dram(name, shape, dt, kind="ExternalInput")
    G.x = ext("x", [NT, D])
    G.pos = ext("pos", [NS, S], I32)
    G.consts = ext("consts", [128, C_END])
    G.ev_w_in = ext("ev_w_in", [2, D, 3584])
    G.ev_ret_norm_g = ext("ev_ret_norm_g", [2, 512])
    G.ev_pool_w = ext("ev_pool_w", [2, 4, 128, 128])
    G.ev_pool_scale = ext("ev_pool_scale", [2, 128, 4])
    G.ev_w_out = ext("ev_w_out", [2, D, D])
    G.od_w_in = ext("od_w_in", [2, D, 3584])
    G.od_small = ext("od_small", [2, 128, 36])
    G.od_gate_a_w = ext("od_gate_a_w", [2, 8, 64, 64])
    G.od_gate_x_w = ext("od_gate_x_w", [2, 8, 64, 64])
    G.od_w_out = ext("od_w_out", [2, D, D])
    G.ffn_w_in = ext("ffn_w_in", [DEPTH, D, 2 * DFF])
    G.ffn_w_out = ext("ffn_w_out", [DEPTH, DFF, D])
    G.ln_g = ext("ln_g", [DEPTH, 2, D])
    G.ln_b = ext("ln_b", [DEPTH, 2, D])
    G.out = P.dram("out", [NT, D], F32, kind="ExternalOutput")
    dk = "ExternalOutput" if debug else "Internal"
    G.hA = P.dram("hA", [NT, D], F32, kind=dk)
    G.hB = P.dram("hB", [NT, D], F32, kind=dk)
    G.catT = P.dram("catT", [8, 128, NT], BF16, kind=dk)
    G.cosT = P.dram("cosT", [NS, 128, S], F32, kind=dk)
    G.sinT = P.dram("sinT", [NS, 128, S], F32, kind=dk)
    G.qT = P.dram("qT", [NH, 128, NT], BF16, kind=dk)
    G.kT = P.dram("kT", [NH, 128, NT], BF16, kind=dk)
    G.vd = P.dram("vd", [NT, 512], BF16, kind=dk)
    G.x_b = blk_bufs("x_b", G.x.t, NB)
    G.hA_b = blk_bufs("hA_b", G.hA.t, NB)
    G.hB_b = blk_bufs("hB_b", G.hB.t, NB)
    G.out_b = blk_bufs("out_b", G.out.t, NB)
    G.catT_b = blk_bufs("catT_b", G.catT.t, NB)
    G.qkv_b = blk_bufs("qkv_b", G.qT.t, NB)

    phase_R(P, G)
    fin = []
    hin, hin_b = G.x, G.x_b
    for li, layer in enumerate(layers):
        l2 = layer // 2
        last = (li == len(layers) - 1)
        if layer % 2 == 0:
            phase_E(P, G, l2, hin, hin_b)
            w_out = G.ev_w_out
        else:
            phase_O1(P, G, l2, hin, hin_b)
            phase_O2(P, G)
            w_out = G.od_w_out
        phase_P(P, G, w_out.t[l2], w_out, G.ln_g.t[layer, 0:1, :], G.ln_b.t[layer, 0:1, :], G.ln_g, hin, hin_b, G.hB, G.hB_b)
        hout, hout_b = (G.out, G.out_b) if last else (G.hA, G.hA_b)
        fin = phase_F(P, G, G.ffn_w_in.t[layer], G.ffn_w_in, G.ffn_w_out.t[layer], G.ffn_w_out,
                      G.ln_g.t[layer, 1:2, :], G.ln_b.t[layer, 1:2, :], G.ln_g, G.hB, G.hB_b, hout, hout_b)
        hin, hin_b = G.hA, G.hA_b
    P.finish(fin)
    return nc, P


def host_prep(inp):
    f = lambda a: np.ascontiguousarray(np.asarray(a, dtype=np.float32))
    swap = np.concatenate([np.arange(h * 128 + 64, h * 128 + 128).tolist() + np.arange(h * 128, h * 128 + 64).tolist() for h in range(NH)]).astype(np.int64)
    ev = f(inp["ev_w_in"])
    ev_ext = np.concatenate([ev, ev[:, :, 0:512][:, :, swap], ev[:, :, 512:1024][:, :, swap]], axis=2)
    od = f(inp["od_w_in"])
    od_ext = np.concatenate([od, od[:, :, 1024:1536][:, :, swap], od[:, :, 1536:2048][:, :, swap]], axis=2)
    psc = f(inp["ev_pool_scale"]).reshape(2, 4, 128).transpose(0, 2, 1)
    cw = f(inp["od_conv_w"])
    cols = [cw[:, k, :] for k in range(4)] + [f(inp["od_conv_b"]), f(inp["od_gate_a_b"]), f(inp["od_gate_x_b"]), f(inp["od_lru_lambda"]),
                                              f(inp["od_conv_b"])]
    sm = np.stack(cols, axis=-1)
    sm = sm.reshape(2, 4, 128, 9).transpose(0, 2, 1, 3).reshape(2, 128, 36)
    shared = {
        "consts": make_consts(),
        "ev_w_in": np.ascontiguousarray(ev_ext), "ev_ret_norm_g": f(inp["ev_ret_norm_g"]), "ev_pool_w": f(inp["ev_pool_w"]),
        "ev_pool_scale": np.ascontiguousarray(psc), "ev_w_out": f(inp["ev_w_out"]),
        "od_w_in": np.ascontiguousarray(od_ext), "od_small": np.ascontiguousarray(sm),
        "od_gate_a_w": f(inp["od_gate_a_w"]), "od_gate_x_w": f(inp["od_gate_x_w"]), "od_w_out": f(inp["od_w_out"]),
        "ffn_w_in": f(inp["ffn_w_in"]), "ffn_w_out": f(inp["ffn_w_out"]), "ln_g": f(inp["ln_g"]), "ln_b": f(inp["ln_b"]),
    }
    return shared


def kernel(**inp):
    x = np.asarray(inp["x"], dtype=np.float32)
    pos = np.asarray(inp["positions"], dtype=np.int32)
    B, S, _ = x.shape
    ncores = 8
    NS = B // ncores
    shared = host_prep(inp)
    nc, P = build_program(NS, S, list(range(DEPTH)))
    in_maps = []
    for c in range(ncores):
        m = dict(shared)
        m["x"] = np.ascontiguousarray(x[c * NS:(c + 1) * NS].reshape(NS * S, D))
        m["pos"] = np.ascontiguousarray(pos[c * NS:(c + 1) * NS])
        in_maps.append(m)
    res = run_bass_kernel_spmd(nc, in_maps, core_ids=list(range(ncores)))
    out = np.concatenate([np.asarray(r["out"], dtype=np.float32).reshape(NS, S, D) for r in res.results], axis=0)
    return out


def phase_O1(P, G, l2, hin, hin_b):
    S, NS, NB = G.S, G.NS, G.NB
    WC = 3584
    with contextlib.ExitStack() as sc:
        P.scope = sc
        wi = P.sbuf("O_wi", [128, 8, WC], BF16)
        wi_k = [Buf(f"O_wi{k}", wi.t) for k in range(8)]
        stages = [(P.sbuf(f"O_stg{i}", [128, 512], F32), P.dsem()) for i in range(3)]
        sidx = [0]
        for k in range(8):
            for q in range(7):
                load_weight(P, stages, sidx, G.od_w_in.t[l2, k * 128:(k + 1) * 128, q * 512:(q + 1) * 512], G.od_w_in,
                            wi.t[:, k, q * 512:(q + 1) * 512], wi_k[k], 512)
        gst = P.sbuf("O_gst", [128, 4, 128], F32)
        gsd = P.dsem()
        ga = P.sbuf("O_ga", [128, 4, 128], BF16)
        gx = P.sbuf("O_gx", [128, 4, 128], BF16)
        P.op("pool", lambda e: e.memset(gst[:], 0.0), writes=[gst])
        for wsrc, dst in ((G.od_gate_a_w, ga), (G.od_gate_x_w, gx)):
            v4 = wsrc.t[l2].rearrange("(c two) i d -> two i c d", two=2)
            P.dma("sp", gst[0:64, :, 0:64], v4[0], gsd, reads=[wsrc], writes=[gst])
            P.dma("sp", gst[64:128, :, 64:128], v4[1], gsd, reads=[wsrc], writes=[gst])
            P.op("dve", lambda e, dst=dst: e.tensor_copy(out=dst[:], in_=gst[:]), reads=[gst], writes=[dst])
        spar = P.sbuf("O_spar", [128, 36], F32)
        P.dma("sp", spar[:], G.od_small.t[l2], P.dsem(), reads=[G.od_small], writes=[spar])
        sp3 = spar.t[:].rearrange("p (c k) -> p c k", k=9)
        sm_ = {n: P.sbuf(f"O_{n}", [128, 4], F32) for n in ("ex", "den", "z", "z2", "p1", "zp", "c8", "c16")}
        ex, den, z, z2, p1, zp, c8, c16 = (sm_[n] for n in ("ex", "den", "z", "z2", "p1", "zp", "c8", "c16"))
        P.op("act", lambda e: e.activation(out=ex[:], in_=sp3[:, :, 7], func=AF.Exp, scale=-1.0), reads=[spar], writes=[ex])
        P.op("dve", lambda e: e.tensor_scalar_add(out=den[:], in0=ex[:], scalar1=2.0), reads=[ex], writes=[den])
        P.op("dve", lambda e: e.reciprocal(out=den[:], in_=den[:]), reads=[den], writes=[den])
        P.op("dve", lambda e: e.tensor_tensor(out=z[:], in0=ex[:], in1=den[:], op=ALU.mult), reads=[ex, den], writes=[z])
        P.op("dve", lambda e: e.tensor_tensor(out=z2[:], in0=z[:], in1=z[:], op=ALU.mult), reads=[z], writes=[z2])
        P.op("dve", lambda e: e.tensor_scalar(out=p1[:], in0=z2[:], scalar1=1.0 / 7.0, scalar2=1.0 / 5.0, op0=ALU.mult, op1=ALU.add), reads=[z2], writes=[p1])
        P.op("dve", lambda e: e.tensor_tensor(out=p1[:], in0=p1[:], in1=z2[:], op=ALU.mult), reads=[p1, z2], writes=[p1])
        P.op("dve", lambda e: e.tensor_scalar_add(out=p1[:], in0=p1[:], scalar1=1.0 / 3.0), reads=[p1], writes=[p1])
        P.op("dve", lambda e: e.tensor_tensor(out=p1[:], in0=p1[:], in1=z2[:], op=ALU.mult), reads=[p1, z2], writes=[p1])
        P.op("dve", lambda e: e.scalar_tensor_tensor(out=zp[:], in0=p1[:], scalar=1.0, in1=z[:], op0=ALU.add, op1=ALU.mult), reads=[p1, z], writes=[zp])
        P.op("dve", lambda e: e.tensor_scalar(out=c8[:], in0=zp[:], scalar1=-16.0, scalar2=None, op0=ALU.mult), reads=[zp], writes=[c8])
        P.op("dve", lambda e: e.tensor_scalar(out=c16[:], in0=zp[:], scalar1=-32.0, scalar2=None, op0=ALU.mult), reads=[zp], writes=[c16])
        ident = P.sbuf("O_id", [128, 128], F32)
        P.dma("sp", ident[:], G.consts.t[:, C_ID:C_ID + 128], P.dsem(), reads=[G.consts], writes=[ident])
        half = P.sbuf("O_half", [128, TB], F32)
        P.op("pool", lambda e: e.memset(half[:], 0.5), writes=[half])

        cs = P.sbuf("O_cs", [128, TB], F32)
        sn = P.sbuf("O_sn", [128, TB], F32)
        csd = P.dsem()
        hb = [P.sbuf(f"O_h{j}", [128, D], F32) for j in range(NJ)]
        hds = [P.dsem() for _ in range(NJ)]
        hT = P.sbuf("O_hT", [128, 8, TB], BF16)
        hT_k = [Buf(f"O_hT{k}", hT.t) for k in range(8)]
        gl = P.sbuf("O_gl", [128, 4, TB], F32)
        uh = P.sbuf("O_uh", [128, 4, 3 + TB], F32)
        uc = P.sbuf("O_uc", [128, 4, TB], F32)
        ucb = P.sbuf("O_ucb", [128, 4, TB], BF16)
        rr = P.sbuf("O_rr", [128, 4, TB], F32)
        ii = P.sbuf("O_ii", [128, 4, TB], F32)
        aa = [P.sbuf(f"O_aa{i}", [128, TB], F32) for i in range(2)]
        w1 = [P.sbuf(f"O_w1{i}", [128, TB], F32) for i in range(2)]
        hs = [P.sbuf(f"O_hs{i}", [128, TB], F32) for i in range(2)]
        carry = P.sbuf("O_carry", [128, 4], F32)
        catl = P.sbuf("O_catl", [128, 4, TB], BF16)
        catd = P.dsem()
        qr = P.sbuf("O_qr", [128, NH, TB], BF16)
        kr = P.sbuf("O_kr", [128, NH, TB], BF16)
        qd_, kd_ = P.dsem(), P.dsem()
        t1 = P.sbuf("O_t1", [128, TB], F32)
        t2 = P.sbuf("O_t2", [128, TB], F32)
        vbt = P.sbuf("O_vbt", [128, NJ, 512], BF16)
        vds = P.dsem()
        psA = [P.psum(f"O_psA{i}", [128, 512], F32) for i in range(4)]
        psR = [P.psum(f"O_psR{i}", [128, 512], F32) for i in range(2)]
        psI = [P.psum(f"O_psI{i}", [128, 512], F32) for i in range(2)]

        def loads(b):
            t0 = b * TB
            s = t0 // S
            ts = t0 - s * S
            P.dma("sp", cs[:], G.cosT.t[s, :, ts:ts + TB], csd, reads=[G.cosT], writes=[cs])
            P.dma("sp", sn[:], G.sinT.t[s, :, ts:ts + TB], csd, reads=[G.sinT], writes=[sn])
            for j in range(NJ):
                P.dma("sp", hb[j][:], hin.t[t0 + j * 128:t0 + (j + 1) * 128, :], hds[j], reads=[hin_b[b]], writes=[hb[j]])

        def proj(X, c0, tokmajor=False, j=0):
            for k in range(8):
                if tokmajor:
                    P.op("pe", lambda e, k=k: e.matmul(X[:, :], lhsT=hT.t[:, k, j * 128:(j + 1) * 128], rhs=wi.t[:, k, c0:c0 + 512],
                                                       start=(k == 0), stop=(k == 7)), reads=[wi_k[k], hT_k[k]], writes=[X])
                else:
                    P.op("pe", lambda e, k=k: e.matmul(X[:, :], lhsT=wi.t[:, k, c0:c0 + 128], rhs=hT.t[:, k, :],
                                                       start=(k == 0), stop=(k == 7)), reads=[wi_k[k], hT_k[k]], writes=[X])

        loads(0)
        for b in range(NB):
            t0 = b * TB
            first = (t0 % S == 0)
            if first:
                P.op("pool", lambda e: e.memset(uh[:, :, 0:3], 0.0), writes=[uh])
                P.op("pool", lambda e: e.memset(carry[:], 0.0), writes=[carry])
            transpose_block(P, hb, ident, psA, hT, hT_k)
            na = 0
            for c in range(4):
                X = psA[na % 4]
                na += 1
                proj(X, c * 128)
                P.op("act", lambda e, X=X, c=c: e.activation(out=gl[:, c, :], in_=X[:, :], func=AF.Gelu_apprx_tanh), reads=[X], writes=[gl])
            for c in range(4):
                X = psA[na % 4]
                na += 1
                proj(X, 512 + c * 128)
                P.op("act", lambda e, X=X, c=c: e.activation(out=uh[:, c, 3:3 + TB], in_=X[:, :], func=AF.Copy), reads=[X], writes=[uh])
            for c in range(4):
                P.op("dve", lambda e, c=c: e.tensor_scalar(out=uc[:, c, :], in0=uh[:, c, 0:TB], scalar1=sp3[:, c, 0:1], scalar2=sp3[:, c, 4:5],
                                                          op0=ALU.mult, op1=ALU.add), reads=[uh, spar], writes=[uc])
                for k in range(1, 4):
                    P.op("dve", lambda e, c=c, k=k: e.scalar_tensor_tensor(out=uc[:, c, :], in0=uh[:, c, k:k + TB], scalar=sp3[:, c, k:k + 1],
                                                                          in1=uc[:, c, :], op0=ALU.mult, op1=ALU.add), reads=[uh, spar, uc], writes=[uc])
            P.op("pool", lambda e: e.tensor_copy(out=ucb[:], in_=uc[:]), reads=[uc], writes=[ucb])
            P.op("act", lambda e: e.activation(out=uh[:, :, 0:3], in_=uh[:, :, TB:TB + 3], func=AF.Copy), reads=[uh], writes=[uh])
            for c in range(4):
                R_, I_ = psR[c % 2], psI[c % 2]
                P.op("pe", lambda e, R_=R_, c=c: e.matmul(R_[:, :], lhsT=ga[:, c, :], rhs=ucb[:, c, :], start=True, stop=True), reads=[ga, ucb], writes=[R_])
                P.op("pe", lambda e, I_=I_, c=c: e.matmul(I_[:, :], lhsT=gx[:, c, :], rhs=ucb[:, c, :], start=True, stop=True), reads=[gx, ucb], writes=[I_])
                P.op("act", lambda e, R_=R_, c=c: e.activation(out=rr[:, c, :], in_=R_[:, :], func=AF.Sigmoid, bias=sp3[:, c, 5:6]), reads=[R_, spar], writes=[rr])
                P.op("act", lambda e, I_=I_, c=c: e.activation(out=ii[:, c, :], in_=I_[:, :], func=AF.Sigmoid, bias=sp3[:, c, 6:7]), reads=[I_, spar], writes=[ii])
            for c in range(4):
                a_, w_, h_ = aa[c % 2], w1[c % 2], hs[c % 2]
                P.op("act", lambda e, a_=a_, c=c: e.activation(out=a_[:], in_=rr[:, c, :], func=AF.Exp, scale=c8[:, c:c + 1]), reads=[rr, c8], writes=[a_])
                P.op("act", lambda e, w_=w_, c=c: e.activation(out=w_[:], in_=rr[:, c, :], func=AF.Exp, scale=c16[:, c:c + 1]), reads=[rr, c16], writes=[w_])
                P.op("dve", lambda e, w_=w_: e.tensor_scalar(out=w_[:], in0=w_[:], scalar1=-1.0, scalar2=1.0, op0=ALU.mult, op1=ALU.add), reads=[w_], writes=[w_])
                P.op("pool", lambda e, w_=w_: e.tensor_tensor(out=w_[:], in0=w_[:], in1=half[:], op=ALU.pow), reads=[w_, half], writes=[w_])
                P.op("pool", lambda e, c=c: e.tensor_tensor(out=ii[:, c, :], in0=ii[:, c, :], in1=uc[:, c, :], op=ALU.mult), reads=[ii, uc], writes=[ii])
                P.op("pool", lambda e, w_=w_, c=c: e.tensor_tensor(out=w_[:], in0=w_[:], in1=ii[:, c, :], op=ALU.mult), reads=[w_, ii], writes=[w_])
                P.op("dve", lambda e, a_=a_, w_=w_, h_=h_, c=c: e.tensor_tensor_scan(out=h_[:], data0=a_[:], data1=w_[:], initial=carry[:, c:c + 1],
                                                                                   op0=ALU.mult, op1=ALU.add), reads=[a_, w_, carry], writes=[h_])
                P.op("act", lambda e, h_=h_, c=c: e.activation(out=carry[:, c:c + 1], in_=h_[:, TB - 1:TB], func=AF.Copy), reads=[h_], writes=[carry])
                P.op("pool", lambda e, h_=h_, c=c: e.tensor_tensor(out=catl[:, c, :], in0=h_[:], in1=gl[:, c, :], op=ALU.mult), reads=[h_, gl], writes=[catl])
            P.dma("sp", G.catT.t[0:4, :, t0:t0 + TB].rearrange("c p t -> p c t"), catl[:], catd, reads=[catl], writes=[G.catT_b[b]])
            for which, col, colsw, dstb in (("q", 1024, 2560, qr), ("k", 1536, 3072, kr)):
                for hd in range(NH):
                    X, Y = psA[na % 4], psA[(na + 1) % 4]
                    na += 2
                    proj(X, col + hd * 128)
                    proj(Y, colsw + hd * 128)
                    rope_head(P, X, Y, cs, sn, t1, t2)
                    P.op("act", lambda e, hd=hd, dstb=dstb: e.activation(out=dstb[:, hd, :], in_=t1[:], func=AF.Copy), reads=[t1], writes=[dstb])
            P.dma("sp", G.qT.t[:, :, t0:t0 + TB].rearrange("h p t -> p h t"), qr[:], qd_, reads=[qr], writes=[G.qkv_b[b]])
            P.dma("sp", G.kT.t[:, :, t0:t0 + TB].rearrange("h p t -> p h t"), kr[:], kd_, reads=[kr], writes=[G.qkv_b[b]])
            for j in range(NJ):
                X = psA[na % 4]
                na += 1
                proj(X, 2048, tokmajor=True, j=j)
                P.op("act", lambda e, X=X, j=j: e.activation(out=vbt[:, j, :], in_=X[:, :], func=AF.Copy), reads=[X], writes=[vbt])
            P.dma("sp", G.vd.t[t0:t0 + TB, :].rearrange("(j p) f -> p j f", p=128), vbt[:], vds, reads=[vbt], writes=[G.qkv_b[b]])
            if b + 1 < NB:
                loads(b + 1)
        P.end_phase()
    P.scope = P.es


def phase_O2(P, G):
    S, NS, NB = G.S, G.NS, G.NB
    BPS = S // TB
    SKEW = 3
    NPT = SKEW + 3
    scale = float(DH ** -0.5)
    with contextlib.ExitStack() as sc:
        P.scope = sc
        mk32 = P.sbuf("A_mk32", [128, 256], F32)
        P.dma("sp", mk32[:], G.consts.t[:, C_MASK:C_MASK + 256], P.dsem(), reads=[G.consts], writes=[mk32])
        mask = P.sbuf("A_mask", [128, 256], BF16)
        P.op("dve", lambda e: e.tensor_copy(out=mask[:], in_=mk32[:]), reads=[mk32], writes=[mask])
        ones = P.sbuf("A_ones", [128, 128], BF16)
        P.op("pool", lambda e: e.memset(ones[:], 1.0), writes=[ones])
        qT = [P.sbuf(f"A_qT{i}", [128, S], BF16) for i in range(2)]
        kT = [P.sbuf(f"A_kT{i}", [128, S], BF16) for i in range(2)]
        qkd = [P.dsem() for _ in range(2)]
        vs = [P.sbuf(f"A_vs{i}", [128, S // 128, 128], BF16) for i in range(2)]
        vsd = [P.dsem() for _ in range(2)]
        num = P.sbuf("A_num", [128, S], F32)
        den = P.sbuf("A_den", [128, S], F32)
        y = P.sbuf("A_y", [128, S], BF16)
        yd = P.dsem()
        ebf = [P.sbuf(f"A_eb{i}", [128, 256], BF16) for i in range(3)]
        PT = [P.sbuf(f"A_PT{i}", [128, 256], BF16) for i in range(NPT)]
        psS = [P.psum(f"A_psS{i}", [128, 512], F32) for i in range(3)]
        psN = [P.psum(f"A_psN{i}", [128, 512], F32) for i in range(2)]
        psD = [P.psum(f"A_psD{i}", [128, 512], F32) for i in range(2)]
        vcount = 0
        gcount = 0
        iters = [(s, hd) for s in range(NS) for hd in range(NH)]

        def qk_load(n):
            s, hd = iters[n]
            sblk = [G.qkv_b[s * BPS + bb] for bb in range(BPS)]
            i = n % 2
            P.dma("sp", qT[i][:], G.qT.t[hd, :, s * S:(s + 1) * S], qkd[i], reads=sblk, writes=[qT[i]])
            P.dma("sp", kT[i][:], G.kT.t[hd, :, s * S:(s + 1) * S], qkd[i], reads=sblk, writes=[kT[i]])

        qk_load(0)
        for n, (s, hd) in enumerate(iters):
            if True:
                sblk = [G.qkv_b[s * BPS + bb] for bb in range(BPS)]
                i = n % 2
                q_, k_ = qT[i], kT[i]
                if n + 1 < len(iters):
                    qk_load(n + 1)
                per = []
                for pi, (W, dil) in enumerate(DIL_PATTERNS):
                    nb = S // dil // 128
                    v_ = vs[vcount % 2]
                    vdm = vsd[vcount % 2]
                    vcount += 1
                    lds, bks = [], []
                    for r in range(dil):
                        src = G.vd.t[s * S + r:(s + 1) * S:dil, hd * 128:(hd + 1) * 128].rearrange("(kb p) e -> p kb e", p=128)
                        lds.append(("load", v_, vdm, r, nb, src))
                        for kb in range(nb):
                            bks.append(("blk", pi, dil, nb, r, kb, v_))
                    per.append((lds, bks))
                tasks = list(per[0][0])
                for pi in range(len(per)):
                    bks = per[pi][1]
                    tasks += bks[:8]
                    if pi + 1 < len(per):
                        tasks += per[pi + 1][0]
                    tasks += bks[8:]
                blks = [t for t in tasks if t[0] == "blk"]
                pending = []
                ptmap = {}
                bi = 0

                def stage2(idx):
                    nonlocal gcount
                    _, pi, dil, nb, r, qb, v_ = blks[idx]
                    gi = qb % 4
                    N, Dn = psN[gcount % 2], psD[gcount % 2]
                    reg = slice(gi * 128, (gi + 1) * 128)
                    ptc = ptmap[idx]
                    if qb >= 1:
                        ptp = ptmap[idx - 1]
                        for dst, lw, lwB in ((N, v_, None), (Dn, None, ones)):
                            l0 = v_[:, r * nb + qb - 1, :] if lwB is None else ones[:]
                            l1 = v_[:, r * nb + qb, :] if lwB is None else ones[:]
                            rb = v_ if lwB is None else ones
                            P.op("pe", lambda e, dst=dst, l0=l0, ptp=ptp: e.matmul(dst[:, reg], lhsT=l0, rhs=ptp[:, 128:256], start=True, stop=False),
                                 reads=[rb, ptp], writes=[dst])
                            P.op("pe", lambda e, dst=dst, l1=l1, ptc=ptc: e.matmul(dst[:, reg], lhsT=l1, rhs=ptc[:, 0:128], start=False, stop=True),
                                 reads=[rb, ptc], writes=[dst])
                    else:
                        P.op("pe", lambda e: e.matmul(N[:, reg], lhsT=v_[:, r * nb + qb, :], rhs=ptc[:, 0:128], start=True, stop=True),
                             reads=[v_, ptc], writes=[N])
                        P.op("pe", lambda e: e.matmul(Dn[:, reg], lhsT=ones[:], rhs=ptc[:, 0:128], start=True, stop=True),
                             reads=[ones, ptc], writes=[Dn])
                    if gi == 3 or qb == nb - 1:
                        ng = gi + 1
                        qb0 = qb - gi
                        lo = r + dil * 128 * qb0
                        hi = lo + dil * (128 * ng - 1) + 1
                        nsl = num[:, lo:hi:dil]
                        dsl = den[:, lo:hi:dil]
                        if pi == 0:
                            P.op("act", lambda e: e.activation(out=nsl, in_=N[:, 0:ng * 128], func=AF.Copy), reads=[N], writes=[num])
                            P.op("dve", lambda e: e.tensor_copy(out=dsl, in_=Dn[:, 0:ng * 128]), reads=[Dn], writes=[den])
                        else:
                            P.op("dve", lambda e: e.tensor_tensor(out=nsl, in0=nsl, in1=N[:, 0:ng * 128], op=ALU.add), reads=[num, N], writes=[num])
                            P.op("dve", lambda e: e.tensor_tensor(out=dsl, in0=dsl, in1=Dn[:, 0:ng * 128], op=ALU.add), reads=[den, Dn], writes=[den])
                        gcount += 1

                for t in tasks:
                    if t[0] == "load":
                        _, v_, vdm, r, nb, src = t
                        P.dma("sp", v_[:, r * nb:(r + 1) * nb, :], src, vdm, reads=sblk, writes=[v_])
                        continue
                    _, pi, dil, nb, r, kb, v_ = t
                    nq = 256 if kb < nb - 1 else 128
                    base = r + dil * 128 * kb
                    ksl = k_[:, base:base + dil * 127 + 1:dil]
                    qsl = q_[:, base:base + dil * (nq - 1) + 1:dil]
                    Sb = psS[bi % 3]
                    eb = ebf[bi % 3]
                    pt = PT[bi % NPT]
                    ptmap[bi] = pt
                    P.op("pe", lambda e, Sb=Sb, ksl=ksl, qsl=qsl, nq=nq: e.matmul(Sb[:, 0:nq], lhsT=ksl, rhs=qsl, start=True, stop=True),
                         reads=[k_, q_], writes=[Sb])
                    P.op("act", lambda e, Sb=Sb, eb=eb, nq=nq: e.activation(out=eb[:, 0:nq], in_=Sb[:, 0:nq], func=AF.Exp, scale=scale),
                         reads=[Sb], writes=[eb])
                    P.op("pool", lambda e, eb=eb, pt=pt, nq=nq: e.tensor_tensor(out=pt[:, 0:nq], in0=eb[:, 0:nq], in1=mask[:, 0:nq], op=ALU.mult),
                         reads=[eb, mask], writes=[pt])
                    pending.append(bi)
                    bi += 1
                    if len(pending) > SKEW:
                        stage2(pending.pop(0))
                while pending:
                    stage2(pending.pop(0))
                P.op("dve", lambda e: e.reciprocal(out=den[:], in_=den[:]), reads=[den], writes=[den])
                P.op("pool", lambda e: e.tensor_tensor(out=y[:], in0=num[:], in1=den[:], op=ALU.mult), reads=[num, den], writes=[y])
                P.dma("sp", G.catT.t[4 + hd, :, s * S:(s + 1) * S], y[:], yd, reads=[y], writes=[G.catT_b[s * BPS + bb] for bb in range(BPS)])
        P.end_phase()
    P.scope = P.es
```

```python
import contextlib
import math
import numpy as np
import concourse.bass as bass
import concourse.mybir as mybir
from concourse.bass_utils import run_bass_kernel_spmd

F32 = mybir.dt.float32
BF16 = mybir.dt.bfloat16
I32 = mybir.dt.int32
AF = mybir.ActivationFunctionType
ALU = mybir.AluOpType
AX = mybir.AxisListType

D = 1024
DFF = 2816
NH = 4
DH = 128
TB = 512
NJ = TB // 128
DEPTH = 4
ALPHA = float((2 * DEPTH) ** 0.25)
EPS = 1e-5
POOL_W = (2, 4, 8, 16)
DIL_PATTERNS = ((128, 1), (512, 4), (2048, 16))
EPOCH = 30000

C_INVF, C_SGN, C_ID, C_DEC, C_KDEC, C_CDEC, C_QDEC, C_MASK, C_PCNT, C_END = (
    0, 1, 2, 130, 642, 1154, 1666, 3714, 3970, 4034)


class Buf:
    __slots__ = ("name", "t", "excl", "lw", "rd", "rd_dma")

    def __init__(self, name, t=None, excl=False):
        self.name = name
        self.t = t
        self.excl = excl
        self.lw = None
        self.rd = {}
        self.rd_dma = []

    def __getitem__(self, k):
        return self.t[k]


class DSem:
    __slots__ = ("sem", "cnt")

    def __init__(self, sem):
        self.sem = sem
        self.cnt = 0


class Op:
    __slots__ = ("eng", "fn", "deps", "sig", "sem", "val", "isdma", "dsem")

    def __init__(self, eng, fn, isdma=False, dsem=None):
        self.eng = eng
        self.fn = fn
        self.deps = ()
        self.sig = False
        self.sem = None
        self.val = 0
        self.isdma = isdma
        self.dsem = dsem


class Prog:
    ENGS = ("pe", "act", "dve", "pool", "sp")

    def __init__(self, nc):
        self.nc = nc
        self.es = contextlib.ExitStack()
        self.scope = self.es
        self.ops = []
        self.eng = {"pe": nc.tensor, "act": nc.scalar, "dve": nc.vector, "pool": nc.gpsimd, "sp": nc.sync}
        self.nsem = 0
        self.nbuf = 0
        self.cnt = {}
        self.cursem = {}
        self.waited = {e: {} for e in self.ENGS}
        self.pending_dma = []
        self.nwait = 0
        self.ninst = 0
        self.dsem_pool = []
        self.phase_dsems = []
        self.bar_t = self.es.enter_context(self.nc.sbuf_tensor("bar_scr", [128, 8], F32))

    def sbuf(self, name, shape, dt):
        self.nbuf += 1
        t = self.scope.enter_context(self.nc.sbuf_tensor(f"{name}_{self.nbuf}", list(shape), dt))
        return Buf(name, t)

    def psum(self, name, shape, dt):
        self.nbuf += 1
        t = self.scope.enter_context(self.nc.psum_tensor(f"{name}_{self.nbuf}", list(shape), dt))
        return Buf(name, t, excl=True)

    def dram(self, name, shape, dt, kind="Internal"):
        t = self.nc.dram_tensor(name, list(shape), dt, kind=kind)
        return Buf(name, t.ap())

    def new_sem(self, name=None):
        self.nsem += 1
        return self.es.enter_context(self.nc.semaphore(name or f"s{self.nsem}"))

    def dsem(self):
        d = self.dsem_pool.pop() if self.dsem_pool else DSem(self.new_sem())
        self.phase_dsems.append(d)
        return d

    def end_phase(self):
        self.barrier()
        self.flush()
        self.dsem_pool.extend(self.phase_dsems)
        self.phase_dsems = []

    def _add(self, op, reads, writes):
        eng = op.eng
        deps = set()
        for b in reads:
            if b.lw is not None:
                deps.add(b.lw)
            if b.excl:
                for e, r in b.rd.items():
                    if e != eng:
                        deps.add(r)
        for b in writes:
            if b.lw is not None:
                deps.add(b.lw)
            for r in b.rd.values():
                deps.add(r)
            for r in b.rd_dma:
                deps.add(r)
        if eng == "pe" and not op.isdma:
            deps = {d for d in deps if d.isdma or d.eng != "pe"}
        op.deps = tuple(deps)
        for d in deps:
            d.sig = True
        for b in reads:
            if op.isdma:
                b.rd_dma.append(op)
            else:
                b.rd[eng] = op
        for b in writes:
            b.lw = op
            b.rd = {}
            b.rd_dma = []
        self.ops.append(op)
        return op

    def op(self, eng, fn, reads=(), writes=()):
        return self._add(Op(eng, fn), reads, writes)

    def dma(self, q, out, in_, dsem, reads=(), writes=(), **kw):
        def fn(e, out=out, in_=in_, kw=kw):
            return e.dma_start(out=out, in_=in_, **kw)
        o = Op(q, fn, isdma=True, dsem=dsem)
        o.sig = True
        self.pending_dma.append(o)
        return self._add(o, reads, writes)

    def barrier(self):
        if self.bar_t is None:
            t = self.es.enter_context(self.nc.sbuf_tensor("bar_scr", [128, 8], F32))
            self.bar_t = t
        t = self.bar_t
        first = []
        for i, e in enumerate(("act", "dve", "pool")):
            if e == "act":
                o = Op(e, (lambda en, i=i: en.memzero(t[:, i:i + 1])))
            else:
                o = Op(e, (lambda en, i=i: en.memset(t[:, i:i + 1], 0.0)))
            o.sig = True
            self.ops.append(o)
            first.append(o)
        last_pe = None
        for o in reversed(self.ops):
            if o.eng == "pe" and not o.isdma:
                last_pe = o
                break
        if last_pe is not None:
            last_pe.sig = True
            first.append(last_pe)
        deps = tuple(first) + tuple(self.pending_dma)
        self.pending_dma = []
        self.join_deps = deps

    def flush(self, with_join=True):
        self._emit()
        jd = getattr(self, "join_deps", None)
        if jd:
            for en in self.ENGS:
                self._waits(en, jd)
            self.join_deps = None

    def _assign(self, op):
        if op.isdma:
            op.dsem.cnt += 16
            op.sem = op.dsem.sem
            op.val = op.dsem.cnt
        elif op.sig:
            e = op.eng
            if e not in self.cursem or self.cnt[e] >= EPOCH:
                self.cursem[e] = self.new_sem(f"e_{e}_{self.nsem}")
                self.cnt[e] = 0
            self.cnt[e] += 1
            op.sem = self.cursem[e]
            op.val = self.cnt[e]

    def _waits(self, en, deps):
        e = self.eng[en]
        w = self.waited[en]
        need = {}
        for d in deps:
            k = id(d.sem)
            if w.get(k, 0) >= d.val:
                continue
            if k not in need or need[k][1] < d.val:
                need[k] = (d.sem, d.val)
        for k, (s, v) in need.items():
            e.wait_ge(s, v)
            w[k] = v
            self.nwait += 1

    def _emit(self):
        for op in self.ops:
            self._assign(op)
        for op in self.ops:
            self._waits(op.eng, op.deps)
            ins = op.fn(self.eng[op.eng])
            if op.isdma:
                ins.then_inc(op.sem, 16)
            elif op.sig:
                ins.then_inc(op.sem, 1)
            op.fn = None
            self.ninst += 1
        self.ops = []

    def finish(self, final_ops=()):
        self._emit()
        e = self.eng["sp"]
        for d in final_ops:
            e.wait_ge(d.sem, d.val)
        self.es.close()


def make_consts():
    c = np.zeros((128, C_END), np.float64)
    inv = 10000.0 ** (-(np.arange(0, DH, 2, dtype=np.float32) / np.float32(DH)).astype(np.float32))
    inv = inv.astype(np.float32)
    p = np.arange(128)
    c[:, C_INVF] = inv[p % 64]
    c[:, C_SGN] = np.where(p < 64, -1.0, 1.0)
    c[:, C_ID:C_ID + 128] = np.eye(128)
    lg = np.log1p(-(2.0 ** (-5.0 - np.arange(NH, dtype=np.float64))))
    i = np.arange(128)
    sc = DH ** -0.5
    for h in range(NH):
        rel = i[None, :] - i[:, None]
        c[:, C_DEC + h * 128:C_DEC + (h + 1) * 128] = np.where(rel >= 0, sc * np.exp(np.maximum(rel, 0) * lg[h]), 0.0)
        c[:, C_KDEC + h * 128:C_KDEC + (h + 1) * 128] = (sc * np.exp((127 - i) * lg[h]))[:, None]
        c[:, C_CDEC + h * 128:C_CDEC + (h + 1) * 128] = np.exp(128 * lg[h])
        t = np.arange(TB)
        c[:, C_QDEC + h * TB:C_QDEC + (h + 1) * TB] = np.exp(((t % 128) + 1) * lg[h])[None, :]
    cc = np.arange(256)
    c[:, C_MASK:C_MASK + 256] = ((cc[None, :] >= p[:, None]) & (cc[None, :] <= p[:, None] + 128)).astype(np.float64)
    for g, w in enumerate(POOL_W):
        t = np.arange(16)
        c[:, C_PCNT + g * 16:C_PCNT + (g + 1) * 16] = (1.0 / np.minimum(t + 1, w))[None, :]
    return c.astype(np.float32)


class Ctx:
    pass


def load_weight(P, stages, sidx, src_ap, srcBuf, dst_ap, dstBuf, ncols):
    st, ds = stages[sidx[0] % len(stages)]
    ce = ("act", "pool", "dve")[sidx[0] % 3]
    sidx[0] += 1
    P.dma("sp", st[:, 0:ncols], src_ap, ds, reads=[srcBuf], writes=[st])
    if ce == "act":
        P.op("act", lambda e: e.activation(out=dst_ap, in_=st[:, 0:ncols], func=AF.Copy), reads=[st], writes=[dstBuf])
    else:
        P.op(ce, lambda e: e.tensor_copy(out=dst_ap, in_=st[:, 0:ncols]), reads=[st], writes=[dstBuf])


def bulk_load(P, items, width, nstage=8):
    outer = P.scope
    with contextlib.ExitStack() as st:
        P.scope = st
        stages = [(P.sbuf(f"stg{i}", [128, width], F32), P.dsem()) for i in range(nstage)]
        for n_, (src_ap, srcBuf, dst_ap, dstBuf, ncols) in enumerate(items):
            stg, ds = stages[n_ % nstage]
            P.dma("sp" if n_ % 2 == 0 else "act", stg[:, 0:ncols], src_ap, ds, reads=[srcBuf], writes=[stg])
            if n_ % 2 == 0:
                P.op("dve", lambda e, stg=stg, dst_ap=dst_ap, ncols=ncols: e.tensor_copy(out=dst_ap, in_=stg[:, 0:ncols]), reads=[stg], writes=[dstBuf])
            else:
                P.op("act", lambda e, stg=stg, dst_ap=dst_ap, ncols=ncols: e.activation(out=dst_ap, in_=stg[:, 0:ncols], func=AF.Copy), reads=[stg], writes=[dstBuf])
        P.barrier()
        P.flush()
    P.scope = outer


def ln_A(P, r_ap, rBuf, sm, k):
    i = k % 3
    st, mv, ve, nm, rs, nb, nh = sm["st"][i], sm["mv"][i], sm["ve"][i], sm["nm"][i], sm["rs"][i], sm["nb"][i], sm["nh"]
    for h in range(2):
        P.op("dve", lambda e, h=h: e.bn_stats(out=st[:, h, :], in_=r_ap[:, h * 512:(h + 1) * 512]), reads=[rBuf], writes=[st])
    P.op("dve", lambda e: e.bn_aggr(out=mv[:], in_=st[:].rearrange("p a b -> p (a b)")), reads=[st], writes=[mv])
    P.op("dve", lambda e: e.tensor_scalar_add(out=ve[:], in0=mv[:, 1:2], scalar1=EPS), reads=[mv], writes=[ve])
    P.op("dve", lambda e: e.tensor_scalar(out=nm[:], in0=mv[:, 0:1], scalar1=-1.0, scalar2=None, op0=ALU.mult), reads=[mv], writes=[nm])
    P.op("pool", lambda e: e.tensor_tensor(out=rs[:], in0=ve[:], in1=nh[:, 0:1], op=ALU.pow), reads=[ve, nh], writes=[rs])
    P.op("pool", lambda e: e.tensor_tensor(out=nb[:], in0=nm[:], in1=rs[:], op=ALU.mult), reads=[nm, rs], writes=[nb])
    return rs, nb


def ln_B(P, r_ap, rBuf, o_ap, oBuf, g_tab, b_tab, gbBuf, rs, nb):
    P.op("act", lambda e: e.activation(out=o_ap, in_=r_ap, func=AF.Identity, bias=nb[:], scale=rs[:]), reads=[rBuf, nb, rs], writes=[oBuf])
    P.op("dve", lambda e: e.tensor_tensor(out=o_ap, in0=o_ap, in1=g_tab, op=ALU.mult), reads=[oBuf, gbBuf], writes=[oBuf])
    P.op("pool", lambda e: e.tensor_tensor(out=o_ap, in0=o_ap, in1=b_tab, op=ALU.add), reads=[oBuf, gbBuf], writes=[oBuf])


def ln_smalls(P, tag):
    sm = {}
    sm["st"] = [P.sbuf(f"{tag}_st{i}", [128, 2, 6], F32) for i in range(3)]
    sm["mv"] = [P.sbuf(f"{tag}_mv{i}", [128, 2], F32) for i in range(3)]
    for n in ("ve", "nm", "rs", "nb"):
        sm[n] = [P.sbuf(f"{tag}_{n}{i}", [128, 1], F32) for i in range(3)]
    sm["nh"] = P.sbuf(f"{tag}_nh", [128, 16], F32)
    P.op("pool", lambda e: e.memset(sm["nh"][:], -0.5), writes=[sm["nh"]])
    return sm


def blk_bufs(name, ap, nblk):
    return [Buf(f"{name}{b}", ap) for b in range(nblk)]


def phase_P(P, G, w_out_ap, wBuf, lng_ap, lnb_ap, lnBuf, hin, hin_b, hout, hout_b):
    NT, NB = G.NT, G.NB
    with contextlib.ExitStack() as sc:
        P.scope = sc
        wo = P.sbuf("P_wo", [128, 8, D], BF16)
        wo_k = [Buf(f"P_wo{k}", wo.t) for k in range(8)]
        bulk_load(P, [(w_out_ap[k * 128:(k + 1) * 128, :], wBuf, wo.t[:, k, :], wo_k[k], 1024) for k in range(8)], 1024)
        gb = P.sbuf("P_gb", [128, 2, D], F32)
        gds = P.dsem()
        P.dma("sp", gb[:, 0, :], lng_ap.partition_broadcast(128), gds, reads=[lnBuf], writes=[gb])
        P.dma("sp", gb[:, 1, :], lnb_ap.partition_broadcast(128), gds, reads=[lnBuf], writes=[gb])
        sm = ln_smalls(P, "P")
        ct = [P.sbuf(f"P_ct{i}", [128, 8, TB], BF16) for i in range(2)]
        ctd = [P.dsem() for _ in range(2)]
        hb = [[P.sbuf(f"P_h{i}_{j}", [128, D], F32) for j in range(NJ)] for i in range(2)]
        hd_ = [[P.dsem() for j in range(NJ)] for i in range(2)]
        ob = [P.sbuf(f"P_o{i}", [128, D], F32) for i in range(2)]
        od = [P.dsem() for _ in range(2)]
        ps = [P.psum(f"P_ps{i}", [128, 512], F32) for i in range(4)]

        def loads(b):
            i = b % 2
            t0 = b * TB
            P.dma("sp", ct[i][:], G.catT.t[:, :, t0:t0 + TB].rearrange("c p t -> p c t"), ctd[i], reads=[G.catT_b[b]], writes=[ct[i]])
            for j in range(NJ):
                P.dma("sp", hb[i][j][:], hin.t[t0 + j * 128:t0 + (j + 1) * 128, :], hd_[i][j], reads=[hin_b[b]], writes=[hb[i][j]])

        loads(0)
        kk = 0
        pend = None
        for b in range(NB):
            i = b % 2
            t0 = b * TB
            if b + 1 < NB:
                loads(b + 1)
            for j in range(NJ):
                for n in range(2):
                    pb = ps[(2 * j + n) % 4]
                    for k in range(8):
                        P.op("pe", lambda e, pb=pb, k=k, j=j, n=n, i=i: e.matmul(
                            pb[:, :], lhsT=ct[i][:, k, j * 128:(j + 1) * 128], rhs=wo.t[:, k, n * 512:(n + 1) * 512],
                            start=(k == 0), stop=(k == 7)), reads=[ct[i], wo_k[k]], writes=[pb])
                    P.op("dve", lambda e, pb=pb, j=j, n=n, i=i: e.scalar_tensor_tensor(
                        out=hb[i][j][:, n * 512:(n + 1) * 512], in0=hb[i][j][:, n * 512:(n + 1) * 512], scalar=ALPHA,
                        in1=pb[:, :], op0=ALU.mult, op1=ALU.add), reads=[hb[i][j], pb], writes=[hb[i][j]])
                rs_, nb_ = ln_A(P, hb[i][j][:], hb[i][j], sm, kk)

                def fin_tile(j=j, kk=kk, rs_=rs_, nb_=nb_, i=i, t0=t0, b=b):
                    o = ob[kk % 2]
                    ln_B(P, hb[i][j][:], hb[i][j], o[:], o, gb[:, 0, :], gb[:, 1, :], gb, rs_, nb_)
                    P.dma("sp", hout.t[t0 + j * 128:t0 + (j + 1) * 128, :], o[:], od[kk % 2], reads=[o], writes=[hout_b[b]])
                if pend is not None:
                    pend()
                pend = fin_tile
                kk += 1
            pend()
            pend = None
        P.end_phase()
    P.scope = P.es


def phase_F(P, G, wi_ap, wiBuf, wf_ap, wfBuf, lng_ap, lnb_ap, lnBuf, hin, hin_b, hout, hout_b):
    NT, NB = G.NT, G.NB
    NC = DFF // 128
    fin = []
    with contextlib.ExitStack() as sc:
        P.scope = sc
        wi = P.sbuf("F_wi", [128, 8, 2 * DFF], BF16)
        wi_k = [Buf(f"F_wi{k}", wi.t) for k in range(8)]
        wf = P.sbuf("F_wf", [128, NC, D], BF16)
        wf_k = [Buf(f"F_wf{k}", wf.t) for k in range(NC)]
        items = []
        for k in range(8):
            for q in range(4):
                items.append((wi_ap[k * 128:(k + 1) * 128, q * 1408:(q + 1) * 1408], wiBuf, wi.t[:, k, q * 1408:(q + 1) * 1408], wi_k[k], 1408))
        for k in range(NC):
            items.append((wf_ap[k * 128:(k + 1) * 128, :], wfBuf, wf.t[:, k, :], wf_k[k], 1024))
        bulk_load(P, items, 1408)
        gb = P.sbuf("F_gb", [128, 2, D], F32)
        gds = P.dsem()
        P.dma("sp", gb[:, 0, :], lng_ap.partition_broadcast(128), gds, reads=[lnBuf], writes=[gb])
        P.dma("sp", gb[:, 1, :], lnb_ap.partition_broadcast(128), gds, reads=[lnBuf], writes=[gb])
        ident = P.sbuf("F_id", [128, 128], F32)
        P.dma("sp", ident[:], G.consts.t[:, C_ID:C_ID + 128], P.dsem(), reads=[G.consts], writes=[ident])
        sm = ln_smalls(P, "F")
        NR = 6
        hbr = [P.sbuf(f"F_h{j}", [128, D], F32) for j in range(NR)]
        hdr = [P.dsem() for j in range(NR)]
        ob = [P.sbuf(f"F_o{i}", [128, D], F32) for i in range(2)]
        od = [P.dsem() for _ in range(2)]
        hT = P.sbuf("F_hT", [128, 8, TB], BF16)
        hT_k = [Buf(f"F_hT{k}", hT.t) for k in range(8)]
        act = P.sbuf("F_act", [128, NC, TB], BF16)
        act_k = [Buf(f"F_act{k}", act.t) for k in range(NC)]
        sg = [P.sbuf(f"F_sg{i}", [128, TB], F32) for i in range(2)]
        psO = [P.psum(f"F_psO{i}", [128, 512], F32) for i in range(2)]
        psG = [P.psum(f"F_psG{i}", [128, 512], F32) for i in range(2)]
        psU = [P.psum(f"F_psU{i}", [128, 512], F32) for i in range(2)]
        psT = [P.psum(f"F_psT{i}", [128, 512], F32) for i in range(2)]

        def load_tile(b, j):
            t0 = b * TB
            r = (4 * b + j) % NR
            P.dma("sp", hbr[r][:], hin.t[t0 + j * 128:t0 + (j + 1) * 128, :], hdr[r], reads=[hin_b[b]], writes=[hbr[r]])

        for j in range(NJ):
            load_tile(0, j)
        if NB > 1:
            load_tile(1, 0)
            load_tile(1, 1)
        kk = 0
        pend = None

        def do_transposes(b):
            hb = [hbr[(4 * b + j) % NR] for j in range(NJ)]
            for c in range(8):
                pt = psT[c % 2]
                for j in range(NJ):
                    P.op("pe", lambda e, pt=pt, j=j, c=c, hb=hb: e.transpose(out=pt[:, j * 128:(j + 1) * 128],
                                                                             in_=hb[j][:, c * 128:(c + 1) * 128], identity=ident[:]),
                         reads=[hb[j], ident], writes=[pt])
                if c % 2 == 0:
                    P.op("dve", lambda e, pt=pt, c=c: e.tensor_copy(out=hT.t[:, c, :], in_=pt[:, :]), reads=[pt], writes=[hT_k[c]])
                else:
                    P.op("act", lambda e, pt=pt, c=c: e.activation(out=hT.t[:, c, :], in_=pt[:, :], func=AF.Copy), reads=[pt], writes=[hT_k[c]])

        do_transposes(0)
        for b in range(NB):
            t0 = b * TB
            hb = [hbr[(4 * b + j) % NR] for j in range(NJ)]
            for c in range(NC):
                pg, pu, s = psG[c % 2], psU[c % 2], sg[c % 2]
                for k in range(8):
                    P.op("pe", lambda e, pg=pg, k=k, c=c: e.matmul(pg[:, :], lhsT=wi.t[:, k, c * 128:(c + 1) * 128], rhs=hT.t[:, k, :],
                                                                   start=(k == 0), stop=(k == 7)), reads=[wi_k[k], hT_k[k]], writes=[pg])
                for k in range(8):
                    P.op("pe", lambda e, pu=pu, k=k, c=c: e.matmul(pu[:, :], lhsT=wi.t[:, k, DFF + c * 128:DFF + (c + 1) * 128], rhs=hT.t[:, k, :],
                                                                   start=(k == 0), stop=(k == 7)), reads=[wi_k[k], hT_k[k]], writes=[pu])
                P.op("act", lambda e, pg=pg, s=s: e.activation(out=s[:], in_=pg[:, :], func=AF.Silu), reads=[pg], writes=[s])
                P.op("dve", lambda e, pu=pu, s=s, c=c: e.tensor_tensor(out=act.t[:, c, :], in0=s[:], in1=pu[:, :], op=ALU.mult),
                     reads=[s, pu], writes=[act_k[c]])
            for j in range(NJ):
                for n in range(2):
                    pb = psO[n]
                    for k in range(NC):
                        P.op("pe", lambda e, pb=pb, k=k, j=j, n=n: e.matmul(
                            pb[:, :], lhsT=act.t[:, k, j * 128:(j + 1) * 128], rhs=wf.t[:, k, n * 512:(n + 1) * 512],
                            start=(k == 0), stop=(k == NC - 1)), reads=[act_k[k], wf_k[k]], writes=[pb])
                    P.op("dve", lambda e, pb=pb, j=j, n=n, hb=hb: e.scalar_tensor_tensor(
                        out=hb[j][:, n * 512:(n + 1) * 512], in0=hb[j][:, n * 512:(n + 1) * 512], scalar=ALPHA,
                        in1=pb[:, :], op0=ALU.mult, op1=ALU.add), reads=[hb[j], pb], writes=[hb[j]])
                    if j == NJ - 1 and n == 0 and b + 1 < NB:
                        do_transposes(b + 1)
                rs_, nb_ = ln_A(P, hb[j][:], hb[j], sm, kk)

                def fin_tile(j=j, kk=kk, rs_=rs_, nb_=nb_, t0=t0, b=b, hb=hb):
                    o = ob[kk % 2]
                    ln_B(P, hb[j][:], hb[j], o[:], o, gb[:, 0, :], gb[:, 1, :], gb, rs_, nb_)
                    if j < 2 and b + 1 < NB:
                        load_tile(b + 1, j + 2)
                    if j >= 2 and b + 2 < NB:
                        load_tile(b + 2, j - 2)
                    fin.append(P.dma("sp", hout.t[t0 + j * 128:t0 + (j + 1) * 128, :], o[:], od[kk % 2], reads=[o], writes=[hout_b[b]]))
                if pend is not None:
                    pend()
                pend = fin_tile
                kk += 1
            pend()
            pend = None
        P.end_phase()
    P.scope = P.es
    return fin


def phase_R(P, G):
    S, NS = G.S, G.NS
    MAGIC = 12582912.0
    HI = 6.28125
    LO = 2.0 * math.pi - HI
    PIL = 3.1415925
    with contextlib.ExitStack() as sc:
        P.scope = sc
        cs = P.sbuf("R_cs", [128, 2], F32)
        P.dma("sp", cs[:], G.consts.t[:, 0:2], P.dsem(), reads=[G.consts], writes=[cs])
        pi = P.sbuf("R_pi", [128, S], I32)
        ang = P.sbuf("R_ang", [128, S], F32)
        kq = P.sbuf("R_k", [128, S], F32)
        r = P.sbuf("R_r", [128, S], F32)
        oc = P.sbuf("R_oc", [128, S], F32)
        os_ = P.sbuf("R_os", [128, S], F32)
        d1, d2, d3 = P.dsem(), P.dsem(), P.dsem()
        for s in range(NS):
            P.dma("sp", pi[:], G.pos.t[s:s + 1, :].partition_broadcast(128), d1, reads=[G.pos], writes=[pi])
            P.op("dve", lambda e: e.tensor_copy(out=ang[:], in_=pi[:]), reads=[pi], writes=[ang])
            P.op("dve", lambda e: e.tensor_scalar(out=ang[:], in0=ang[:], scalar1=cs[:, 0:1], scalar2=None, op0=ALU.mult),
                 reads=[ang, cs], writes=[ang])
            P.op("dve", lambda e: e.tensor_scalar(out=kq[:], in0=ang[:], scalar1=float(1.0 / (2.0 * math.pi)), scalar2=MAGIC,
                                                  op0=ALU.mult, op1=ALU.add), reads=[ang], writes=[kq])
            P.op("dve", lambda e: e.tensor_scalar_add(out=kq[:], in0=kq[:], scalar1=-MAGIC), reads=[kq], writes=[kq])
            P.op("dve", lambda e: e.scalar_tensor_tensor(out=r[:], in0=kq[:], scalar=-HI, in1=ang[:], op0=ALU.mult, op1=ALU.add),
                 reads=[kq, ang], writes=[r])
            P.op("dve", lambda e: e.scalar_tensor_tensor(out=r[:], in0=kq[:], scalar=-LO, in1=r[:], op0=ALU.mult, op1=ALU.add),
                 reads=[kq, r], writes=[r])
            P.op("dve", lambda e: e.tensor_scalar(out=r[:], in0=r[:], scalar1=-PIL, scalar2=PIL, op0=ALU.max, op1=ALU.min),
                 reads=[r], writes=[r])
            P.op("act", lambda e: e.activation(out=os_[:], in_=r[:], func=AF.Sin, scale=cs[:, 1:2]), reads=[r, cs], writes=[os_])
            P.op("dve", lambda e: e.scalar_tensor_tensor(out=kq[:], in0=r[:], scalar=-1.0, in1=r[:], op0=ALU.mult, op1=ALU.max), reads=[r], writes=[kq])
            P.op("act", lambda e: e.activation(out=oc[:], in_=kq[:], func=AF.Sin, scale=-1.0, bias=float(math.pi / 2)),
                 reads=[kq], writes=[oc])
            P.dma("sp", G.cosT.t[s, :, :], oc[:], d2, reads=[oc], writes=[G.cosT])
            P.dma("sp", G.sinT.t[s, :, :], os_[:], d3, reads=[os_], writes=[G.sinT])
        P.end_phase()
    P.scope = P.es


def transpose_block(P, hb, ident, psT, hT, hT_k):
    for c in range(8):
        pt = psT[c % len(psT)]
        for j in range(NJ):
            P.op("pe", lambda e, pt=pt, j=j, c=c: e.transpose(out=pt[:, j * 128:(j + 1) * 128],
                                                              in_=hb[j][:, c * 128:(c + 1) * 128], identity=ident[:]),
                 reads=[hb[j], ident], writes=[pt])
        if c % 2 == 0:
            P.op("dve", lambda e, pt=pt, c=c: e.tensor_copy(out=hT.t[:, c, :], in_=pt[:, :]), reads=[pt], writes=[hT_k[c]])
        else:
            P.op("act", lambda e, pt=pt, c=c: e.activation(out=hT.t[:, c, :], in_=pt[:, :], func=AF.Copy), reads=[pt], writes=[hT_k[c]])


def rope_head(P, X, Y, cs, sn, t1, t2):
    P.op("dve", lambda e: e.tensor_tensor(out=t1[:], in0=X[:, :], in1=cs[:], op=ALU.mult), reads=[X, cs], writes=[t1])
    P.op("dve", lambda e: e.tensor_tensor(out=t2[:], in0=Y[:, :], in1=sn[:], op=ALU.mult), reads=[Y, sn], writes=[t2])
    P.op("pool", lambda e: e.tensor_tensor(out=t1[:], in0=t1[:], in1=t2[:], op=ALU.add), reads=[t1, t2], writes=[t1])


def phase_E(P, G, l2, hin, hin_b):
    S, NS, NB = G.S, G.NS, G.NB
    BPS = S // TB
    WC = 3584
    with contextlib.ExitStack() as sc:
        P.scope = sc
        wi = P.sbuf("E_wi", [128, 8, WC], BF16)
        wi_k = [Buf(f"E_wi{k}", wi.t) for k in range(8)]
        bulk_load(P, [(G.ev_w_in.t[l2, k * 128:(k + 1) * 128, q * 1792:(q + 1) * 1792], G.ev_w_in,
                       wi.t[:, k, q * 1792:(q + 1) * 1792], wi_k[k], 1792) for k in range(8) for q in range(2)], 1792)
        pw = P.sbuf("E_pw", [128, 4, 128], BF16)
        pst, pds = P.sbuf("E_pst", [128, 512], F32), P.dsem()
        P.dma("sp", pst[:, 0:512].rearrange("p (g d) -> p g d", g=4), G.ev_pool_w.t[l2].rearrange("g c d -> c g d"), pds,
              reads=[G.ev_pool_w], writes=[pst])
        P.op("dve", lambda e: e.tensor_copy(out=pw[:].rearrange("p g d -> p (g d)"), in_=pst[:, 0:512]), reads=[pst], writes=[pw])
        psc = P.sbuf("E_psc", [128, 4], F32)
        P.dma("sp", psc[:], G.ev_pool_scale.t[l2], P.dsem(), reads=[G.ev_pool_scale], writes=[psc])
        gain = P.sbuf("E_gain", [128, 512], F32)
        P.dma("sp", gain[:], G.ev_ret_norm_g.t[l2:l2 + 1, :].partition_broadcast(128), P.dsem(), reads=[G.ev_ret_norm_g], writes=[gain])
        ct = P.sbuf("E_ct", [128, C_END - C_ID], F32)
        P.dma("sp", ct[:], G.consts.t[:, C_ID:C_END], P.dsem(), reads=[G.consts], writes=[ct])
        o_ = lambda c: c - C_ID
        ident = Buf("E_ident", ct.t[:, o_(C_ID):o_(C_ID) + 128])
        dec = ct.t[:, o_(C_DEC):o_(C_DEC) + 512]
        kdec = ct.t[:, o_(C_KDEC):o_(C_KDEC) + 512]
        cdec = ct.t[:, o_(C_CDEC):o_(C_CDEC) + 512]
        qdec = ct.t[:, o_(C_QDEC):o_(C_QDEC) + 2048]
        pcnt = ct.t[:, o_(C_PCNT):o_(C_PCNT) + 64]
        ident_bf = P.sbuf("E_idbf", [128, 128], BF16)
        P.op("dve", lambda e: e.tensor_copy(out=ident_bf[:], in_=ct.t[:, 0:128]), reads=[ct], writes=[ident_bf])
        identF = Buf("E_identF", ct.t[:, 0:128])
        identF.lw = ct.lw
        nh = P.sbuf("E_nh", [128, 16], F32)
        P.op("pool", lambda e: e.memset(nh[:], -0.5), writes=[nh])

        cs = [P.sbuf(f"E_cs{i}", [128, TB], F32) for i in range(1)] * 2
        sn = [P.sbuf(f"E_sn{i}", [128, TB], F32) for i in range(1)] * 2
        csd = [P.dsem() for _ in range(1)] * 2
        hb = [P.sbuf(f"E_h{j}", [128, D], F32) for j in range(NJ)]
        hds = [P.dsem() for _ in range(NJ)]
        hT = P.sbuf("E_hT", [128, 8, TB], BF16)
        hT_k = [Buf(f"E_hT{k}", hT.t) for k in range(8)]
        qr = P.sbuf("E_qr", [128, NH, TB], BF16)
        kr = P.sbuf("E_kr", [128, NH, TB], BF16)
        qd = P.sbuf("E_qd", [128, NH, TB], BF16)
        t1 = [P.sbuf(f"E_t1{i}", [128, TB], F32) for i in range(2)]
        t2 = [P.sbuf(f"E_t2{i}", [128, TB], F32) for i in range(2)]
        vb = P.sbuf("E_vb", [128, NJ, 512], BF16)
        vdb = P.sbuf("E_vdb", [128, NJ, 512], BF16)
        gsg = P.sbuf("E_gsg", [128, NJ, 512], F32)
        xh = P.sbuf("E_xh", [128, 4, 16 + TB], F32)
        sa = P.sbuf("E_sa", [128, 16 + TB], F32)
        sb = P.sbuf("E_sb", [128, 16 + TB], F32)
        pooled = P.sbuf("E_pooled", [128, 4, TB], BF16)
        state = P.sbuf("E_state", [128, 512], F32)
        stmp = P.sbuf("E_stmp", [128, 512], F32)
        sbf = [P.sbuf(f"E_sbf{i}", [128, 512], BF16) for i in range(6)]
        PT = [P.sbuf(f"E_PT{i}", [128, 512], BF16) for i in range(2)]
        ktok = [P.sbuf(f"E_ktok{i}", [128, 512], BF16) for i in range(2)]
        osb = P.sbuf("E_osb", [128, NJ, 512], F32)
        sq = P.sbuf("E_sq", [128, NJ, 512], F32)
        s1 = P.sbuf("E_s1", [128, 16], F32)
        s2 = P.sbuf("E_s2", [128, 16], F32)
        mean = P.sbuf("E_mean", [128, 16], F32)
        msq = P.sbuf("E_msq", [128, 16], F32)
        var = P.sbuf("E_var", [128, 16], F32)
        rstd = P.sbuf("E_rstd", [128, 16], F32)
        nb = P.sbuf("E_nb", [128, 16], F32)
        cat = [P.sbuf(f"E_cat{i}", [128, 8, TB], BF16) for i in range(1)] * 2
        catd = [P.dsem() for _ in range(1)] * 2
        psA = [P.psum(f"E_psA{i}", [128, 512], F32) for i in range(4)]
        psS = [P.psum(f"E_psS{i}", [128, 512], F32) for i in range(2)]
        psTB = P.psum("E_psTB", [128, 1024], BF16)
        psKV = P.psum("E_psKV", [128, 512], F32)
        psO = psS

        def loads(b):
            t0 = b * TB
            s = t0 // S
            ts = t0 - s * S
            i = b % 2
            P.dma("sp", cs[i][:], G.cosT.t[s, :, ts:ts + TB], csd[i], reads=[G.cosT], writes=[cs[i]])
            P.dma("sp", sn[i][:], G.sinT.t[s, :, ts:ts + TB], csd[i], reads=[G.sinT], writes=[sn[i]])
            for j in range(NJ):
                P.dma("sp", hb[j][:], hin.t[t0 + j * 128:t0 + (j + 1) * 128, :], hds[j], reads=[hin_b[b]], writes=[hb[j]])

        st_ = {"na": 0, "gch": 0}

        def nextA():
            X = psA[st_["na"] % 4]
            st_["na"] += 1
            return X

        def partA(b):
            i = b % 2
            for c0_ in (0, 4):
                for c in range(c0_, c0_ + 4):
                    pt = nextA()
                    for j in range(NJ):
                        P.op("pe", lambda e, pt=pt, j=j, c=c: e.transpose(out=pt[:, j * 128:(j + 1) * 128], in_=hb[j][:, c * 128:(c + 1) * 128],
                                                                          identity=identF[:]), reads=[hb[j], identF], writes=[pt])
                    if c % 2 == 0:
                        P.op("dve", lambda e, pt=pt, c=c: e.tensor_copy(out=hT.t[:, c, :], in_=pt[:, :]), reads=[pt], writes=[hT_k[c]])
                    else:
                        P.op("act", lambda e, pt=pt, c=c: e.activation(out=hT.t[:, c, :], in_=pt[:, :], func=AF.Copy), reads=[pt], writes=[hT_k[c]])
                yield
            nr = 0
            for which, col, colsw in (("q", 0, 2560), ("k", 512, 3072)):
                for hd in range(NH):
                    X, Y = nextA(), nextA()
                    for k in range(8):
                        P.op("pe", lambda e, X=X, k=k, c0=col + hd * 128: e.matmul(X[:, :], lhsT=wi.t[:, k, c0:c0 + 128], rhs=hT.t[:, k, :],
                                                                                  start=(k == 0), stop=(k == 7)), reads=[wi_k[k], hT_k[k]], writes=[X])
                    for k in range(8):
                        P.op("pe", lambda e, Y=Y, k=k, c0=colsw + hd * 128: e.matmul(Y[:, :], lhsT=wi.t[:, k, c0:c0 + 128], rhs=hT.t[:, k, :],
                                                                                    start=(k == 0), stop=(k == 7)), reads=[wi_k[k], hT_k[k]], writes=[Y])
                    a, bb = t1[nr % 2], t2[nr % 2]
                    nr += 1
                    rope_head(P, X, Y, cs[i], sn[i], a, bb)
                    if which == "q":
                        P.op("act", lambda e, a=a, hd=hd: e.activation(out=qr[:, hd, :], in_=a[:], func=AF.Copy), reads=[a], writes=[qr])
                        P.op("pool", lambda e, a=a, hd=hd: e.tensor_tensor(out=qd[:, hd, :], in0=a[:], in1=qdec[:, hd * TB:(hd + 1) * TB], op=ALU.mult),
                             reads=[a, ct], writes=[qd])
                    else:
                        P.op("act", lambda e, a=a, hd=hd: e.activation(out=kr[:, hd, :], in_=a[:], func=AF.Copy), reads=[a], writes=[kr])
                    yield
            if b + 1 < NB:
                loads(b + 1)

        def partB(b):
            t0 = b * TB
            i = b % 2
            first = (t0 % S == 0)
            if first:
                P.op("pool", lambda e: e.memset(state[:], 0.0), writes=[state])
                sb0 = sbf[st_["gch"] % 6]
                P.op("pool", lambda e, sb0=sb0: e.memset(sb0[:], 0.0), writes=[sb0])
                P.op("pool", lambda e: e.memset(xh[:, :, 0:16], 0.0), writes=[xh])
            for j in range(NJ):
                X = nextA()
                for k in range(8):
                    P.op("pe", lambda e, X=X, k=k, j=j: e.matmul(X[:, :], lhsT=hT.t[:, k, j * 128:(j + 1) * 128], rhs=wi.t[:, k, 1024:1536],
                                                                 start=(k == 0), stop=(k == 7)), reads=[wi_k[k], hT_k[k]], writes=[X])
                P.op("act", lambda e, X=X, j=j: e.activation(out=vb[:, j, :], in_=X[:, :], func=AF.Copy), reads=[X], writes=[vb])
                P.op("dve", lambda e, X=X, j=j: e.tensor_tensor(out=vdb[:, j, :], in0=X[:, :], in1=kdec, op=ALU.mult), reads=[X, ct], writes=[vdb])
                X = nextA()
                for k in range(8):
                    P.op("pe", lambda e, X=X, k=k, j=j: e.matmul(X[:, :], lhsT=hT.t[:, k, j * 128:(j + 1) * 128], rhs=wi.t[:, k, 1536:2048],
                                                                 start=(k == 0), stop=(k == 7)), reads=[wi_k[k], hT_k[k]], writes=[X])
                P.op("act", lambda e, X=X, j=j: e.activation(out=gsg[:, j, :], in_=X[:, :], func=AF.Silu), reads=[X], writes=[gsg])
                P.op("pool", lambda e, j=j: e.tensor_tensor(out=gsg[:, j, :], in0=gsg[:, j, :], in1=gain[:], op=ALU.mult), reads=[gsg, gain], writes=[gsg])
            for j in range(NJ):
                Sb = psS[j % 2]
                for hd in range(NH):
                    P.op("pe", lambda e, Sb=Sb, hd=hd, j=j: e.matmul(Sb[:, hd * 128:(hd + 1) * 128], lhsT=kr[:, hd, j * 128:(j + 1) * 128],
                                                                     rhs=qr[:, hd, j * 128:(j + 1) * 128], start=True, stop=True),
                         reads=[kr, qr], writes=[Sb])
                pt = PT[j % 2]
                P.op("dve", lambda e, Sb=Sb, pt=pt: e.tensor_tensor(out=pt[:], in0=Sb[:, :], in1=dec, op=ALU.mult), reads=[Sb, ct], writes=[pt])
                for hd in range(NH):
                    P.op("pe", lambda e, hd=hd, j=j: e.transpose(out=psTB[:, (j % 2) * 512 + hd * 128:(j % 2) * 512 + (hd + 1) * 128],
                                                                 in_=kr[:, hd, j * 128:(j + 1) * 128], identity=ident_bf[:]),
                         reads=[kr, ident_bf], writes=[psTB])
                kt = ktok[j % 2]
                P.op("act", lambda e, kt=kt, j=j: e.activation(out=kt[:], in_=psTB[:, (j % 2) * 512:(j % 2 + 1) * 512], func=AF.Copy),
                     reads=[psTB], writes=[kt])
                gi, w = j, POOL_W[j]
                X = nextA()
                for k in range(8):
                    P.op("pe", lambda e, X=X, k=k, c0=2048 + gi * 128: e.matmul(X[:, :], lhsT=wi.t[:, k, c0:c0 + 128], rhs=hT.t[:, k, :],
                                                                               start=(k == 0), stop=(k == 7)), reads=[wi_k[k], hT_k[k]], writes=[X])
                P.op("act", lambda e, X=X, gi=gi: e.activation(out=xh[:, gi, 16:16 + TB], in_=X[:, :], func=AF.Copy), reads=[X], writes=[xh])
                for hd in range(NH):
                    P.op("pe", lambda e, kt=kt, hd=hd, j=j: e.matmul(psKV[:, hd * 128:(hd + 1) * 128], lhsT=kt[:, hd * 128:(hd + 1) * 128],
                                                                     rhs=vdb[:, j, hd * 128:(hd + 1) * 128], start=True, stop=True),
                         reads=[kt, vdb], writes=[psKV])
                O = psO[j % 2]
                sbc = sbf[st_["gch"] % 6]
                for hd in range(NH):
                    P.op("pe", lambda e, O=O, pt=pt, hd=hd, j=j: e.matmul(O[:, hd * 128:(hd + 1) * 128], lhsT=pt[:, hd * 128:(hd + 1) * 128],
                                                                          rhs=vb[:, j, hd * 128:(hd + 1) * 128], start=True, stop=False),
                         reads=[pt, vb], writes=[O])
                    P.op("pe", lambda e, O=O, sbc=sbc, hd=hd, j=j: e.matmul(O[:, hd * 128:(hd + 1) * 128], lhsT=qd[:, hd, j * 128:(j + 1) * 128],
                                                                            rhs=sbc[:, hd * 128:(hd + 1) * 128], start=False, stop=True),
                         reads=[qd, sbc], writes=[O])
                P.op("act", lambda e, O=O, j=j: e.activation(out=osb[:, j, :], in_=O[:, :], func=AF.Copy), reads=[O], writes=[osb])
                sbn = sbf[(st_["gch"] + 1) % 6]
                P.op("pool", lambda e: e.tensor_tensor(out=stmp[:], in0=state[:], in1=cdec, op=ALU.mult), reads=[state, ct], writes=[stmp])
                P.op("dve", lambda e: e.tensor_tensor(out=state[:], in0=stmp[:], in1=psKV[:, :], op=ALU.add), reads=[stmp, psKV], writes=[state])
                P.op("act", lambda e, sbn=sbn: e.activation(out=sbn[:], in_=state[:], func=AF.Copy), reads=[state], writes=[sbn])
                st_["gch"] += 1
                cur, curBuf = xh.t[:, gi, :], xh
                sh = 1
                tgl = [sa, sb]
                ti = 0
                while sh < w:
                    dst = tgl[ti % 2]
                    lo = 2 * sh - 1
                    P.op("pool", lambda e, dst=dst, cur=cur, sh=sh, lo=lo: e.tensor_tensor(out=dst[:, lo:16 + TB], in0=cur[:, lo:16 + TB],
                                                                                         in1=cur[:, lo - sh:16 + TB - sh], op=ALU.add),
                         reads=[curBuf], writes=[dst])
                    cur, curBuf = dst.t, dst
                    sh *= 2
                    ti += 1
                P.op("dve", lambda e, cur=cur, gi=gi, w=w: e.scalar_tensor_tensor(out=pooled[:, gi, :], in0=cur[:, 16:16 + TB], scalar=float(1.0 / w),
                                                                                 in1=xh[:, gi, 16:16 + TB], op0=ALU.mult, op1=ALU.subtract),
                     reads=[curBuf, xh], writes=[pooled])
                if first:
                    tmp = tgl[ti % 2]
                    P.op("pool", lambda e, cur=cur, gi=gi, tmp=tmp: e.tensor_tensor(out=tmp[:, 0:16], in0=cur[:, 16:32], in1=pcnt[:, gi * 16:(gi + 1) * 16], op=ALU.mult),
                         reads=[curBuf, ct], writes=[tmp])
                    P.op("pool", lambda e, gi=gi, tmp=tmp: e.tensor_tensor(out=pooled[:, gi, 0:16], in0=tmp[:, 0:16], in1=xh[:, gi, 16:32], op=ALU.subtract),
                         reads=[tmp, xh, pooled], writes=[pooled])
                X = nextA()
                P.op("pe", lambda e, X=X, gi=gi: e.matmul(X[:, :], lhsT=pw[:, gi, :], rhs=pooled[:, gi, :], start=True, stop=True),
                     reads=[pw, pooled], writes=[X])
                P.op("act", lambda e, X=X, gi=gi, i=i: e.activation(out=cat[i][:, 4 + gi, :], in_=X[:, :], func=AF.Identity, scale=psc[:, gi:gi + 1]),
                     reads=[X, psc], writes=[cat[i]])
            P.op("act", lambda e: e.activation(out=xh[:, :, 0:16], in_=xh[:, :, TB:TB + 16], func=AF.Copy), reads=[xh], writes=[xh])

        def partT(b):
            t0 = b * TB
            i = b % 2
            o3 = osb.t[:].rearrange("p j (h e) -> p (j h) e", h=NH)
            q3 = sq.t[:].rearrange("p j (h e) -> p (j h) e", h=NH)
            P.op("dve", lambda e: e.tensor_reduce(out=s1[:], in_=o3, axis=AX.X, op=ALU.add), reads=[osb], writes=[s1])
            P.op("pool", lambda e: e.tensor_tensor(out=sq[:], in0=osb[:], in1=osb[:], op=ALU.mult), reads=[osb], writes=[sq])
            yield
            P.op("dve", lambda e: e.tensor_reduce(out=s2[:], in_=q3, axis=AX.X, op=ALU.add), reads=[sq], writes=[s2])
            P.op("dve", lambda e: e.tensor_scalar(out=mean[:], in0=s1[:], scalar1=1.0 / DH, scalar2=None, op0=ALU.mult), reads=[s1], writes=[mean])
            P.op("dve", lambda e: e.tensor_tensor(out=msq[:], in0=mean[:], in1=mean[:], op=ALU.mult), reads=[mean], writes=[msq])
            P.op("dve", lambda e: e.scalar_tensor_tensor(out=var[:], in0=s2[:], scalar=1.0 / DH, in1=msq[:], op0=ALU.mult, op1=ALU.subtract),
                 reads=[s2, msq], writes=[var])
            P.op("dve", lambda e: e.tensor_scalar_add(out=var[:], in0=var[:], scalar1=EPS), reads=[var], writes=[var])
            P.op("dve", lambda e: e.tensor_scalar(out=msq[:], in0=mean[:], scalar1=-1.0, scalar2=None, op0=ALU.mult), reads=[mean], writes=[msq])
            P.op("pool", lambda e: e.tensor_tensor(out=rstd[:], in0=var[:], in1=nh[:], op=ALU.pow), reads=[var, nh], writes=[rstd])
            P.op("pool", lambda e: e.tensor_tensor(out=nb[:], in0=msq[:], in1=rstd[:], op=ALU.mult), reads=[msq, rstd], writes=[nb])
            yield
            P.op("pool", lambda e: e.tensor_tensor(out=q3, in0=o3, in1=rstd[:].unsqueeze(2).to_broadcast([128, 16, DH]), op=ALU.mult),
                 reads=[osb, rstd], writes=[sq])
            yield
            P.op("pool", lambda e: e.tensor_tensor(out=q3, in0=q3, in1=nb[:].unsqueeze(2).to_broadcast([128, 16, DH]), op=ALU.add),
                 reads=[sq, nb], writes=[sq])
            yield
            P.op("dve", lambda e: e.tensor_tensor(out=sq[:], in0=sq[:], in1=gsg[:], op=ALU.mult), reads=[sq, gsg], writes=[sq])
            yield
            for j in range(NJ):
                X = psS[j % 2]
                for hd in range(NH):
                    P.op("pe", lambda e, X=X, hd=hd, j=j: e.transpose(out=X[:, hd * 128:(hd + 1) * 128], in_=sq[:, j, hd * 128:(hd + 1) * 128],
                                                                      identity=identF[:]), reads=[sq, identF], writes=[X])
                outap = cat[i][:, 0:4, j * 128:(j + 1) * 128]
                inap = X[:, :].rearrange("p (h t) -> p h t", h=NH)
                if j % 2 == 0:
                    P.op("dve", lambda e, outap=outap, inap=inap: e.tensor_copy(out=outap, in_=inap), reads=[X], writes=[cat[i]])
                else:
                    P.op("act", lambda e, outap=outap, inap=inap: e.activation(out=outap, in_=inap, func=AF.Copy), reads=[X], writes=[cat[i]])
                yield
            P.dma("sp", G.catT.t[:, :, t0:t0 + TB].rearrange("c p t -> p c t"), cat[i][:], catd[i], reads=[cat[i]], writes=[G.catT_b[b]])

        def interleave(*gens):
            gens = [g for g in gens if g is not None]
            while gens:
                for g in list(gens):
                    try:
                        next(g)
                    except StopIteration:
                        gens.remove(g)

        loads(0)
        interleave(partA(0))
        for b in range(NB):
            partB(b)
            interleave(partT(b), partA(b + 1) if b + 1 < NB else None)
        P.end_phase()
    P.scope = P.es


def build_program(NS, S, layers, debug=False):
    nc = bass.Bass("TRN2", target_bir_lowering=False)
    P = Prog(nc)
    G = Ctx()
    G.NS, G.S = NS, S
    G.NT = NS * S
    G.NB = G.NT // TB
    NT, NB = G.NT, G.NB
    ext = lambda name, shape, dt=F32: P.dram(name, shape, dt, kind="ExternalInput")
    G.x = ext("x", [NT, D])
    G.pos = ext("pos", [NS, S], I32)
    G.consts = ext("consts", [128, C_END])
    G.ev_w_in = ext("ev_w_in", [2, D, 3584])
    G.ev_ret_norm_g = ext("ev_ret_norm_g", [2, 512])
    G.ev_pool_w = ext("ev_pool_w", [2, 4, 128, 128])
    G.ev_pool_scale = ext("ev_pool_scale", [2, 128, 4])
    G.ev_w_out = ext("ev_w_out", [2, D, D])
    G.od_w_in = ext("od_w_in", [2, D, 3584])
    G.od_small = ext("od_small", [2, 128, 36])
    G.od_gate_a_w = ext("od_gate_a_w", [2, 8, 64, 64])
    G.od_gate_x_w = ext("od_gate_x_w", [2, 8, 64, 64])
    G.od_w_out = ext("od_w_out", [2, D, D])
    G.ffn_w_in = ext("ffn_w_in", [DEPTH, D, 2 * DFF])
    G.ffn_w_out = ext("ffn_w_out", [DEPTH, DFF, D])
    G.ln_g = ext("ln_g", [DEPTH, 2, D])
    G.ln_b = ext("ln_b", [DEPTH, 2, D])
    G.out = P.dram("out", [NT, D], F32, kind="ExternalOutput")
    dk = "ExternalOutput" if debug else "Internal"
    G.hA = P.dram("hA", [NT, D], F32, kind=dk)
    G.hB = P.dram("hB", [NT, D], F32, kind=dk)
    G.catT = P.dram("catT", [8, 128, NT], BF16, kind=dk)
    G.cosT = P.dram("cosT", [NS, 128, S], F32, kind=dk)
    G.sinT = P.dram("sinT", [NS, 128, S], F32, kind=dk)
    G.qT = P.dram("qT", [NH, 128, NT], BF16, kind=dk)
    G.kT = P.dram("kT", [NH, 128, NT], BF16, kind=dk)
    G.vd = P.dram("vd", [NT, 512], BF16, kind=dk)
    G.x_b = blk_bufs("x_b", G.x.t, NB)
    G.hA_b = blk_bufs("hA_b", G.hA.t, NB)
    G.hB_b = blk_bufs("hB_b", G.hB.t, NB)
    G.out_b = blk_bufs("out_b", G.out.t, NB)
    G.catT_b = blk_bufs("catT_b", G.catT.t, NB)
    G.qkv_b = blk_bufs("qkv_b", G.qT.t, NB)

    phase_R(P, G)
    fin = []
    hin, hin_b = G.x, G.x_b
    for li, layer in enumerate(layers):
        l2 = layer // 2
        last = (li == len(layers) - 1)
        if layer % 2 == 0:
            phase_E(P, G, l2, hin, hin_b)
            w_out = G.ev_w_out
        else:
            phase_O1(P, G, l2, hin, hin_b)
            phase_O2(P, G)
            w_out = G.od_w_out
        phase_P(P, G, w_out.t[l2], w_out, G.ln_g.t[layer, 0:1, :], G.ln_b.t[layer, 0:1, :], G.ln_g, hin, hin_b, G.hB, G.hB_b)
        hout, hout_b = (G.out, G.out_b) if last else (G.hA, G.hA_b)
        fin = phase_F(P, G, G.ffn_w_in.t[layer], G.ffn_w_in, G.ffn_w_out.t[layer], G.ffn_w_out,
                      G.ln_g.t[layer, 1:2, :], G.ln_b.t[layer, 1:2, :], G.ln_g, G.hB, G.hB_b, hout, hout_b)
        hin, hin_b = G.hA, G.hA_b
    P.finish(fin)
    return nc, P


def host_prep(inp):
    f = lambda a: np.ascontiguousarray(np.asarray(a, dtype=np.float32))
    swap = np.concatenate([np.arange(h * 128 + 64, h * 128 + 128).tolist() + np.arange(h * 128, h * 128 + 64).tolist() for h in range(NH)]).astype(np.int64)
    ev = f(inp["ev_w_in"])
    ev_ext = np.concatenate([ev, ev[:, :, 0:512][:, :, swap], ev[:, :, 512:1024][:, :, swap]], axis=2)
    od = f(inp["od_w_in"])
    od_ext = np.concatenate([od, od[:, :, 1024:1536][:, :, swap], od[:, :, 1536:2048][:, :, swap]], axis=2)
    psc = f(inp["ev_pool_scale"]).reshape(2, 4, 128).transpose(0, 2, 1)
    cw = f(inp["od_conv_w"])
    cols = [cw[:, k, :] for k in range(4)] + [f(inp["od_conv_b"]), f(inp["od_gate_a_b"]), f(inp["od_gate_x_b"]), f(inp["od_lru_lambda"]),
                                              f(inp["od_conv_b"])]
    sm = np.stack(cols, axis=-1)
    sm = sm.reshape(2, 4, 128, 9).transpose(0, 2, 1, 3).reshape(2, 128, 36)
    shared = {
        "consts": make_consts(),
        "ev_w_in": np.ascontiguousarray(ev_ext), "ev_ret_norm_g": f(inp["ev_ret_norm_g"]), "ev_pool_w": f(inp["ev_pool_w"]),
        "ev_pool_scale": np.ascontiguousarray(psc), "ev_w_out": f(inp["ev_w_out"]),
        "od_w_in": np.ascontiguousarray(od_ext), "od_small": np.ascontiguousarray(sm),
        "od_gate_a_w": f(inp["od_gate_a_w"]), "od_gate_x_w": f(inp["od_gate_x_w"]), "od_w_out": f(inp["od_w_out"]),
        "ffn_w_in": f(inp["ffn_w_in"]), "ffn_w_out": f(inp["ffn_w_out"]), "ln_g": f(inp["ln_g"]), "ln_b": f(inp["ln_b"]),
    }
    return shared


def kernel(**inp):
    x = np.asarray(inp["x"], dtype=np.float32)
    pos = np.asarray(inp["positions"], dtype=np.int32)
    B, S, _ = x.shape
    ncores = 8
    NS = B // ncores
    shared = host_prep(inp)
    nc, P = build_program(NS, S, list(range(DEPTH)))
    in_maps = []
    for c in range(ncores):
        m = dict(shared)
        m["x"] = np.ascontiguousarray(x[c * NS:(c + 1) * NS].reshape(NS * S, D))
        m["pos"] = np.ascontiguousarray(pos[c * NS:(c + 1) * NS])
        in_maps.append(m)
    res = run_bass_kernel_spmd(nc, in_maps, core_ids=list(range(ncores)))
    out = np.concatenate([np.asarray(r["out"], dtype=np.float32).reshape(NS, S, D) for r in res.results], axis=0)
    return out


def phase_O1(P, G, l2, hin, hin_b):
    S, NS, NB = G.S, G.NS, G.NB
    WC = 3584
    with contextlib.ExitStack() as sc:
        P.scope = sc
        wi = P.sbuf("O_wi", [128, 8, WC], BF16)
        wi_k = [Buf(f"O_wi{k}", wi.t) for k in range(8)]
        bulk_load(P, [(G.od_w_in.t[l2, k * 128:(k + 1) * 128, q * 1792:(q + 1) * 1792], G.od_w_in,
                       wi.t[:, k, q * 1792:(q + 1) * 1792], wi_k[k], 1792) for k in range(8) for q in range(2)], 1792)
        gst = P.sbuf("O_gst", [128, 4, 128], F32)
        gsd = P.dsem()
        ga = P.sbuf("O_ga", [128, 4, 128], BF16)
        gx = P.sbuf("O_gx", [128, 4, 128], BF16)
        P.op("pool", lambda e: e.memset(gst[:], 0.0), writes=[gst])
        for wsrc, dst in ((G.od_gate_a_w, ga), (G.od_gate_x_w, gx)):
            v4 = wsrc.t[l2].rearrange("(c two) i d -> two i c d", two=2)
            P.dma("sp", gst[0:64, :, 0:64], v4[0], gsd, reads=[wsrc], writes=[gst])
            P.dma("sp", gst[64:128, :, 64:128], v4[1], gsd, reads=[wsrc], writes=[gst])
            P.op("dve", lambda e, dst=dst: e.tensor_copy(out=dst[:], in_=gst[:]), reads=[gst], writes=[dst])
        spar = P.sbuf("O_spar", [128, 36], F32)
        P.dma("sp", spar[:], G.od_small.t[l2], P.dsem(), reads=[G.od_small], writes=[spar])
        sp3 = spar.t[:].rearrange("p (c k) -> p c k", k=9)
        sm_ = {n: P.sbuf(f"O_{n}", [128, 4], F32) for n in ("ex", "den", "z", "z2", "p1", "zp", "c8", "c16")}
        ex, den, z, z2, p1, zp, c8, c16 = (sm_[n] for n in ("ex", "den", "z", "z2", "p1", "zp", "c8", "c16"))
        P.op("act", lambda e: e.activation(out=ex[:], in_=sp3[:, :, 7], func=AF.Exp, scale=-1.0), reads=[spar], writes=[ex])
        P.op("dve", lambda e: e.tensor_scalar_add(out=den[:], in0=ex[:], scalar1=2.0), reads=[ex], writes=[den])
        P.op("dve", lambda e: e.reciprocal(out=den[:], in_=den[:]), reads=[den], writes=[den])
        P.op("dve", lambda e: e.tensor_tensor(out=z[:], in0=ex[:], in1=den[:], op=ALU.mult), reads=[ex, den], writes=[z])
        P.op("dve", lambda e: e.tensor_tensor(out=z2[:], in0=z[:], in1=z[:], op=ALU.mult), reads=[z], writes=[z2])
        P.op("dve", lambda e: e.tensor_scalar(out=p1[:], in0=z2[:], scalar1=1.0 / 7.0, scalar2=1.0 / 5.0, op0=ALU.mult, op1=ALU.add), reads=[z2], writes=[p1])
        P.op("dve", lambda e: e.tensor_tensor(out=p1[:], in0=p1[:], in1=z2[:], op=ALU.mult), reads=[p1, z2], writes=[p1])
        P.op("dve", lambda e: e.tensor_scalar_add(out=p1[:], in0=p1[:], scalar1=1.0 / 3.0), reads=[p1], writes=[p1])
        P.op("dve", lambda e: e.tensor_tensor(out=p1[:], in0=p1[:], in1=z2[:], op=ALU.mult), reads=[p1, z2], writes=[p1])
        P.op("dve", lambda e: e.scalar_tensor_tensor(out=zp[:], in0=p1[:], scalar=1.0, in1=z[:], op0=ALU.add, op1=ALU.mult), reads=[p1, z], writes=[zp])
        P.op("dve", lambda e: e.tensor_scalar(out=c8[:], in0=zp[:], scalar1=-16.0, scalar2=None, op0=ALU.mult), reads=[zp], writes=[c8])
        P.op("dve", lambda e: e.tensor_scalar(out=c16[:], in0=zp[:], scalar1=-32.0, scalar2=None, op0=ALU.mult), reads=[zp], writes=[c16])
        ident = P.sbuf("O_id", [128, 128], F32)
        P.dma("sp", ident[:], G.consts.t[:, C_ID:C_ID + 128], P.dsem(), reads=[G.consts], writes=[ident])
        half = P.sbuf("O_half", [128, TB], F32)
        P.op("pool", lambda e: e.memset(half[:], 0.5), writes=[half])

        cs = P.sbuf("O_cs", [128, TB], F32)
        sn = P.sbuf("O_sn", [128, TB], F32)
        csd = P.dsem()
        hb = [P.sbuf(f"O_h{j}", [128, D], F32) for j in range(NJ)]
        hds = [P.dsem() for _ in range(NJ)]
        hT = P.sbuf("O_hT", [128, 8, TB], BF16)
        hT_k = [Buf(f"O_hT{k}", hT.t) for k in range(8)]
        gl = P.sbuf("O_gl", [128, 4, TB], F32)
        uh = P.sbuf("O_uh", [128, 4, 3 + TB], F32)
        uc = P.sbuf("O_uc", [128, 4, TB], F32)
        ucb = P.sbuf("O_ucb", [128, 4, TB], BF16)
        rr = P.sbuf("O_rr", [128, 4, TB], F32)
        ii = P.sbuf("O_ii", [128, 4, TB], F32)
        aa = [P.sbuf(f"O_aa{i}", [128, TB], F32) for i in range(2)]
        w1 = [P.sbuf(f"O_w1{i}", [128, TB], F32) for i in range(2)]
        hs = [P.sbuf(f"O_hs{i}", [128, TB], F32) for i in range(2)]
        carry = P.sbuf("O_carry", [128, 4], F32)
        catl = P.sbuf("O_catl", [128, 4, TB], BF16)
        catd = P.dsem()
        qr = P.sbuf("O_qr", [128, NH, TB], BF16)
        kr = P.sbuf("O_kr", [128, NH, TB], BF16)
        qd_, kd_ = P.dsem(), P.dsem()
        t1 = P.sbuf("O_t1", [128, TB], F32)
        t2 = P.sbuf("O_t2", [128, TB], F32)
        vbt = P.sbuf("O_vbt", [128, NJ, 512], BF16)
        vds = P.dsem()
        psA = [P.psum(f"O_psA{i}", [128, 512], F32) for i in range(4)]
        psR = [P.psum(f"O_psR{i}", [128, 512], F32) for i in range(2)]
        psI = [P.psum(f"O_psI{i}", [128, 512], F32) for i in range(2)]

        def loads(b):
            t0 = b * TB
            s = t0 // S
            ts = t0 - s * S
            P.dma("sp", cs[:], G.cosT.t[s, :, ts:ts + TB], csd, reads=[G.cosT], writes=[cs])
            P.dma("sp", sn[:], G.sinT.t[s, :, ts:ts + TB], csd, reads=[G.sinT], writes=[sn])
            for j in range(NJ):
                P.dma("sp", hb[j][:], hin.t[t0 + j * 128:t0 + (j + 1) * 128, :], hds[j], reads=[hin_b[b]], writes=[hb[j]])

        def proj(X, c0, tokmajor=False, j=0):
            for k in range(8):
                if tokmajor:
                    P.op("pe", lambda e, k=k: e.matmul(X[:, :], lhsT=hT.t[:, k, j * 128:(j + 1) * 128], rhs=wi.t[:, k, c0:c0 + 512],
                                                       start=(k == 0), stop=(k == 7)), reads=[wi_k[k], hT_k[k]], writes=[X])
                else:
                    P.op("pe", lambda e, k=k: e.matmul(X[:, :], lhsT=wi.t[:, k, c0:c0 + 128], rhs=hT.t[:, k, :],
                                                       start=(k == 0), stop=(k == 7)), reads=[wi_k[k], hT_k[k]], writes=[X])

        st_ = {"na": 0}

        def nextA():
            X = psA[st_["na"] % 4]
            st_["na"] += 1
            return X

        def partA(b):
            t0 = b * TB
            for c0_ in (0, 4):
                for c in range(c0_, c0_ + 4):
                    pt = nextA()
                    for j in range(NJ):
                        P.op("pe", lambda e, pt=pt, j=j, c=c: e.transpose(out=pt[:, j * 128:(j + 1) * 128], in_=hb[j][:, c * 128:(c + 1) * 128],
                                                                          identity=ident[:]), reads=[hb[j], ident], writes=[pt])
                    if c % 2 == 0:
                        P.op("dve", lambda e, pt=pt, c=c: e.tensor_copy(out=hT.t[:, c, :], in_=pt[:, :]), reads=[pt], writes=[hT_k[c]])
                    else:
                        P.op("act", lambda e, pt=pt, c=c: e.activation(out=hT.t[:, c, :], in_=pt[:, :], func=AF.Copy), reads=[pt], writes=[hT_k[c]])
                yield
            for which, col, colsw, dstb in (("q", 1024, 2560, qr), ("k", 1536, 3072, kr)):
                for hd in range(NH):
                    X, Y = nextA(), nextA()
                    proj(X, col + hd * 128)
                    proj(Y, colsw + hd * 128)
                    rope_head(P, X, Y, cs, sn, t1, t2)
                    P.op("act", lambda e, hd=hd, dstb=dstb: e.activation(out=dstb[:, hd, :], in_=t1[:], func=AF.Copy), reads=[t1], writes=[dstb])
                    yield
            P.dma("sp", G.qT.t[:, :, t0:t0 + TB].rearrange("h p t -> p h t"), qr[:], qd_, reads=[qr], writes=[G.qkv_b[b]])
            P.dma("sp", G.kT.t[:, :, t0:t0 + TB].rearrange("h p t -> p h t"), kr[:], kd_, reads=[kr], writes=[G.qkv_b[b]])
            if b + 1 < NB:
                loads(b + 1)

        def partB(b):
            t0 = b * TB
            first = (t0 % S == 0)
            if first:
                P.op("pool", lambda e: e.memset(uh[:, :, 0:3], 0.0), writes=[uh])
                P.op("pool", lambda e: e.memset(carry[:], 0.0), writes=[carry])
            for c in range(4):
                X = nextA()
                proj(X, 512 + c * 128)
                P.op("act", lambda e, X=X, c=c: e.activation(out=uh[:, c, 3:3 + TB], in_=X[:, :], func=AF.Copy), reads=[X], writes=[uh])
            for c in range(4):
                X = nextA()
                proj(X, c * 128)
                P.op("act", lambda e, X=X, c=c: e.activation(out=gl[:, c, :], in_=X[:, :], func=AF.Gelu_apprx_tanh), reads=[X], writes=[gl])
                P.op("dve", lambda e, c=c: e.tensor_scalar(out=uc[:, c, :], in0=uh[:, c, 0:TB], scalar1=sp3[:, c, 0:1], scalar2=sp3[:, c, 4:5],
                                                          op0=ALU.mult, op1=ALU.add), reads=[uh, spar], writes=[uc])
                for k in range(1, 4):
                    P.op("dve", lambda e, c=c, k=k: e.scalar_tensor_tensor(out=uc[:, c, :], in0=uh[:, c, k:k + TB], scalar=sp3[:, c, k:k + 1],
                                                                          in1=uc[:, c, :], op0=ALU.mult, op1=ALU.add), reads=[uh, spar, uc], writes=[uc])
            P.op("pool", lambda e: e.tensor_copy(out=ucb[:], in_=uc[:]), reads=[uc], writes=[ucb])
            P.op("act", lambda e: e.activation(out=uh[:, :, 0:3], in_=uh[:, :, TB:TB + 3], func=AF.Copy), reads=[uh], writes=[uh])
            for j in range(NJ):
                X = nextA()
                proj(X, 2048, tokmajor=True, j=j)
                P.op("act", lambda e, X=X, j=j: e.activation(out=vbt[:, j, :], in_=X[:, :], func=AF.Copy), reads=[X], writes=[vbt])
            P.dma("sp", G.vd.t[t0:t0 + TB, :].rearrange("(j p) f -> p j f", p=128), vbt[:], vds, reads=[vbt], writes=[G.qkv_b[b]])
            for c in range(4):
                R_, I_ = psR[c % 2], psI[c % 2]
                P.op("pe", lambda e, R_=R_, c=c: e.matmul(R_[:, :], lhsT=ga[:, c, :], rhs=ucb[:, c, :], start=True, stop=True), reads=[ga, ucb], writes=[R_])
                P.op("pe", lambda e, I_=I_, c=c: e.matmul(I_[:, :], lhsT=gx[:, c, :], rhs=ucb[:, c, :], start=True, stop=True), reads=[gx, ucb], writes=[I_])
                P.op("act", lambda e, R_=R_, c=c: e.activation(out=rr[:, c, :], in_=R_[:, :], func=AF.Sigmoid, bias=sp3[:, c, 5:6]), reads=[R_, spar], writes=[rr])
                P.op("act", lambda e, I_=I_, c=c: e.activation(out=ii[:, c, :], in_=I_[:, :], func=AF.Sigmoid, bias=sp3[:, c, 6:7]), reads=[I_, spar], writes=[ii])

        def partT(b):
            t0 = b * TB
            for c in range(4):
                a_, w_, h_ = aa[c % 2], w1[c % 2], hs[c % 2]
                P.op("act", lambda e, a_=a_, c=c: e.activation(out=a_[:], in_=rr[:, c, :], func=AF.Exp, scale=c8[:, c:c + 1]), reads=[rr, c8], writes=[a_])
                P.op("act", lambda e, w_=w_, c=c: e.activation(out=w_[:], in_=rr[:, c, :], func=AF.Exp, scale=c16[:, c:c + 1]), reads=[rr, c16], writes=[w_])
                P.op("act", lambda e, w_=w_: e.activation(out=w_[:], in_=w_[:], func=AF.Sqrt, scale=-1.0, bias=1.0), reads=[w_], writes=[w_])
                P.op("pool", lambda e, c=c: e.tensor_tensor(out=ii[:, c, :], in0=ii[:, c, :], in1=uc[:, c, :], op=ALU.mult), reads=[ii, uc], writes=[ii])
                P.op("dve", lambda e, w_=w_, c=c: e.tensor_tensor(out=w_[:], in0=w_[:], in1=ii[:, c, :], op=ALU.mult), reads=[w_, ii], writes=[w_])
                P.op("dve", lambda e, a_=a_, w_=w_, h_=h_, c=c: e.tensor_tensor_scan(out=h_[:], data0=a_[:], data1=w_[:], initial=carry[:, c:c + 1],
                                                                                   op0=ALU.mult, op1=ALU.add), reads=[a_, w_, carry], writes=[h_])
                P.op("act", lambda e, h_=h_, c=c: e.activation(out=carry[:, c:c + 1], in_=h_[:, TB - 1:TB], func=AF.Copy), reads=[h_], writes=[carry])
                P.op("pool", lambda e, h_=h_, c=c: e.tensor_tensor(out=catl[:, c, :], in0=h_[:], in1=gl[:, c, :], op=ALU.mult), reads=[h_, gl], writes=[catl])
                yield
                yield
            P.dma("sp", G.catT.t[0:4, :, t0:t0 + TB].rearrange("c p t -> p c t"), catl[:], catd, reads=[catl], writes=[G.catT_b[b]])

        def interleave(*gens):
            gens = [g for g in gens if g is not None]
            while gens:
                for g in list(gens):
                    try:
                        next(g)
                    except StopIteration:
                        gens.remove(g)

        loads(0)
        interleave(partA(0))
        for b in range(NB):
            partB(b)
            interleave(partT(b), partA(b + 1) if b + 1 < NB else None)
        P.end_phase()
    P.scope = P.es


def phase_O2(P, G):
    S, NS, NB = G.S, G.NS, G.NB
    BPS = S // TB
    SKEW = 3
    NPT = SKEW + 3
    scale = float(DH ** -0.5)
    with contextlib.ExitStack() as sc:
        P.scope = sc
        mk32 = P.sbuf("A_mk32", [128, 256], F32)
        P.dma("sp", mk32[:], G.consts.t[:, C_MASK:C_MASK + 256], P.dsem(), reads=[G.consts], writes=[mk32])
        mask = P.sbuf("A_mask", [128, 256], BF16)
        P.op("dve", lambda e: e.tensor_copy(out=mask[:], in_=mk32[:]), reads=[mk32], writes=[mask])
        ones = P.sbuf("A_ones", [128, 128], BF16)
        P.op("pool", lambda e: e.memset(ones[:], 1.0), writes=[ones])
        qT = [P.sbuf(f"A_qT{i}", [128, S], BF16) for i in range(2)]
        kT = [P.sbuf(f"A_kT{i}", [128, S], BF16) for i in range(2)]
        qkd = [P.dsem() for _ in range(2)]
        vs = [P.sbuf(f"A_vs{i}", [128, S // 128, 128], BF16) for i in range(2)]
        vsd = [P.dsem() for _ in range(2)]
        num = P.sbuf("A_num", [128, S], F32)
        den = P.sbuf("A_den", [128, S], F32)
        rden = P.sbuf("A_rden", [128, S], F32)
        y = P.sbuf("A_y", [128, S], BF16)
        yd = P.dsem()
        ebf = [P.sbuf(f"A_eb{i}", [128, 256], BF16) for i in range(3)]
        PT = [P.sbuf(f"A_PT{i}", [128, 256], BF16) for i in range(NPT)]
        psS = [P.psum(f"A_psS{i}", [128, 512], F32) for i in range(3)]
        psN = [P.psum(f"A_psN{i}", [128, 512], F32) for i in range(2)]
        psD = [P.psum(f"A_psD{i}", [128, 512], F32) for i in range(2)]
        vcount = 0
        gcount = 0
        iters = [(s, hd) for s in range(NS) for hd in range(NH)]

        def qk_load(n):
            s, hd = iters[n]
            sblk = [G.qkv_b[s * BPS + bb] for bb in range(BPS)]
            i = n % 2
            P.dma("sp", qT[i][:], G.qT.t[hd, :, s * S:(s + 1) * S], qkd[i], reads=sblk, writes=[qT[i]])
            P.dma("sp", kT[i][:], G.kT.t[hd, :, s * S:(s + 1) * S], qkd[i], reads=sblk, writes=[kT[i]])

        qk_load(0)
        for n, (s, hd) in enumerate(iters):
            if True:
                sblk = [G.qkv_b[s * BPS + bb] for bb in range(BPS)]
                i = n % 2
                q_, k_ = qT[i], kT[i]
                if n + 1 < len(iters):
                    qk_load(n + 1)
                per = []
                for pi, (W, dil) in enumerate(DIL_PATTERNS):
                    nb = S // dil // 128
                    v_ = vs[vcount % 2]
                    vdm = vsd[vcount % 2]
                    vcount += 1
                    lds, bks = [], []
                    for r in range(dil):
                        src = G.vd.t[s * S + r:(s + 1) * S:dil, hd * 128:(hd + 1) * 128].rearrange("(kb p) e -> p kb e", p=128)
                        lds.append(("load", v_, vdm, r, nb, src))
                        for kb in range(nb):
                            bks.append(("blk", pi, dil, nb, r, kb, v_))
                    per.append((lds, bks))
                tasks = list(per[0][0])
                for pi in range(len(per)):
                    bks = per[pi][1]
                    tasks += bks[:8]
                    if pi + 1 < len(per):
                        tasks += per[pi + 1][0]
                    tasks += bks[8:]
                blks = [t for t in tasks if t[0] == "blk"]
                pending = []
                ptmap = {}
                bi = 0

                def stage2(idx):
                    nonlocal gcount
                    _, pi, dil, nb, r, qb, v_ = blks[idx]
                    gi = qb % 4
                    N, Dn = psN[gcount % 2], psD[gcount % 2]
                    reg = slice(gi * 128, (gi + 1) * 128)
                    ptc = ptmap[idx]
                    if qb >= 1:
                        ptp = ptmap[idx - 1]
                        for dst, lw, lwB in ((N, v_, None), (Dn, None, ones)):
                            l0 = v_[:, r * nb + qb - 1, :] if lwB is None else ones[:]
                            l1 = v_[:, r * nb + qb, :] if lwB is None else ones[:]
                            rb = v_ if lwB is None else ones
                            P.op("pe", lambda e, dst=dst, l0=l0, ptp=ptp: e.matmul(dst[:, reg], lhsT=l0, rhs=ptp[:, 128:256], start=True, stop=False),
                                 reads=[rb, ptp], writes=[dst])
                            P.op("pe", lambda e, dst=dst, l1=l1, ptc=ptc: e.matmul(dst[:, reg], lhsT=l1, rhs=ptc[:, 0:128], start=False, stop=True),
                                 reads=[rb, ptc], writes=[dst])
                    else:
                        P.op("pe", lambda e: e.matmul(N[:, reg], lhsT=v_[:, r * nb + qb, :], rhs=ptc[:, 0:128], start=True, stop=True),
                             reads=[v_, ptc], writes=[N])
                        P.op("pe", lambda e: e.matmul(Dn[:, reg], lhsT=ones[:], rhs=ptc[:, 0:128], start=True, stop=True),
                             reads=[ones, ptc], writes=[Dn])
                    if gi == 3 or qb == nb - 1:
                        ng = gi + 1
                        qb0 = qb - gi
                        lo = r + dil * 128 * qb0
                        hi = lo + dil * (128 * ng - 1) + 1
                        nsl = num[:, lo:hi:dil]
                        dsl = den[:, lo:hi:dil]
                        if pi == 0:
                            P.op("act", lambda e: e.activation(out=nsl, in_=N[:, 0:ng * 128], func=AF.Copy), reads=[N], writes=[num])
                            P.op("dve", lambda e: e.tensor_copy(out=dsl, in_=Dn[:, 0:ng * 128]), reads=[Dn], writes=[den])
                        else:
                            P.op("dve", lambda e: e.tensor_tensor(out=nsl, in0=nsl, in1=N[:, 0:ng * 128], op=ALU.add), reads=[num, N], writes=[num])
                            P.op("dve", lambda e: e.tensor_tensor(out=dsl, in0=dsl, in1=Dn[:, 0:ng * 128], op=ALU.add), reads=[den, Dn], writes=[den])
                        gcount += 1

                for t in tasks:
                    if t[0] == "load":
                        _, v_, vdm, r, nb, src = t
                        P.dma("sp", v_[:, r * nb:(r + 1) * nb, :], src, vdm, reads=sblk, writes=[v_])
                        continue
                    _, pi, dil, nb, r, kb, v_ = t
                    nq = 256 if kb < nb - 1 else 128
                    base = r + dil * 128 * kb
                    ksl = k_[:, base:base + dil * 127 + 1:dil]
                    qsl = q_[:, base:base + dil * (nq - 1) + 1:dil]
                    Sb = psS[bi % 3]
                    eb = ebf[bi % 3]
                    pt = PT[bi % NPT]
                    ptmap[bi] = pt
                    P.op("pe", lambda e, Sb=Sb, ksl=ksl, qsl=qsl, nq=nq: e.matmul(Sb[:, 0:nq], lhsT=ksl, rhs=qsl, start=True, stop=True),
                         reads=[k_, q_], writes=[Sb])
                    P.op("act", lambda e, Sb=Sb, eb=eb, nq=nq: e.activation(out=eb[:, 0:nq], in_=Sb[:, 0:nq], func=AF.Exp, scale=scale),
                         reads=[Sb], writes=[eb])
                    P.op("pool" if bi % 3 == 0 else "dve", lambda e, eb=eb, pt=pt, nq=nq: e.tensor_tensor(out=pt[:, 0:nq], in0=eb[:, 0:nq], in1=mask[:, 0:nq], op=ALU.mult),
                         reads=[eb, mask], writes=[pt])
                    pending.append(bi)
                    bi += 1
                    if len(pending) > SKEW:
                        stage2(pending.pop(0))
                while pending:
                    stage2(pending.pop(0))
                P.op("act", lambda e: e.activation(out=rden[:], in_=den[:], func=AF.Ln), reads=[den], writes=[rden])
                P.op("act", lambda e: e.activation(out=rden[:], in_=rden[:], func=AF.Exp, scale=-1.0), reads=[rden], writes=[rden])
                P.op("pool", lambda e: e.tensor_tensor(out=y[:], in0=num[:], in1=rden[:], op=ALU.mult), reads=[num, rden], writes=[y])
                P.dma("sp", G.catT.t[4 + hd, :, s * S:(s + 1) * S], y[:], yd, reads=[y], writes=[G.catT_b[s * BPS + bb] for bb in range(BPS)])
        P.end_phase()
    P.scope = P.es
```

```python
import contextlib
import math
import numpy as np
import concourse.bass as bass
import concourse.mybir as mybir
from concourse.bass_utils import run_bass_kernel_spmd

F32 = mybir.dt.float32
BF16 = mybir.dt.bfloat16
I32 = mybir.dt.int32
AF = mybir.ActivationFunctionType
ALU = mybir.AluOpType
AX = mybir.AxisListType

D = 1024
DFF = 2816
NH = 4
DH = 128
TB = 512
NJ = TB // 128
DEPTH = 4
ALPHA = float((2 * DEPTH) ** 0.25)
EPS = 1e-5
POOL_W = (2, 4, 8, 16)
DIL_PATTERNS = ((128, 1), (512, 4), (2048, 16))
EPOCH = 30000
CDEC = [float(math.exp(128.0 * math.log1p(-(2.0 ** (-5.0 - h))))) for h in range(NH)]

C_INVF, C_SGN, C_ID, C_DEC, C_KDEC, C_CDEC, C_QDEC, C_MASK, C_PCNT, C_END = (
    0, 1, 2, 130, 642, 1154, 1666, 3714, 3970, 4034)


class Buf:
    __slots__ = ("name", "t", "excl", "lw", "rd", "rd_dma")

    def __init__(self, name, t=None, excl=False):
        self.name = name
        self.t = t
        self.excl = excl
        self.lw = None
        self.rd = {}
        self.rd_dma = []

    def __getitem__(self, k):
        return self.t[k]


class DSem:
    __slots__ = ("sem", "cnt")

    def __init__(self, sem):
        self.sem = sem
        self.cnt = 0


class Op:
    __slots__ = ("eng", "fn", "deps", "sig", "sem", "val", "isdma", "dsem")

    def __init__(self, eng, fn, isdma=False, dsem=None):
        self.eng = eng
        self.fn = fn
        self.deps = ()
        self.sig = False
        self.sem = None
        self.val = 0
        self.isdma = isdma
        self.dsem = dsem


class Prog:
    ENGS = ("pe", "act", "dve", "pool", "sp")

    def __init__(self, nc):
        self.nc = nc
        self.es = contextlib.ExitStack()
        self.scope = self.es
        self.ops = []
        self.eng = {"pe": nc.tensor, "act": nc.scalar, "dve": nc.vector, "pool": nc.gpsimd, "sp": nc.sync}
        self.nsem = 0
        self.nbuf = 0
        self.cnt = {}
        self.cursem = {}
        self.waited = {e: {} for e in self.ENGS}
        self.pending_dma = []
        self.nwait = 0
        self.ninst = 0
        self.dsem_pool = []
        self.phase_dsems = []
        self.bar_t = self.es.enter_context(self.nc.sbuf_tensor("bar_scr", [128, 8], F32))

    def sbuf(self, name, shape, dt):
        self.nbuf += 1
        t = self.scope.enter_context(self.nc.sbuf_tensor(f"{name}_{self.nbuf}", list(shape), dt))
        return Buf(name, t)

    def psum(self, name, shape, dt):
        self.nbuf += 1
        t = self.scope.enter_context(self.nc.psum_tensor(f"{name}_{self.nbuf}", list(shape), dt))
        return Buf(name, t, excl=True)

    def dram(self, name, shape, dt, kind="Internal"):
        t = self.nc.dram_tensor(name, list(shape), dt, kind=kind)
        return Buf(name, t.ap())

    def new_sem(self, name=None):
        self.nsem += 1
        return self.es.enter_context(self.nc.semaphore(name or f"s{self.nsem}"))

    def dsem(self):
        d = self.dsem_pool.pop() if self.dsem_pool else DSem(self.new_sem())
        self.phase_dsems.append(d)
        return d

    def end_phase(self):
        self.barrier()
        self.flush()
        self.dsem_pool.extend(self.phase_dsems)
        self.phase_dsems = []

    def _add(self, op, reads, writes):
        eng = op.eng
        deps = set()
        for b in reads:
            if b.lw is not None:
                deps.add(b.lw)
            if b.excl:
                for e, r in b.rd.items():
                    if e != eng:
                        deps.add(r)
        for b in writes:
            if b.lw is not None:
                deps.add(b.lw)
            for r in b.rd.values():
                deps.add(r)
            for r in b.rd_dma:
                deps.add(r)
        if eng == "pe" and not op.isdma:
            deps = {d for d in deps if d.isdma or d.eng != "pe"}
        op.deps = tuple(deps)
        for d in deps:
            d.sig = True
        for b in reads:
            if op.isdma:
                b.rd_dma.append(op)
            else:
                b.rd[eng] = op
        for b in writes:
            b.lw = op
            b.rd = {}
            b.rd_dma = []
        self.ops.append(op)
        return op

    def op(self, eng, fn, reads=(), writes=()):
        return self._add(Op(eng, fn), reads, writes)

    def dma(self, q, out, in_, dsem, reads=(), writes=(), **kw):
        def fn(e, out=out, in_=in_, kw=kw):
            return e.dma_start(out=out, in_=in_, **kw)
        o = Op(q, fn, isdma=True, dsem=dsem)
        o.sig = True
        self.pending_dma.append(o)
        return self._add(o, reads, writes)

    def barrier(self):
        if self.bar_t is None:
            t = self.es.enter_context(self.nc.sbuf_tensor("bar_scr", [128, 8], F32))
            self.bar_t = t
        t = self.bar_t
        first = []
        for i, e in enumerate(("act", "dve", "pool")):
            if e == "act":
                o = Op(e, (lambda en, i=i: en.memzero(t[:, i:i + 1])))
            else:
                o = Op(e, (lambda en, i=i: en.memset(t[:, i:i + 1], 0.0)))
            o.sig = True
            self.ops.append(o)
            first.append(o)
        last_pe = None
        for o in reversed(self.ops):
            if o.eng == "pe" and not o.isdma:
                last_pe = o
                break
        if last_pe is not None:
            last_pe.sig = True
            first.append(last_pe)
        deps = tuple(first) + tuple(self.pending_dma)
        self.pending_dma = []
        self.join_deps = deps

    def flush(self, with_join=True):
        self._emit()
        jd = getattr(self, "join_deps", None)
        if jd:
            for en in self.ENGS:
                self._waits(en, jd)
            self.join_deps = None

    def _assign(self, op):
        if op.isdma:
            op.dsem.cnt += 16
            op.sem = op.dsem.sem
            op.val = op.dsem.cnt
        elif op.sig:
            e = op.eng
            if e not in self.cursem or self.cnt[e] >= EPOCH:
                self.cursem[e] = self.new_sem(f"e_{e}_{self.nsem}")
                self.cnt[e] = 0
            self.cnt[e] += 1
            op.sem = self.cursem[e]
            op.val = self.cnt[e]

    def _waits(self, en, deps):
        e = self.eng[en]
        w = self.waited[en]
        need = {}
        for d in deps:
            k = id(d.sem)
            if w.get(k, 0) >= d.val:
                continue
            if k not in need or need[k][1] < d.val:
                need[k] = (d.sem, d.val)
        for k, (s, v) in need.items():
            e.wait_ge(s, v)
            w[k] = v
            self.nwait += 1

    def _emit(self):
        for op in self.ops:
            self._assign(op)
        for op in self.ops:
            self._waits(op.eng, op.deps)
            ins = op.fn(self.eng[op.eng])
            if op.isdma:
                ins.then_inc(op.sem, 16)
            elif op.sig:
                ins.then_inc(op.sem, 1)
            op.fn = None
            self.ninst += 1
        self.ops = []

    def finish(self, final_ops=()):
        self._emit()
        e = self.eng["sp"]
        for d in final_ops:
            e.wait_ge(d.sem, d.val)
        self.es.close()


def make_consts():
    c = np.zeros((128, C_END), np.float64)
    inv = 10000.0 ** (-(np.arange(0, DH, 2, dtype=np.float32) / np.float32(DH)).astype(np.float32))
    inv = inv.astype(np.float32)
    p = np.arange(128)
    c[:, C_INVF] = inv[p % 64]
    c[:, C_SGN] = np.where(p < 64, -1.0, 1.0)
    c[:, C_ID:C_ID + 128] = np.eye(128)
    lg = np.log1p(-(2.0 ** (-5.0 - np.arange(NH, dtype=np.float64))))
    i = np.arange(128)
    sc = DH ** -0.5
    for h in range(NH):
        rel = i[None, :] - i[:, None]
        c[:, C_DEC + h * 128:C_DEC + (h + 1) * 128] = np.where(rel >= 0, sc * np.exp(np.maximum(rel, 0) * lg[h]), 0.0)
        c[:, C_KDEC + h * 128:C_KDEC + (h + 1) * 128] = (sc * np.exp((127 - i) * lg[h]))[:, None]
        c[:, C_CDEC + h * 128:C_CDEC + (h + 1) * 128] = np.exp(128 * lg[h])
        t = np.arange(TB)
        c[:, C_QDEC + h * TB:C_QDEC + (h + 1) * TB] = np.exp(((t % 128) + 1) * lg[h])[None, :]
    cc = np.arange(256)
    c[:, C_MASK:C_MASK + 256] = ((cc[None, :] >= p[:, None]) & (cc[None, :] <= p[:, None] + 128)).astype(np.float64)
    for g, w in enumerate(POOL_W):
        t = np.arange(16)
        c[:, C_PCNT + g * 16:C_PCNT + (g + 1) * 16] = (1.0 / np.minimum(t + 1, w))[None, :]
    return c.astype(np.float32)


class Ctx:
    pass


def load_weight(P, stages, sidx, src_ap, srcBuf, dst_ap, dstBuf, ncols):
    st, ds = stages[sidx[0] % len(stages)]
    ce = ("act", "pool", "dve")[sidx[0] % 3]
    sidx[0] += 1
    P.dma("sp", st[:, 0:ncols], src_ap, ds, reads=[srcBuf], writes=[st])
    if ce == "act":
        P.op("act", lambda e: e.activation(out=dst_ap, in_=st[:, 0:ncols], func=AF.Copy), reads=[st], writes=[dstBuf])
    else:
        P.op(ce, lambda e: e.tensor_copy(out=dst_ap, in_=st[:, 0:ncols]), reads=[st], writes=[dstBuf])


def bulk_load(P, items, width, nstage=8):
    outer = P.scope
    with contextlib.ExitStack() as st:
        P.scope = st
        stages = [(P.sbuf(f"stg{i}", [128, width], F32), P.dsem()) for i in range(nstage)]
        for n_, (src_ap, srcBuf, dst_ap, dstBuf, ncols) in enumerate(items):
            stg, ds = stages[n_ % nstage]
            P.dma("sp" if n_ % 2 == 0 else "act", stg[:, 0:ncols], src_ap, ds, reads=[srcBuf], writes=[stg])
            if n_ % 2 == 0:
                P.op("dve", lambda e, stg=stg, dst_ap=dst_ap, ncols=ncols: e.tensor_copy(out=dst_ap, in_=stg[:, 0:ncols]), reads=[stg], writes=[dstBuf])
            else:
                P.op("act", lambda e, stg=stg, dst_ap=dst_ap, ncols=ncols: e.activation(out=dst_ap, in_=stg[:, 0:ncols], func=AF.Copy), reads=[stg], writes=[dstBuf])
        P.barrier()
        P.flush()
    P.scope = outer


def ln_A(P, r_ap, rBuf, sm, k):
    i = k % 3
    st, mv, ve, nm, rs, nb, nh = sm["st"][i], sm["mv"][i], sm["ve"][i], sm["nm"][i], sm["rs"][i], sm["nb"][i], sm["nh"]
    for h in range(2):
        P.op("dve", lambda e, h=h: e.bn_stats(out=st[:, h, :], in_=r_ap[:, h * 512:(h + 1) * 512]), reads=[rBuf], writes=[st])
    P.op("dve", lambda e: e.bn_aggr(out=mv[:], in_=st[:].rearrange("p a b -> p (a b)")), reads=[st], writes=[mv])
    P.op("dve", lambda e: e.tensor_scalar_add(out=ve[:], in0=mv[:, 1:2], scalar1=EPS), reads=[mv], writes=[ve])
    P.op("dve", lambda e: e.tensor_scalar(out=nm[:], in0=mv[:, 0:1], scalar1=-1.0, scalar2=None, op0=ALU.mult), reads=[mv], writes=[nm])
    P.op("pool", lambda e: e.tensor_tensor(out=rs[:], in0=ve[:], in1=nh[:, 0:1], op=ALU.pow), reads=[ve, nh], writes=[rs])
    P.op("pool", lambda e: e.tensor_tensor(out=nb[:], in0=nm[:], in1=rs[:], op=ALU.mult), reads=[nm, rs], writes=[nb])
    return rs, nb


def ln_A2(P, r_ap, rBuf, ssum, junk, sm, k):
    i = k % 3
    mv, ve, nm, rs, nb, nh = sm["mv"][i], sm["ve"][i], sm["nm"][i], sm["rs"][i], sm["nb"][i], sm["nh"]
    sq = sm["st"][i]
    P.op("act", lambda e: e.activation(out=junk[:], in_=r_ap, func=AF.Square, accum_out=sq[:, 0, 0:1]), reads=[rBuf], writes=[junk, sq])
    P.op("dve", lambda e: e.tensor_tensor(out=mv[:, 0:1], in0=ssum[:, 0:1], in1=ssum[:, 1:2], op=ALU.add), reads=[ssum], writes=[mv])
    P.op("dve", lambda e: e.tensor_scalar(out=nm[:], in0=mv[:, 0:1], scalar1=-1.0 / D, scalar2=None, op0=ALU.mult), reads=[mv], writes=[nm])
    P.op("dve", lambda e: e.tensor_tensor(out=mv[:, 1:2], in0=nm[:], in1=nm[:], op=ALU.mult), reads=[nm], writes=[mv])
    P.op("dve", lambda e: e.scalar_tensor_tensor(out=ve[:], in0=sq[:, 0, 0:1], scalar=1.0 / D, in1=mv[:, 1:2], op0=ALU.mult, op1=ALU.subtract),
         reads=[sq, mv], writes=[ve])
    P.op("dve", lambda e: e.tensor_scalar_add(out=ve[:], in0=ve[:], scalar1=EPS), reads=[ve], writes=[ve])
    P.op("pool", lambda e: e.tensor_tensor(out=rs[:], in0=ve[:], in1=nh[:, 0:1], op=ALU.pow), reads=[ve, nh], writes=[rs])
    P.op("pool", lambda e: e.tensor_tensor(out=nb[:], in0=nm[:], in1=rs[:], op=ALU.mult), reads=[nm, rs], writes=[nb])
    return rs, nb


def ln_B(P, r_ap, rBuf, o_ap, oBuf, g_tab, b_tab, gbBuf, rs, nb):
    P.op("act", lambda e: e.activation(out=o_ap, in_=r_ap, func=AF.Identity, bias=nb[:], scale=rs[:]), reads=[rBuf, nb, rs], writes=[oBuf])
    P.op("dve", lambda e: e.tensor_tensor(out=o_ap, in0=o_ap, in1=g_tab, op=ALU.mult), reads=[oBuf, gbBuf], writes=[oBuf])
    P.op("pool", lambda e: e.tensor_tensor(out=o_ap, in0=o_ap, in1=b_tab, op=ALU.add), reads=[oBuf, gbBuf], writes=[oBuf])


def ln_smalls(P, tag):
    sm = {}
    sm["st"] = [P.sbuf(f"{tag}_st{i}", [128, 2, 6], F32) for i in range(3)]
    sm["mv"] = [P.sbuf(f"{tag}_mv{i}", [128, 2], F32) for i in range(3)]
    for n in ("ve", "nm", "rs", "nb"):
        sm[n] = [P.sbuf(f"{tag}_{n}{i}", [128, 1], F32) for i in range(3)]
    sm["nh"] = P.sbuf(f"{tag}_nh", [128, 16], F32)
    P.op("pool", lambda e: e.memset(sm["nh"][:], -0.5), writes=[sm["nh"]])
    return sm


def blk_bufs(name, ap, nblk):
    return [Buf(f"{name}{b}", ap) for b in range(nblk)]


def phase_P(P, G, w_out_ap, wBuf, lng_ap, lnb_ap, lnBuf, hin, hin_b, hout, hout_b):
    NT, NB = G.NT, G.NB
    with contextlib.ExitStack() as sc:
        P.scope = sc
        wo = P.sbuf("P_wo", [128, 8, D], BF16)
        wo_k = [Buf(f"P_wo{k}", wo.t) for k in range(8)]
        bulk_load(P, [(w_out_ap[k * 128:(k + 1) * 128, :], wBuf, wo.t[:, k, :], wo_k[k], 1024) for k in range(8)], 1024)
        gb = P.sbuf("P_gb", [128, 2, D], F32)
        gds = P.dsem()
        P.dma("sp", gb[:, 0, :], lng_ap.partition_broadcast(128), gds, reads=[lnBuf], writes=[gb])
        P.dma("sp", gb[:, 1, :], lnb_ap.partition_broadcast(128), gds, reads=[lnBuf], writes=[gb])
        sm = ln_smalls(P, "P")
        ssums = [P.sbuf(f"P_ss{i}", [128, 2], F32) for i in range(3)]
        junk = P.sbuf("P_junk", [128, D], F32)
        ct = [P.sbuf(f"P_ct{i}", [128, 8, TB], BF16) for i in range(2)]
        ctd = [P.dsem() for _ in range(2)]
        hb = [[P.sbuf(f"P_h{i}_{j}", [128, D], F32) for j in range(NJ)] for i in range(3)]
        hd_ = [[P.dsem() for j in range(NJ)] for i in range(3)]
        ob = [P.sbuf(f"P_o{i}", [128, D], F32) for i in range(4)]
        od = [P.dsem() for _ in range(4)]
        ps = [P.psum(f"P_ps{i}", [128, 512], F32) for i in range(4)]

        def loads(b):
            i = b % 2
            h3 = b % 3
            t0 = b * TB
            P.dma("sp", ct[i][:], G.catT.t[:, :, t0:t0 + TB].rearrange("c p t -> p c t"), ctd[i], reads=[G.catT_b[b]], writes=[ct[i]])
            for j in range(NJ):
                P.dma("sp", hb[h3][j][:], hin.t[t0 + j * 128:t0 + (j + 1) * 128, :], hd_[h3][j], reads=[hin_b[b]], writes=[hb[h3][j]])

        loads(0)
        kk = 0
        pend = None
        st4 = {n: [P.sbuf(f"P4_{n}{i}", [128, 4], F32) for i in range(2)] for n in ("s", "q", "nm", "msq", "ve", "rs", "nb")}
        ss8 = [P.sbuf(f"P4_ss{i}", [128, 8], F32) for i in range(2)]
        nh4 = P.sbuf("P4_nh", [128, 4], F32)
        P.op("pool", lambda e: e.memset(nh4[:], -0.5), writes=[nh4])
        for b in range(NB):
            i = b % 2
            t0 = b * TB
            h3 = b % 3
            if b + 1 < NB:
                loads(b + 1)
            ss, sq4 = ss8[i], st4["q"][i]
            for j in range(NJ):
                for n in range(2):
                    pb = ps[(2 * j + n) % 4]
                    for k in range(8):
                        P.op("pe", lambda e, pb=pb, k=k, j=j, n=n, i=i: e.matmul(
                            pb[:, :], lhsT=ct[i][:, k, j * 128:(j + 1) * 128], rhs=wo.t[:, k, n * 512:(n + 1) * 512],
                            start=(k == 0), stop=(k == 7)), reads=[ct[i], wo_k[k]], writes=[pb])
                    P.op("dve", lambda e, pb=pb, j=j, n=n, h3=h3, ss=ss: e.scalar_tensor_tensor(
                        out=hb[h3][j][:, n * 512:(n + 1) * 512], in0=hb[h3][j][:, n * 512:(n + 1) * 512], scalar=ALPHA,
                        in1=pb[:, :], op0=ALU.mult, op1=ALU.add, accum_out=ss[:, 2 * j + n:2 * j + n + 1]), reads=[hb[h3][j], pb], writes=[hb[h3][j], ss])
                P.op("act", lambda e, j=j, h3=h3, sq4=sq4: e.activation(out=junk[:], in_=hb[h3][j][:], func=AF.Square, accum_out=sq4[:, j:j + 1]),
                     reads=[hb[h3][j]], writes=[junk, sq4])
                if pend is not None:
                    try:
                        next(pend)
                    except StopIteration:
                        pend = None

            def tail(b=b, i=i, h3=h3, t0=t0, ss=ss, sq4=sq4):
                nonlocal kk
                s_, nm, msq, ve, rs, nb_ = (st4[n][i] for n in ("s", "nm", "msq", "ve", "rs", "nb"))
                P.op("dve", lambda e: e.tensor_reduce(out=s_[:], in_=ss[:].rearrange("p (j n) -> p j n", n=2), axis=AX.X, op=ALU.add), reads=[ss], writes=[s_])
                P.op("dve", lambda e: e.tensor_scalar(out=nm[:], in0=s_[:], scalar1=-1.0 / D, scalar2=None, op0=ALU.mult), reads=[s_], writes=[nm])
                P.op("dve", lambda e: e.tensor_tensor(out=msq[:], in0=nm[:], in1=nm[:], op=ALU.mult), reads=[nm], writes=[msq])
                P.op("dve", lambda e: e.scalar_tensor_tensor(out=ve[:], in0=sq4[:], scalar=1.0 / D, in1=msq[:], op0=ALU.mult, op1=ALU.subtract),
                     reads=[sq4, msq], writes=[ve])
                P.op("dve", lambda e: e.tensor_scalar_add(out=ve[:], in0=ve[:], scalar1=EPS), reads=[ve], writes=[ve])
                P.op("pool", lambda e: e.tensor_tensor(out=rs[:], in0=ve[:], in1=nh4[:], op=ALU.pow), reads=[ve, nh4], writes=[rs])
                P.op("pool", lambda e: e.tensor_tensor(out=nb_[:], in0=nm[:], in1=rs[:], op=ALU.mult), reads=[nm, rs], writes=[nb_])
                for j in range(NJ):
                    o = ob[kk % 4]
                    P.op("act", lambda e, j=j, o=o: e.activation(out=o[:], in_=hb[h3][j][:], func=AF.Identity, bias=nb_[:, j:j + 1], scale=rs[:, j:j + 1]),
                         reads=[hb[h3][j], nb_, rs], writes=[o])
                    P.op("dve", lambda e, o=o: e.tensor_tensor(out=o[:], in0=o[:], in1=gb[:, 0, :], op=ALU.mult), reads=[o, gb], writes=[o])
                    P.op("pool", lambda e, o=o: e.tensor_tensor(out=o[:], in0=o[:], in1=gb[:, 1, :], op=ALU.add), reads=[o, gb], writes=[o])
                    P.dma("sp", hout.t[t0 + j * 128:t0 + (j + 1) * 128, :], o[:], od[kk % 4], reads=[o], writes=[hout_b[b]])
                    kk += 1
                    if j % 2 == 1:
                        yield
            if pend is not None:
                for _ in pend:
                    pass
            pend = tail()
        for _ in pend:
            pass
        P.end_phase()
    P.scope = P.es


def phase_F(P, G, wi_ap, wiBuf, wf_ap, wfBuf, lng_ap, lnb_ap, lnBuf, hin, hin_b, hout, hout_b):
    NT, NB = G.NT, G.NB
    NC = DFF // 128
    fin = []
    with contextlib.ExitStack() as sc:
        P.scope = sc
        wi = P.sbuf("F_wi", [128, 8, 2 * DFF], BF16)
        wi_k = [Buf(f"F_wi{k}", wi.t) for k in range(8)]
        wf = P.sbuf("F_wf", [128, NC, D], BF16)
        wf_k = [Buf(f"F_wf{k}", wf.t) for k in range(NC)]
        items = []
        for k in range(8):
            for q in range(4):
                items.append((wi_ap[k * 128:(k + 1) * 128, q * 1408:(q + 1) * 1408], wiBuf, wi.t[:, k, q * 1408:(q + 1) * 1408], wi_k[k], 1408))
        for k in range(NC):
            items.append((wf_ap[k * 128:(k + 1) * 128, :], wfBuf, wf.t[:, k, :], wf_k[k], 1024))
        bulk_load(P, items, 1408)
        gb = P.sbuf("F_gb", [128, 2, D], F32)
        gds = P.dsem()
        P.dma("sp", gb[:, 0, :], lng_ap.partition_broadcast(128), gds, reads=[lnBuf], writes=[gb])
        P.dma("sp", gb[:, 1, :], lnb_ap.partition_broadcast(128), gds, reads=[lnBuf], writes=[gb])
        ident = P.sbuf("F_id", [128, 128], F32)
        P.dma("sp", ident[:], G.consts.t[:, C_ID:C_ID + 128], P.dsem(), reads=[G.consts], writes=[ident])
        sm = ln_smalls(P, "F")
        NR = 6
        hbr = [P.sbuf(f"F_h{j}", [128, D], F32) for j in range(NR)]
        hdr = [P.dsem() for j in range(NR)]
        ob = [P.sbuf(f"F_o{i}", [128, D], F32) for i in range(2)]
        od = [P.dsem() for _ in range(2)]
        hT = P.sbuf("F_hT", [128, 8, TB], BF16)
        hT_k = [Buf(f"F_hT{k}", hT.t) for k in range(8)]
        act = P.sbuf("F_act", [128, NC, TB], BF16)
        act_k = [Buf(f"F_act{k}", act.t) for k in range(NC)]
        sg = [P.sbuf(f"F_sg{i}", [128, TB], F32) for i in range(2)]
        psO = [P.psum(f"F_psO{i}", [128, 512], F32) for i in range(2)]
        psG = [P.psum(f"F_psG{i}", [128, 512], F32) for i in range(2)]
        psU = [P.psum(f"F_psU{i}", [128, 512], F32) for i in range(2)]
        psT = [P.psum(f"F_psT{i}", [128, 512], F32) for i in range(2)]

        def load_tile(b, j):
            t0 = b * TB
            r = (4 * b + j) % NR
            P.dma("sp", hbr[r][:], hin.t[t0 + j * 128:t0 + (j + 1) * 128, :], hdr[r], reads=[hin_b[b]], writes=[hbr[r]])

        for j in range(NJ):
            load_tile(0, j)
        if NB > 1:
            load_tile(1, 0)
            load_tile(1, 1)
        kk = 0
        pend = None

        def do_transposes(b):
            hb = [hbr[(4 * b + j) % NR] for j in range(NJ)]
            for c in range(8):
                pt = psT[c % 2]
                for j in range(NJ):
                    P.op("pe", lambda e, pt=pt, j=j, c=c, hb=hb: e.transpose(out=pt[:, j * 128:(j + 1) * 128],
                                                                             in_=hb[j][:, c * 128:(c + 1) * 128], identity=ident[:]),
                         reads=[hb[j], ident], writes=[pt])
                if c % 2 == 0:
                    P.op("dve", lambda e, pt=pt, c=c: e.tensor_copy(out=hT.t[:, c, :], in_=pt[:, :]), reads=[pt], writes=[hT_k[c]])
                else:
                    P.op("act", lambda e, pt=pt, c=c: e.activation(out=hT.t[:, c, :], in_=pt[:, :], func=AF.Copy), reads=[pt], writes=[hT_k[c]])

        do_transposes(0)
        for b in range(NB):
            t0 = b * TB
            hb = [hbr[(4 * b + j) % NR] for j in range(NJ)]
            for c in range(NC):
                pg, pu, s = psG[c % 2], psU[c % 2], sg[c % 2]
                for k in range(8):
                    P.op("pe", lambda e, pg=pg, k=k, c=c: e.matmul(pg[:, :], lhsT=wi.t[:, k, c * 128:(c + 1) * 128], rhs=hT.t[:, k, :],
                                                                   start=(k == 0), stop=(k == 7)), reads=[wi_k[k], hT_k[k]], writes=[pg])
                for k in range(8):
                    P.op("pe", lambda e, pu=pu, k=k, c=c: e.matmul(pu[:, :], lhsT=wi.t[:, k, DFF + c * 128:DFF + (c + 1) * 128], rhs=hT.t[:, k, :],
                                                                   start=(k == 0), stop=(k == 7)), reads=[wi_k[k], hT_k[k]], writes=[pu])
                P.op("act", lambda e, pg=pg, s=s: e.activation(out=s[:], in_=pg[:, :], func=AF.Silu), reads=[pg], writes=[s])
                P.op("dve", lambda e, pu=pu, s=s, c=c: e.tensor_tensor(out=act.t[:, c, :], in0=s[:], in1=pu[:, :], op=ALU.mult),
                     reads=[s, pu], writes=[act_k[c]])
            for j in range(NJ):
                for n in range(2):
                    pb = psO[n]
                    for k in range(NC):
                        P.op("pe", lambda e, pb=pb, k=k, j=j, n=n: e.matmul(
                            pb[:, :], lhsT=act.t[:, k, j * 128:(j + 1) * 128], rhs=wf.t[:, k, n * 512:(n + 1) * 512],
                            start=(k == 0), stop=(k == NC - 1)), reads=[act_k[k], wf_k[k]], writes=[pb])
                    P.op("dve", lambda e, pb=pb, j=j, n=n, hb=hb: e.scalar_tensor_tensor(
                        out=hb[j][:, n * 512:(n + 1) * 512], in0=hb[j][:, n * 512:(n + 1) * 512], scalar=ALPHA,
                        in1=pb[:, :], op0=ALU.mult, op1=ALU.add), reads=[hb[j], pb], writes=[hb[j]])
                    if j == NJ - 1 and n == 0 and b + 1 < NB:
                        do_transposes(b + 1)
                rs_, nb_ = ln_A(P, hb[j][:], hb[j], sm, kk)

                def fin_tile(j=j, kk=kk, rs_=rs_, nb_=nb_, t0=t0, b=b, hb=hb):
                    o = ob[kk % 2]
                    ln_B(P, hb[j][:], hb[j], o[:], o, gb[:, 0, :], gb[:, 1, :], gb, rs_, nb_)
                    if j < 2 and b + 1 < NB:
                        load_tile(b + 1, j + 2)
                    if j >= 2 and b + 2 < NB:
                        load_tile(b + 2, j - 2)
                    fin.append(P.dma("sp", hout.t[t0 + j * 128:t0 + (j + 1) * 128, :], o[:], od[kk % 2], reads=[o], writes=[hout_b[b]]))
                if pend is not None:
                    pend()
                pend = fin_tile
                kk += 1
            pend()
            pend = None
        P.end_phase()
    P.scope = P.es
    return fin


def phase_R(P, G):
    S, NS = G.S, G.NS
    MAGIC = 12582912.0
    HI = 6.28125
    LO = 2.0 * math.pi - HI
    PIL = 3.1415925
    with contextlib.ExitStack() as sc:
        P.scope = sc
        cs = P.sbuf("R_cs", [128, 2], F32)
        P.dma("sp", cs[:], G.consts.t[:, 0:2], P.dsem(), reads=[G.consts], writes=[cs])
        pi = P.sbuf("R_pi", [128, S], I32)
        ang = P.sbuf("R_ang", [128, S], F32)
        kq = P.sbuf("R_k", [128, S], F32)
        r = P.sbuf("R_r", [128, S], F32)
        oc = P.sbuf("R_oc", [128, S], F32)
        os_ = P.sbuf("R_os", [128, S], F32)
        d1, d2, d3 = P.dsem(), P.dsem(), P.dsem()
        for s in range(NS):
            P.dma("sp", pi[:], G.pos.t[s:s + 1, :].partition_broadcast(128), d1, reads=[G.pos], writes=[pi])
            P.op("dve", lambda e: e.tensor_copy(out=ang[:], in_=pi[:]), reads=[pi], writes=[ang])
            P.op("dve", lambda e: e.tensor_scalar(out=ang[:], in0=ang[:], scalar1=cs[:, 0:1], scalar2=None, op0=ALU.mult),
                 reads=[ang, cs], writes=[ang])
            P.op("dve", lambda e: e.tensor_scalar(out=kq[:], in0=ang[:], scalar1=float(1.0 / (2.0 * math.pi)), scalar2=MAGIC,
                                                  op0=ALU.mult, op1=ALU.add), reads=[ang], writes=[kq])
            P.op("dve", lambda e: e.tensor_scalar_add(out=kq[:], in0=kq[:], scalar1=-MAGIC), reads=[kq], writes=[kq])
            P.op("dve", lambda e: e.scalar_tensor_tensor(out=r[:], in0=kq[:], scalar=-HI, in1=ang[:], op0=ALU.mult, op1=ALU.add),
                 reads=[kq, ang], writes=[r])
            P.op("dve", lambda e: e.scalar_tensor_tensor(out=r[:], in0=kq[:], scalar=-LO, in1=r[:], op0=ALU.mult, op1=ALU.add),
                 reads=[kq, r], writes=[r])
            P.op("dve", lambda e: e.tensor_scalar(out=r[:], in0=r[:], scalar1=-PIL, scalar2=PIL, op0=ALU.max, op1=ALU.min),
                 reads=[r], writes=[r])
            P.op("act", lambda e: e.activation(out=os_[:], in_=r[:], func=AF.Sin, scale=cs[:, 1:2]), reads=[r, cs], writes=[os_])
            P.op("dve", lambda e: e.scalar_tensor_tensor(out=kq[:], in0=r[:], scalar=-1.0, in1=r[:], op0=ALU.mult, op1=ALU.max), reads=[r], writes=[kq])
            P.op("act", lambda e: e.activation(out=oc[:], in_=kq[:], func=AF.Sin, scale=-1.0, bias=float(math.pi / 2)),
                 reads=[kq], writes=[oc])
            P.dma("sp", G.cosT.t[s, :, :], oc[:], d2, reads=[oc], writes=[G.cosT])
            P.dma("sp", G.sinT.t[s, :, :], os_[:], d3, reads=[os_], writes=[G.sinT])
        P.end_phase()
    P.scope = P.es


def transpose_block(P, hb, ident, psT, hT, hT_k):
    for c in range(8):
        pt = psT[c % len(psT)]
        for j in range(NJ):
            P.op("pe", lambda e, pt=pt, j=j, c=c: e.transpose(out=pt[:, j * 128:(j + 1) * 128],
                                                              in_=hb[j][:, c * 128:(c + 1) * 128], identity=ident[:]),
                 reads=[hb[j], ident], writes=[pt])
        if c % 2 == 0:
            P.op("dve", lambda e, pt=pt, c=c: e.tensor_copy(out=hT.t[:, c, :], in_=pt[:, :]), reads=[pt], writes=[hT_k[c]])
        else:
            P.op("act", lambda e, pt=pt, c=c: e.activation(out=hT.t[:, c, :], in_=pt[:, :], func=AF.Copy), reads=[pt], writes=[hT_k[c]])


def rope_head(P, X, Y, cs, sn, t1, t2):
    P.op("dve", lambda e: e.tensor_tensor(out=t1[:], in0=X[:, :], in1=cs[:], op=ALU.mult), reads=[X, cs], writes=[t1])
    P.op("dve", lambda e: e.tensor_tensor(out=t2[:], in0=Y[:, :], in1=sn[:], op=ALU.mult), reads=[Y, sn], writes=[t2])
    P.op("pool", lambda e: e.tensor_tensor(out=t1[:], in0=t1[:], in1=t2[:], op=ALU.add), reads=[t1, t2], writes=[t1])


def phase_E(P, G, l2, hin, hin_b):
    S, NS, NB = G.S, G.NS, G.NB
    BPS = S // TB
    WC = 3584
    with contextlib.ExitStack() as sc:
        P.scope = sc
        wi = P.sbuf("E_wi", [128, 8, WC], BF16)
        wi_k = [Buf(f"E_wi{k}", wi.t) for k in range(8)]
        bulk_load(P, [(G.ev_w_in.t[l2, k * 128:(k + 1) * 128, q * 1792:(q + 1) * 1792], G.ev_w_in,
                       wi.t[:, k, q * 1792:(q + 1) * 1792], wi_k[k], 1792) for k in range(8) for q in range(2)], 1792)
        pw = P.sbuf("E_pw", [128, 4, 128], BF16)
        pst, pds = P.sbuf("E_pst", [128, 512], F32), P.dsem()
        P.dma("sp", pst[:, 0:512].rearrange("p (g d) -> p g d", g=4), G.ev_pool_w.t[l2].rearrange("g c d -> c g d"), pds,
              reads=[G.ev_pool_w], writes=[pst])
        P.op("dve", lambda e: e.tensor_copy(out=pw[:].rearrange("p g d -> p (g d)"), in_=pst[:, 0:512]), reads=[pst], writes=[pw])
        psc = P.sbuf("E_psc", [128, 4], F32)
        P.dma("sp", psc[:], G.ev_pool_scale.t[l2], P.dsem(), reads=[G.ev_pool_scale], writes=[psc])
        gain = P.sbuf("E_gain", [128, 512], F32)
        P.dma("sp", gain[:], G.ev_ret_norm_g.t[l2:l2 + 1, :].partition_broadcast(128), P.dsem(), reads=[G.ev_ret_norm_g], writes=[gain])
        ct = P.sbuf("E_ct", [128, C_END - C_ID], F32)
        P.dma("sp", ct[:], G.consts.t[:, C_ID:C_END], P.dsem(), reads=[G.consts], writes=[ct])
        o_ = lambda c: c - C_ID
        ident = Buf("E_ident", ct.t[:, o_(C_ID):o_(C_ID) + 128])
        dec = ct.t[:, o_(C_DEC):o_(C_DEC) + 512]
        kdec = ct.t[:, o_(C_KDEC):o_(C_KDEC) + 512]
        cdec = ct.t[:, o_(C_CDEC):o_(C_CDEC) + 512]
        qdec = ct.t[:, o_(C_QDEC):o_(C_QDEC) + 2048]
        pcnt = ct.t[:, o_(C_PCNT):o_(C_PCNT) + 64]
        ident_bf = P.sbuf("E_idbf", [128, 128], BF16)
        P.op("dve", lambda e: e.tensor_copy(out=ident_bf[:], in_=ct.t[:, 0:128]), reads=[ct], writes=[ident_bf])
        identF = Buf("E_identF", ct.t[:, 0:128])
        identF.lw = ct.lw
        nh = P.sbuf("E_nh", [128, 16], F32)
        P.op("pool", lambda e: e.memset(nh[:], -0.5), writes=[nh])

        cs = [P.sbuf(f"E_cs{i}", [128, TB], F32) for i in range(1)] * 2
        sn = [P.sbuf(f"E_sn{i}", [128, TB], F32) for i in range(1)] * 2
        csd = [P.dsem() for _ in range(1)] * 2
        snd = [P.dsem() for _ in range(1)] * 2
        hb = [P.sbuf(f"E_h{j}", [128, D], F32) for j in range(NJ)]
        hds = [P.dsem() for _ in range(NJ)]
        hT = P.sbuf("E_hT", [128, 8, TB], BF16)
        hT_k = [Buf(f"E_hT{k}", hT.t) for k in range(8)]
        qr = P.sbuf("E_qr", [128, NH, TB], BF16)
        kr = P.sbuf("E_kr", [128, NH, TB], BF16)
        qd = P.sbuf("E_qd", [128, NH, TB], BF16)
        t1 = [P.sbuf(f"E_t1{i}", [128, TB], F32) for i in range(2)]
        t2 = [P.sbuf(f"E_t2{i}", [128, TB], F32) for i in range(2)]
        vb = P.sbuf("E_vb", [128, NJ, 512], BF16)
        vdb = P.sbuf("E_vdb", [128, NJ, 512], BF16)
        gsg = P.sbuf("E_gsg", [128, NJ, 512], F32)
        xh = P.sbuf("E_xh", [128, 4, 16 + TB], F32)
        sa = P.sbuf("E_sa", [128, 16 + TB], F32)
        sb = P.sbuf("E_sb", [128, 16 + TB], F32)
        pooled = P.sbuf("E_pooled", [128, 4, TB], BF16)
        ws = P.sbuf("E_ws", [128, 4, 16 + TB], F32)
        state = P.sbuf("E_state", [128, 512], F32)
        stmp = P.sbuf("E_stmp", [128, 512], F32)
        sbf = [P.sbuf(f"E_sbf{i}", [128, 512], BF16) for i in range(6)]
        PT = [P.sbuf(f"E_PT{i}", [128, 512], BF16) for i in range(2)]
        ktok = [P.sbuf(f"E_ktok{i}", [128, 512], BF16) for i in range(2)]
        osb = P.sbuf("E_osb", [128, NJ, 512], F32)
        sq = P.sbuf("E_sq", [128, NJ, 512], F32)
        s1 = P.sbuf("E_s1", [128, 16], F32)
        s2 = P.sbuf("E_s2", [128, 16], F32)
        mean = P.sbuf("E_mean", [128, 16], F32)
        msq = P.sbuf("E_msq", [128, 16], F32)
        var = P.sbuf("E_var", [128, 16], F32)
        rstd = P.sbuf("E_rstd", [128, 16], F32)
        nb = P.sbuf("E_nb", [128, 16], F32)
        cat = [P.sbuf(f"E_cat{i}", [128, 8, TB], BF16) for i in range(1)] * 2
        catd = [P.dsem() for _ in range(1)] * 2
        psA = [P.psum(f"E_psA{i}", [128, 512], F32) for i in range(4)]
        psS = [P.psum(f"E_psS{i}", [128, 512], F32) for i in range(2)]
        psTB = P.psum("E_psTB", [128, 1024], BF16)
        psKV = P.psum("E_psKV", [128, 512], F32)
        psO = psS
        G.sbuf_left_E = P.nc.sbuf_bytes_remaining

        def loads(b):
            t0 = b * TB
            s = t0 // S
            ts = t0 - s * S
            i = b % 2
            P.dma("sp", cs[i][:], G.cosT.t[s, :, ts:ts + TB], csd[i], reads=[G.cosT], writes=[cs[i]])
            P.dma("sp", sn[i][:], G.sinT.t[s, :, ts:ts + TB], snd[i], reads=[G.sinT], writes=[sn[i]])
            for j in range(NJ):
                P.dma("sp", hb[j][:], hin.t[t0 + j * 128:t0 + (j + 1) * 128, :], hds[j], reads=[hin_b[b]], writes=[hb[j]])

        st_ = {"na": 0, "gch": 0}

        def nextA():
            X = psA[st_["na"] % 4]
            st_["na"] += 1
            return X

        def partA(b):
            i = b % 2
            for c0_ in (0, 4):
                for c in range(c0_, c0_ + 4):
                    pt = nextA()
                    for j in range(NJ):
                        P.op("pe", lambda e, pt=pt, j=j, c=c: e.transpose(out=pt[:, j * 128:(j + 1) * 128], in_=hb[j][:, c * 128:(c + 1) * 128],
                                                                          identity=identF[:]), reads=[hb[j], identF], writes=[pt])
                    if c % 2 == 0:
                        P.op("dve", lambda e, pt=pt, c=c: e.tensor_copy(out=hT.t[:, c, :], in_=pt[:, :]), reads=[pt], writes=[hT_k[c]])
                    else:
                        P.op("act", lambda e, pt=pt, c=c: e.activation(out=hT.t[:, c, :], in_=pt[:, :], func=AF.Copy), reads=[pt], writes=[hT_k[c]])
                yield
            nr = 0
            for which, col, colsw in (("q", 0, 2560), ("k", 512, 3072)):
                for hd in range(NH):
                    X, Y = nextA(), nextA()
                    for k in range(8):
                        P.op("pe", lambda e, X=X, k=k, c0=col + hd * 128: e.matmul(X[:, :], lhsT=wi.t[:, k, c0:c0 + 128], rhs=hT.t[:, k, :],
                                                                                  start=(k == 0), stop=(k == 7)), reads=[wi_k[k], hT_k[k]], writes=[X])
                    for k in range(8):
                        P.op("pe", lambda e, Y=Y, k=k, c0=colsw + hd * 128: e.matmul(Y[:, :], lhsT=wi.t[:, k, c0:c0 + 128], rhs=hT.t[:, k, :],
                                                                                    start=(k == 0), stop=(k == 7)), reads=[wi_k[k], hT_k[k]], writes=[Y])
                    a, bb = t1[nr % 2], t2[nr % 2]
                    nr += 1
                    rope_head(P, X, Y, cs[i], sn[i], a, bb)
                    if which == "q":
                        P.op("act", lambda e, a=a, hd=hd: e.activation(out=qr[:, hd, :], in_=a[:], func=AF.Copy), reads=[a], writes=[qr])
                        P.op("pool", lambda e, a=a, hd=hd: e.tensor_tensor(out=qd[:, hd, :], in0=a[:], in1=qdec[:, hd * TB:(hd + 1) * TB], op=ALU.mult),
                             reads=[a, ct], writes=[qd])
                    else:
                        P.op("act", lambda e, a=a, hd=hd: e.activation(out=kr[:, hd, :], in_=a[:], func=AF.Copy), reads=[a], writes=[kr])
                    yield
            if b + 1 < NB:
                loads(b + 1)

        def partB(b):
            t0 = b * TB
            i = b % 2
            first = (t0 % S == 0)
            if first:
                P.op("pool", lambda e: e.memset(state[:], 0.0), writes=[state])
                sb0 = sbf[st_["gch"] % 6]
                P.op("pool", lambda e, sb0=sb0: e.memset(sb0[:], 0.0), writes=[sb0])
                P.op("pool", lambda e: e.memset(xh[:, :, 0:16], 0.0), writes=[xh])
            for j in range(NJ):
                X = nextA()
                for k in range(8):
                    P.op("pe", lambda e, X=X, k=k, j=j: e.matmul(X[:, :], lhsT=hT.t[:, k, j * 128:(j + 1) * 128], rhs=wi.t[:, k, 1024:1536],
                                                                 start=(k == 0), stop=(k == 7)), reads=[wi_k[k], hT_k[k]], writes=[X])
                P.op("act", lambda e, X=X, j=j: e.activation(out=vb[:, j, :], in_=X[:, :], func=AF.Copy), reads=[X], writes=[vb])
                P.op("dve", lambda e, X=X, j=j: e.tensor_tensor(out=vdb[:, j, :], in0=X[:, :], in1=kdec, op=ALU.mult), reads=[X, ct], writes=[vdb])
                X = nextA()
                for k in range(8):
                    P.op("pe", lambda e, X=X, k=k, j=j: e.matmul(X[:, :], lhsT=hT.t[:, k, j * 128:(j + 1) * 128], rhs=wi.t[:, k, 1536:2048],
                                                                 start=(k == 0), stop=(k == 7)), reads=[wi_k[k], hT_k[k]], writes=[X])
                P.op("act", lambda e, X=X, j=j: e.activation(out=gsg[:, j, :], in_=X[:, :], func=AF.Silu), reads=[X], writes=[gsg])
                P.op("pool", lambda e, j=j: e.tensor_tensor(out=gsg[:, j, :], in0=gsg[:, j, :], in1=gain[:], op=ALU.mult), reads=[gsg, gain], writes=[gsg])
            for j in range(NJ):
                gi, w = j, POOL_W[j]
                X = nextA()
                for k in range(8):
                    P.op("pe", lambda e, X=X, k=k, c0=2048 + gi * 128: e.matmul(X[:, :], lhsT=wi.t[:, k, c0:c0 + 128], rhs=hT.t[:, k, :],
                                                                               start=(k == 0), stop=(k == 7)), reads=[wi_k[k], hT_k[k]], writes=[X])
                P.op("act", lambda e, X=X, gi=gi: e.activation(out=xh[:, gi, 16:16 + TB], in_=X[:, :], func=AF.Copy), reads=[X], writes=[xh])
                cur, curBuf = xh.t[:, gi, :], xh
                sh = 1
                tgl = [sa, sb]
                ti = 0
                while sh < w:
                    last = (sh * 2 >= w)
                    dst = tgl[ti % 2]
                    dst_ap = ws.t[:, gi, :] if last else dst.t
                    dstBuf = ws if last else dst
                    lo = 2 * sh - 1
                    P.op("pool", lambda e, dst_ap=dst_ap, cur=cur, sh=sh, lo=lo: e.tensor_tensor(out=dst_ap[:, lo:16 + TB], in0=cur[:, lo:16 + TB],
                                                                                               in1=cur[:, lo - sh:16 + TB - sh], op=ALU.add),
                         reads=[curBuf], writes=[dstBuf])
                    cur, curBuf = dst_ap, dstBuf
                    sh *= 2
                    ti += 1
                def pool_fin(cur=cur, curBuf=curBuf, gi=gi, w=w, ti=ti, tgl=tgl, first=first):
                    P.op("dve", lambda e, cur=cur, gi=gi, w=w: e.scalar_tensor_tensor(out=pooled[:, gi, :], in0=cur[:, 16:16 + TB], scalar=float(1.0 / w),
                                                                                     in1=xh[:, gi, 16:16 + TB], op0=ALU.mult, op1=ALU.subtract),
                         reads=[curBuf, xh], writes=[pooled])
                    if first:
                        tmp = tgl[ti % 2]
                        P.op("pool", lambda e, cur=cur, gi=gi, tmp=tmp: e.tensor_tensor(out=tmp[:, 0:16], in0=cur[:, 16:32], in1=pcnt[:, gi * 16:(gi + 1) * 16], op=ALU.mult),
                             reads=[curBuf, ct], writes=[tmp])
                        P.op("pool", lambda e, gi=gi, tmp=tmp: e.tensor_tensor(out=pooled[:, gi, 0:16], in0=tmp[:, 0:16], in1=xh[:, gi, 16:32], op=ALU.subtract),
                             reads=[tmp, xh, pooled], writes=[pooled])
                def pool_mm(gi=gi, i=i, pool_fin=pool_fin):
                    pool_fin()
                    X = nextA()
                    P.op("pe", lambda e, X=X, gi=gi: e.matmul(X[:, :], lhsT=pw[:, gi, :], rhs=pooled[:, gi, :], start=True, stop=True),
                         reads=[pw, pooled], writes=[X])
                    P.op("act", lambda e, X=X, gi=gi, i=i: e.activation(out=cat[i][:, 4 + gi, :], in_=X[:, :], func=AF.Identity, scale=psc[:, gi:gi + 1]),
                         reads=[X, psc], writes=[cat[i]])
                st_.setdefault("pool_q", []).append(pool_mm)
            for j in range(NJ):
                Sb = psS[j % 2]
                for hd in range(NH):
                    P.op("pe", lambda e, Sb=Sb, hd=hd, j=j: e.matmul(Sb[:, hd * 128:(hd + 1) * 128], lhsT=kr[:, hd, j * 128:(j + 1) * 128],
                                                                     rhs=qr[:, hd, j * 128:(j + 1) * 128], start=True, stop=True),
                         reads=[kr, qr], writes=[Sb])
                pt = PT[j % 2]
                P.op("dve", lambda e, Sb=Sb, pt=pt: e.tensor_tensor(out=pt[:], in0=Sb[:, :], in1=dec, op=ALU.mult), reads=[Sb, ct], writes=[pt])
                for hd in range(NH):
                    P.op("pe", lambda e, hd=hd, j=j: e.transpose(out=psTB[:, (j % 2) * 512 + hd * 128:(j % 2) * 512 + (hd + 1) * 128],
                                                                 in_=kr[:, hd, j * 128:(j + 1) * 128], identity=ident_bf[:]),
                         reads=[kr, ident_bf], writes=[psTB])
                kt = ktok[j % 2]
                P.op("act", lambda e, kt=kt, j=j: e.activation(out=kt[:], in_=psTB[:, (j % 2) * 512:(j % 2 + 1) * 512], func=AF.Copy),
                     reads=[psTB], writes=[kt])
                for hd in range(NH):
                    P.op("pe", lambda e, kt=kt, hd=hd, j=j: e.matmul(psKV[:, hd * 128:(hd + 1) * 128], lhsT=kt[:, hd * 128:(hd + 1) * 128],
                                                                     rhs=vdb[:, j, hd * 128:(hd + 1) * 128], start=True, stop=True),
                         reads=[kt, vdb], writes=[psKV])
                O = psO[j % 2]
                sbc = sbf[st_["gch"] % 6]
                for hd in range(NH):
                    P.op("pe", lambda e, O=O, pt=pt, hd=hd, j=j: e.matmul(O[:, hd * 128:(hd + 1) * 128], lhsT=pt[:, hd * 128:(hd + 1) * 128],
                                                                          rhs=vb[:, j, hd * 128:(hd + 1) * 128], start=True, stop=False),
                         reads=[pt, vb], writes=[O])
                    P.op("pe", lambda e, O=O, sbc=sbc, hd=hd, j=j: e.matmul(O[:, hd * 128:(hd + 1) * 128], lhsT=qd[:, hd, j * 128:(j + 1) * 128],
                                                                            rhs=sbc[:, hd * 128:(hd + 1) * 128], start=False, stop=True),
                         reads=[qd, sbc], writes=[O])
                P.op("act", lambda e, O=O, j=j: e.activation(out=osb[:, j, :], in_=O[:, :], func=AF.Copy), reads=[O], writes=[osb])
                sbn = sbf[(st_["gch"] + 1) % 6]
                for hd in range(NH):
                    P.op("dve", lambda e, hd=hd: e.scalar_tensor_tensor(out=state[:, hd * 128:(hd + 1) * 128], in0=state[:, hd * 128:(hd + 1) * 128],
                                                                       scalar=CDEC[hd], in1=psKV[:, hd * 128:(hd + 1) * 128], op0=ALU.mult, op1=ALU.add),
                         reads=[state, psKV], writes=[state])
                P.op("act", lambda e, sbn=sbn: e.activation(out=sbn[:], in_=state[:], func=AF.Copy), reads=[state], writes=[sbn])
                st_["gch"] += 1
            P.op("act", lambda e: e.activation(out=xh[:, :, 0:16], in_=xh[:, :, TB:TB + 16], func=AF.Copy), reads=[xh], writes=[xh])

        def partT(b):
            t0 = b * TB
            i = b % 2
            o3 = osb.t[:].rearrange("p j (h e) -> p (j h) e", h=NH)
            q3 = sq.t[:].rearrange("p j (h e) -> p (j h) e", h=NH)
            P.op("dve", lambda e: e.tensor_reduce(out=s1[:], in_=o3, axis=AX.X, op=ALU.add), reads=[osb], writes=[s1])
            P.op("pool", lambda e: e.tensor_tensor(out=sq[:], in0=osb[:], in1=osb[:], op=ALU.mult), reads=[osb], writes=[sq])
            yield
            for f_ in st_.get("pool_q", []):
                f_()
                yield
            st_["pool_q"] = []
            P.op("dve", lambda e: e.tensor_reduce(out=s2[:], in_=q3, axis=AX.X, op=ALU.add), reads=[sq], writes=[s2])
            P.op("dve", lambda e: e.tensor_scalar(out=mean[:], in0=s1[:], scalar1=1.0 / DH, scalar2=None, op0=ALU.mult), reads=[s1], writes=[mean])
            P.op("dve", lambda e: e.tensor_tensor(out=msq[:], in0=mean[:], in1=mean[:], op=ALU.mult), reads=[mean], writes=[msq])
            P.op("dve", lambda e: e.scalar_tensor_tensor(out=var[:], in0=s2[:], scalar=1.0 / DH, in1=msq[:], op0=ALU.mult, op1=ALU.subtract),
                 reads=[s2, msq], writes=[var])
            P.op("dve", lambda e: e.tensor_scalar_add(out=var[:], in0=var[:], scalar1=EPS), reads=[var], writes=[var])
            P.op("dve", lambda e: e.tensor_scalar(out=msq[:], in0=mean[:], scalar1=-1.0, scalar2=None, op0=ALU.mult), reads=[mean], writes=[msq])
            P.op("pool", lambda e: e.tensor_tensor(out=rstd[:], in0=var[:], in1=nh[:], op=ALU.pow), reads=[var, nh], writes=[rstd])
            P.op("pool", lambda e: e.tensor_tensor(out=nb[:], in0=msq[:], in1=rstd[:], op=ALU.mult), reads=[msq, rstd], writes=[nb])
            yield
            P.op("pool", lambda e: e.tensor_tensor(out=q3, in0=o3, in1=rstd[:].unsqueeze(2).to_broadcast([128, 16, DH]), op=ALU.mult),
                 reads=[osb, rstd], writes=[sq])
            yield
            P.op("pool", lambda e: e.tensor_tensor(out=q3, in0=q3, in1=nb[:].unsqueeze(2).to_broadcast([128, 16, DH]), op=ALU.add),
                 reads=[sq, nb], writes=[sq])
            yield
            P.op("dve", lambda e: e.tensor_tensor(out=sq[:], in0=sq[:], in1=gsg[:], op=ALU.mult), reads=[sq, gsg], writes=[sq])
            for _ in range(12):
                yield
            for j in range(NJ):
                X = psS[j % 2]
                for hd in range(NH):
                    P.op("pe", lambda e, X=X, hd=hd, j=j: e.transpose(out=X[:, hd * 128:(hd + 1) * 128], in_=sq[:, j, hd * 128:(hd + 1) * 128],
                                                                      identity=identF[:]), reads=[sq, identF], writes=[X])
                outap = cat[i][:, 0:4, j * 128:(j + 1) * 128]
                inap = X[:, :].rearrange("p (h t) -> p h t", h=NH)
                if j % 2 == 0:
                    P.op("dve", lambda e, outap=outap, inap=inap: e.tensor_copy(out=outap, in_=inap), reads=[X], writes=[cat[i]])
                else:
                    P.op("act", lambda e, outap=outap, inap=inap: e.activation(out=outap, in_=inap, func=AF.Copy), reads=[X], writes=[cat[i]])
                yield
            P.dma("sp", G.catT.t[:, :, t0:t0 + TB].rearrange("c p t -> p c t"), cat[i][:], catd[i], reads=[cat[i]], writes=[G.catT_b[b]])

        def interleave(*gens):
            gens = [g for g in gens if g is not None]
            while gens:
                for g in list(gens):
                    try:
                        next(g)
                    except StopIteration:
                        gens.remove(g)

        loads(0)
        interleave(partA(0))
        for b in range(NB):
            partB(b)
            interleave(partT(b), partA(b + 1) if b + 1 < NB else None)
        P.end_phase()
    P.scope = P.es


def build_program(NS, S, layers, debug=False):
    nc = bass.Bass("TRN2", target_bir_lowering=False)
    P = Prog(nc)
    G = Ctx()
    G.NS, G.S = NS, S
    G.NT = NS * S
    G.NB = G.NT // TB
    NT, NB = G.NT, G.NB
    ext = lambda name, shape, dt=F32: P.dram(name, shape, dt, kind="ExternalInput")
    G.x = ext("x", [NT, D])
    G.pos = ext("pos", [NS, S], I32)
    G.consts = ext("consts", [128, C_END])
    G.ev_w_in = ext("ev_w_in", [2, D, 3584])
    G.ev_ret_norm_g = ext("ev_ret_norm_g", [2, 512])
    G.ev_pool_w = ext("ev_pool_w", [2, 4, 128, 128])
    G.ev_pool_scale = ext("ev_pool_scale", [2, 128, 4])
    G.ev_w_out = ext("ev_w_out", [2, D, D])
    G.od_w_in = ext("od_w_in", [2, D, 3584])
    G.od_small = ext("od_small", [2, 128, 36])
    G.od_gate_a_w = ext("od_gate_a_w", [2, 8, 64, 64])
    G.od_gate_x_w = ext("od_gate_x_w", [2, 8, 64, 64])
    G.od_w_out = ext("od_w_out", [2, D, D])
    G.ffn_w_in = ext("ffn_w_in", [DEPTH, D, 2 * DFF])
    G.ffn_w_out = ext("ffn_w_out", [DEPTH, DFF, D])
    G.ln_g = ext("ln_g", [DEPTH, 2, D])
    G.ln_b = ext("ln_b", [DEPTH, 2, D])
    G.out = P.dram("out", [NT, D], F32, kind="ExternalOutput")
    dk = "ExternalOutput" if debug else "Internal"
    G.hA = P.dram("hA", [NT, D], F32, kind=dk)
    G.hB = P.dram("hB", [NT, D], F32, kind=dk)
    G.catT = P.dram("catT", [8, 128, NT], BF16, kind=dk)
    G.cosT = P.dram("cosT", [NS, 128, S], F32, kind=dk)
    G.sinT = P.dram("sinT", [NS, 128, S], F32, kind=dk)
    G.qT = P.dram("qT", [NH, 128, NT], BF16, kind=dk)
    G.kT = P.dram("kT", [NH, 128, NT], BF16, kind=dk)
    G.vd = P.dram("vd", [NT, 512], BF16, kind=dk)
    G.x_b = blk_bufs("x_b", G.x.t, NB)
    G.hA_b = blk_bufs("hA_b", G.hA.t, NB)
    G.hB_b = blk_bufs("hB_b", G.hB.t, NB)
    G.out_b = blk_bufs("out_b", G.out.t, NB)
    G.catT_b = blk_bufs("catT_b", G.catT.t, NB)
    G.qkv_b = blk_bufs("qkv_b", G.qT.t, NB)

    phase_R(P, G)
    fin = []
    hin, hin_b = G.x, G.x_b
    for li, layer in enumerate(layers):
        l2 = layer // 2
        last = (li == len(layers) - 1)
        if layer % 2 == 0:
            phase_E(P, G, l2, hin, hin_b)
            w_out = G.ev_w_out
        else:
            phase_O1(P, G, l2, hin, hin_b)
            phase_O2(P, G)
            w_out = G.od_w_out
        phase_P(P, G, w_out.t[l2], w_out, G.ln_g.t[layer, 0:1, :], G.ln_b.t[layer, 0:1, :], G.ln_g, hin, hin_b, G.hB, G.hB_b)
        hout, hout_b = (G.out, G.out_b) if last else (G.hA, G.hA_b)
        fin = phase_F(P, G, G.ffn_w_in.t[layer], G.ffn_w_in, G.ffn_w_out.t[layer], G.ffn_w_out,
                      G.ln_g.t[layer, 1:2, :], G.ln_b.t[layer, 1:2, :], G.ln_g, G.hB, G.hB_b, hout, hout_b)
        hin, hin_b = G.hA, G.hA_b
    P.finish(fin)
    return nc, P


def host_prep(inp):
    f = lambda a: np.ascontiguousarray(np.asarray(a, dtype=np.float32))
    swap = np.concatenate([np.arange(h * 128 + 64, h * 128 + 128).tolist() + np.arange(h * 128, h * 128 + 64).tolist() for h in range(NH)]).astype(np.int64)
    ev = f(inp["ev_w_in"])
    ev_ext = np.concatenate([ev, ev[:, :, 0:512][:, :, swap], ev[:, :, 512:1024][:, :, swap]], axis=2)
    od = f(inp["od_w_in"])
    od_ext = np.concatenate([od, od[:, :, 1024:1536][:, :, swap], od[:, :, 1536:2048][:, :, swap]], axis=2)
    psc = f(inp["ev_pool_scale"]).reshape(2, 4, 128).transpose(0, 2, 1)
    cw = f(inp["od_conv_w"])
    cols = [cw[:, k, :] for k in range(4)] + [f(inp["od_conv_b"]), f(inp["od_gate_a_b"]), f(inp["od_gate_x_b"]), f(inp["od_lru_lambda"]),
                                              f(inp["od_conv_b"])]
    sm = np.stack(cols, axis=-1)
    sm = sm.reshape(2, 4, 128, 9).transpose(0, 2, 1, 3).reshape(2, 128, 36)
    shared = {
        "consts": make_consts(),
        "ev_w_in": np.ascontiguousarray(ev_ext), "ev_ret_norm_g": f(inp["ev_ret_norm_g"]), "ev_pool_w": f(inp["ev_pool_w"]),
        "ev_pool_scale": np.ascontiguousarray(psc), "ev_w_out": f(inp["ev_w_out"]),
        "od_w_in": np.ascontiguousarray(od_ext), "od_small": np.ascontiguousarray(sm),
        "od_gate_a_w": f(inp["od_gate_a_w"]), "od_gate_x_w": f(inp["od_gate_x_w"]), "od_w_out": f(inp["od_w_out"]),
        "ffn_w_in": f(inp["ffn_w_in"]), "ffn_w_out": f(inp["ffn_w_out"]), "ln_g": f(inp["ln_g"]), "ln_b": f(inp["ln_b"]),
    }
    return shared


def kernel(**inp):
    x = np.asarray(inp["x"], dtype=np.float32)
    pos = np.asarray(inp["positions"], dtype=np.int32)
    B, S, _ = x.shape
    ncores = 8
    NS = B // ncores
    shared = host_prep(inp)
    nc, P = build_program(NS, S, list(range(DEPTH)))
    in_maps = []
    for c in range(ncores):
        m = dict(shared)
        m["x"] = np.ascontiguousarray(x[c * NS:(c + 1) * NS].reshape(NS * S, D))
        m["pos"] = np.ascontiguousarray(pos[c * NS:(c + 1) * NS])
        in_maps.append(m)
    res = run_bass_kernel_spmd(nc, in_maps, core_ids=list(range(ncores)))
    out = np.concatenate([np.asarray(r["out"], dtype=np.float32).reshape(NS, S, D) for r in res.results], axis=0)
    return out


def phase_O1(P, G, l2, hin, hin_b):
    S, NS, NB = G.S, G.NS, G.NB
    WC = 3584
    with contextlib.ExitStack() as sc:
        P.scope = sc
        wi = P.sbuf("O_wi", [128, 8, WC], BF16)
        wi_k = [Buf(f"O_wi{k}", wi.t) for k in range(8)]
        bulk_load(P, [(G.od_w_in.t[l2, k * 128:(k + 1) * 128, q * 1792:(q + 1) * 1792], G.od_w_in,
                       wi.t[:, k, q * 1792:(q + 1) * 1792], wi_k[k], 1792) for k in range(8) for q in range(2)], 1792)
        gst = P.sbuf("O_gst", [128, 4, 128], F32)
        gsd = P.dsem()
        ga = P.sbuf("O_ga", [128, 4, 128], BF16)
        gx = P.sbuf("O_gx", [128, 4, 128], BF16)
        P.op("pool", lambda e: e.memset(gst[:], 0.0), writes=[gst])
        for wsrc, dst in ((G.od_gate_a_w, ga), (G.od_gate_x_w, gx)):
            v4 = wsrc.t[l2].rearrange("(c two) i d -> two i c d", two=2)
            P.dma("sp", gst[0:64, :, 0:64], v4[0], gsd, reads=[wsrc], writes=[gst])
            P.dma("sp", gst[64:128, :, 64:128], v4[1], gsd, reads=[wsrc], writes=[gst])
            P.op("dve", lambda e, dst=dst: e.tensor_copy(out=dst[:], in_=gst[:]), reads=[gst], writes=[dst])
        spar = P.sbuf("O_spar", [128, 36], F32)
        P.dma("sp", spar[:], G.od_small.t[l2], P.dsem(), reads=[G.od_small], writes=[spar])
        sp3 = spar.t[:].rearrange("p (c k) -> p c k", k=9)
        sm_ = {n: P.sbuf(f"O_{n}", [128, 4], F32) for n in ("ex", "den", "z", "z2", "p1", "zp", "c8", "c16")}
        ex, den, z, z2, p1, zp, c8, c16 = (sm_[n] for n in ("ex", "den", "z", "z2", "p1", "zp", "c8", "c16"))
        P.op("act", lambda e: e.activation(out=ex[:], in_=sp3[:, :, 7], func=AF.Exp, scale=-1.0), reads=[spar], writes=[ex])
        P.op("dve", lambda e: e.tensor_scalar_add(out=den[:], in0=ex[:], scalar1=2.0), reads=[ex], writes=[den])
        P.op("dve", lambda e: e.reciprocal(out=den[:], in_=den[:]), reads=[den], writes=[den])
        P.op("dve", lambda e: e.tensor_tensor(out=z[:], in0=ex[:], in1=den[:], op=ALU.mult), reads=[ex, den], writes=[z])
        P.op("dve", lambda e: e.tensor_tensor(out=z2[:], in0=z[:], in1=z[:], op=ALU.mult), reads=[z], writes=[z2])
        P.op("dve", lambda e: e.tensor_scalar(out=p1[:], in0=z2[:], scalar1=1.0 / 7.0, scalar2=1.0 / 5.0, op0=ALU.mult, op1=ALU.add), reads=[z2], writes=[p1])
        P.op("dve", lambda e: e.tensor_tensor(out=p1[:], in0=p1[:], in1=z2[:], op=ALU.mult), reads=[p1, z2], writes=[p1])
        P.op("dve", lambda e: e.tensor_scalar_add(out=p1[:], in0=p1[:], scalar1=1.0 / 3.0), reads=[p1], writes=[p1])
        P.op("dve", lambda e: e.tensor_tensor(out=p1[:], in0=p1[:], in1=z2[:], op=ALU.mult), reads=[p1, z2], writes=[p1])
        P.op("dve", lambda e: e.scalar_tensor_tensor(out=zp[:], in0=p1[:], scalar=1.0, in1=z[:], op0=ALU.add, op1=ALU.mult), reads=[p1, z], writes=[zp])
        P.op("dve", lambda e: e.tensor_scalar(out=c8[:], in0=zp[:], scalar1=-16.0, scalar2=None, op0=ALU.mult), reads=[zp], writes=[c8])
        P.op("dve", lambda e: e.tensor_scalar(out=c16[:], in0=zp[:], scalar1=-32.0, scalar2=None, op0=ALU.mult), reads=[zp], writes=[c16])
        ident = P.sbuf("O_id", [128, 128], F32)
        P.dma("sp", ident[:], G.consts.t[:, C_ID:C_ID + 128], P.dsem(), reads=[G.consts], writes=[ident])
        half = P.sbuf("O_half", [128, TB], F32)
        P.op("pool", lambda e: e.memset(half[:], 0.5), writes=[half])

        cs = P.sbuf("O_cs", [128, TB], F32)
        sn = P.sbuf("O_sn", [128, TB], F32)
        csd = P.dsem()
        snd = P.dsem()
        hb = [P.sbuf(f"O_h{j}", [128, D], F32) for j in range(NJ)]
        hds = [P.dsem() for _ in range(NJ)]
        hT = P.sbuf("O_hT", [128, 8, TB], BF16)
        hT_k = [Buf(f"O_hT{k}", hT.t) for k in range(8)]
        gl = P.sbuf("O_gl", [128, 4, TB], F32)
        uh = P.sbuf("O_uh", [128, 4, 3 + TB], F32)
        uc = P.sbuf("O_uc", [128, 4, TB], F32)
        ucb = P.sbuf("O_ucb", [128, 4, TB], BF16)
        rr = P.sbuf("O_rr", [128, 4, TB], F32)
        ii = P.sbuf("O_ii", [128, 4, TB], F32)
        aa = [P.sbuf(f"O_aa{i}", [128, TB], F32) for i in range(2)]
        w1 = [P.sbuf(f"O_w1{i}", [128, TB], F32) for i in range(2)]
        hs = [P.sbuf(f"O_hs{i}", [128, TB], F32) for i in range(2)]
        carry = P.sbuf("O_carry", [128, 4], F32)
        catl = P.sbuf("O_catl", [128, 4, TB], BF16)
        catd = P.dsem()
        qr = P.sbuf("O_qr", [128, NH, TB], BF16)
        kr = P.sbuf("O_kr", [128, NH, TB], BF16)
        qd_, kd_ = P.dsem(), P.dsem()
        t1s = [P.sbuf(f"O_t1{i}", [128, TB], F32) for i in range(2)]
        t2s = [P.sbuf(f"O_t2{i}", [128, TB], F32) for i in range(2)]
        vbt = P.sbuf("O_vbt", [128, NJ, 512], BF16)
        vds = P.dsem()
        psA = [P.psum(f"O_psA{i}", [128, 512], F32) for i in range(4)]
        psR = [P.psum(f"O_psR{i}", [128, 512], F32) for i in range(2)]
        psI = [P.psum(f"O_psI{i}", [128, 512], F32) for i in range(2)]

        def loads(b):
            t0 = b * TB
            s = t0 // S
            ts = t0 - s * S
            P.dma("sp", cs[:], G.cosT.t[s, :, ts:ts + TB], csd, reads=[G.cosT], writes=[cs])
            P.dma("sp", sn[:], G.sinT.t[s, :, ts:ts + TB], snd, reads=[G.sinT], writes=[sn])
            for j in range(NJ):
                P.dma("sp", hb[j][:], hin.t[t0 + j * 128:t0 + (j + 1) * 128, :], hds[j], reads=[hin_b[b]], writes=[hb[j]])

        def proj(X, c0, tokmajor=False, j=0):
            for k in range(8):
                if tokmajor:
                    P.op("pe", lambda e, k=k: e.matmul(X[:, :], lhsT=hT.t[:, k, j * 128:(j + 1) * 128], rhs=wi.t[:, k, c0:c0 + 512],
                                                       start=(k == 0), stop=(k == 7)), reads=[wi_k[k], hT_k[k]], writes=[X])
                else:
                    P.op("pe", lambda e, k=k: e.matmul(X[:, :], lhsT=wi.t[:, k, c0:c0 + 128], rhs=hT.t[:, k, :],
                                                       start=(k == 0), stop=(k == 7)), reads=[wi_k[k], hT_k[k]], writes=[X])

        st_ = {"na": 0}

        def nextA():
            X = psA[st_["na"] % 4]
            st_["na"] += 1
            return X

        def partA(b):
            t0 = b * TB
            for c0_ in (0, 4):
                for c in range(c0_, c0_ + 4):
                    pt = nextA()
                    for j in range(NJ):
                        P.op("pe", lambda e, pt=pt, j=j, c=c: e.transpose(out=pt[:, j * 128:(j + 1) * 128], in_=hb[j][:, c * 128:(c + 1) * 128],
                                                                          identity=ident[:]), reads=[hb[j], ident], writes=[pt])
                    if c % 2 == 0:
                        P.op("dve", lambda e, pt=pt, c=c: e.tensor_copy(out=hT.t[:, c, :], in_=pt[:, :]), reads=[pt], writes=[hT_k[c]])
                    else:
                        P.op("act", lambda e, pt=pt, c=c: e.activation(out=hT.t[:, c, :], in_=pt[:, :], func=AF.Copy), reads=[pt], writes=[hT_k[c]])
                yield
            for which, col, colsw, dstb in (("q", 1024, 2560, qr), ("k", 1536, 3072, kr)):
                for hd in range(NH):
                    X, Y = nextA(), nextA()
                    proj(X, col + hd * 128)
                    proj(Y, colsw + hd * 128)
                    t1, t2 = t1s[hd % 2], t2s[hd % 2]
                    rope_head(P, X, Y, cs, sn, t1, t2)
                    P.op("act", lambda e, hd=hd, dstb=dstb, t1=t1: e.activation(out=dstb[:, hd, :], in_=t1[:], func=AF.Copy), reads=[t1], writes=[dstb])
                    yield
            P.dma("sp", G.qT.t[:, :, t0:t0 + TB].rearrange("h p t -> p h t"), qr[:], qd_, reads=[qr], writes=[G.qkv_b[b]])
            P.dma("sp", G.kT.t[:, :, t0:t0 + TB].rearrange("h p t -> p h t"), kr[:], kd_, reads=[kr], writes=[G.qkv_b[b]])
            if b + 1 < NB:
                loads(b + 1)

        def partB(b):
            t0 = b * TB
            first = (t0 % S == 0)
            if first:
                P.op("pool", lambda e: e.memset(uh[:, :, 0:3], 0.0), writes=[uh])
                P.op("pool", lambda e: e.memset(carry[:], 0.0), writes=[carry])
            for c in range(4):
                X = nextA()
                proj(X, 512 + c * 128)
                P.op("act", lambda e, X=X, c=c: e.activation(out=uh[:, c, 3:3 + TB], in_=X[:, :], func=AF.Copy), reads=[X], writes=[uh])
            for c in range(4):
                X = nextA()
                proj(X, c * 128)
                P.op("act", lambda e, X=X, c=c: e.activation(out=gl[:, c, :], in_=X[:, :], func=AF.Gelu_apprx_tanh), reads=[X], writes=[gl])
                P.op("dve", lambda e, c=c: e.tensor_scalar(out=uc[:, c, :], in0=uh[:, c, 0:TB], scalar1=sp3[:, c, 0:1], scalar2=sp3[:, c, 4:5],
                                                          op0=ALU.mult, op1=ALU.add), reads=[uh, spar], writes=[uc])
                for k in range(1, 4):
                    P.op("dve", lambda e, c=c, k=k: e.scalar_tensor_tensor(out=uc[:, c, :], in0=uh[:, c, k:k + TB], scalar=sp3[:, c, k:k + 1],
                                                                          in1=uc[:, c, :], op0=ALU.mult, op1=ALU.add), reads=[uh, spar, uc], writes=[uc])
            P.op("pool", lambda e: e.tensor_copy(out=ucb[:], in_=uc[:]), reads=[uc], writes=[ucb])
            P.op("act", lambda e: e.activation(out=uh[:, :, 0:3], in_=uh[:, :, TB:TB + 3], func=AF.Copy), reads=[uh], writes=[uh])
            for j in range(NJ):
                X = nextA()
                proj(X, 2048, tokmajor=True, j=j)
                P.op("act", lambda e, X=X, j=j: e.activation(out=vbt[:, j, :], in_=X[:, :], func=AF.Copy), reads=[X], writes=[vbt])
            P.dma("sp", G.vd.t[t0:t0 + TB, :].rearrange("(j p) f -> p j f", p=128), vbt[:], vds, reads=[vbt], writes=[G.qkv_b[b]])
            for c in range(4):
                R_, I_ = psR[c % 2], psI[c % 2]
                P.op("pe", lambda e, R_=R_, c=c: e.matmul(R_[:, :], lhsT=ga[:, c, :], rhs=ucb[:, c, :], start=True, stop=True), reads=[ga, ucb], writes=[R_])
                P.op("pe", lambda e, I_=I_, c=c: e.matmul(I_[:, :], lhsT=gx[:, c, :], rhs=ucb[:, c, :], start=True, stop=True), reads=[gx, ucb], writes=[I_])
                P.op("act", lambda e, R_=R_, c=c: e.activation(out=rr[:, c, :], in_=R_[:, :], func=AF.Sigmoid, bias=sp3[:, c, 5:6]), reads=[R_, spar], writes=[rr])
                P.op("act", lambda e, I_=I_, c=c: e.activation(out=ii[:, c, :], in_=I_[:, :], func=AF.Sigmoid, bias=sp3[:, c, 6:7]), reads=[I_, spar], writes=[ii])

        def partT(b):
            t0 = b * TB
            for c in range(4):
                a_, w_, h_ = aa[c % 2], w1[c % 2], hs[c % 2]
                P.op("act", lambda e, a_=a_, c=c: e.activation(out=a_[:], in_=rr[:, c, :], func=AF.Exp, scale=c8[:, c:c + 1]), reads=[rr, c8], writes=[a_])
                P.op("act", lambda e, w_=w_, c=c: e.activation(out=w_[:], in_=rr[:, c, :], func=AF.Exp, scale=c16[:, c:c + 1]), reads=[rr, c16], writes=[w_])
                P.op("act", lambda e, w_=w_: e.activation(out=w_[:], in_=w_[:], func=AF.Ln, scale=-1.0, bias=1.0), reads=[w_], writes=[w_])
                P.op("act", lambda e, w_=w_: e.activation(out=w_[:], in_=w_[:], func=AF.Exp, scale=0.5), reads=[w_], writes=[w_])
                P.op("pool", lambda e, c=c: e.tensor_tensor(out=ii[:, c, :], in0=ii[:, c, :], in1=uc[:, c, :], op=ALU.mult), reads=[ii, uc], writes=[ii])
                P.op("dve", lambda e, w_=w_, c=c: e.tensor_tensor(out=w_[:], in0=w_[:], in1=ii[:, c, :], op=ALU.mult), reads=[w_, ii], writes=[w_])
                P.op("dve", lambda e, a_=a_, w_=w_, h_=h_, c=c: e.tensor_tensor_scan(out=h_[:], data0=a_[:], data1=w_[:], initial=carry[:, c:c + 1],
                                                                                   op0=ALU.mult, op1=ALU.add), reads=[a_, w_, carry], writes=[h_])
                P.op("act", lambda e, h_=h_, c=c: e.activation(out=carry[:, c:c + 1], in_=h_[:, TB - 1:TB], func=AF.Copy), reads=[h_], writes=[carry])
                P.op("pool", lambda e, h_=h_, c=c: e.tensor_tensor(out=catl[:, c, :], in0=h_[:], in1=gl[:, c, :], op=ALU.mult), reads=[h_, gl], writes=[catl])
                yield
                yield
            P.dma("sp", G.catT.t[0:4, :, t0:t0 + TB].rearrange("c p t -> p c t"), catl[:], catd, reads=[catl], writes=[G.catT_b[b]])

        def interleave(*gens):
            gens = [g for g in gens if g is not None]
            while gens:
                for g in list(gens):
                    try:
                        next(g)
                    except StopIteration:
                        gens.remove(g)

        loads(0)
        interleave(partA(0))
        for b in range(NB):
            partB(b)
            interleave(partT(b), partA(b + 1) if b + 1 < NB else None)
        P.end_phase()
    P.scope = P.es


def phase_O2(P, G):
    S, NS, NB = G.S, G.NS, G.NB
    BPS = S // TB
    SKEW = 3
    NPT = SKEW + 3
    scale = float(DH ** -0.5)
    with contextlib.ExitStack() as sc:
        P.scope = sc
        mk32 = P.sbuf("A_mk32", [128, 256], F32)
        P.dma("sp", mk32[:], G.consts.t[:, C_MASK:C_MASK + 256], P.dsem(), reads=[G.consts], writes=[mk32])
        mask = P.sbuf("A_mask", [128, 256], BF16)
        P.op("dve", lambda e: e.tensor_copy(out=mask[:], in_=mk32[:]), reads=[mk32], writes=[mask])
        ones = P.sbuf("A_ones", [128, 128], BF16)
        P.op("pool", lambda e: e.memset(ones[:], 1.0), writes=[ones])
        qT = [P.sbuf(f"A_qT{i}", [128, S], BF16) for i in range(2)]
        kT = [P.sbuf(f"A_kT{i}", [128, S], BF16) for i in range(2)]
        qkd = [P.dsem() for _ in range(2)]
        vs = [P.sbuf(f"A_vs{i}", [128, S // 128, 128], BF16) for i in range(2)]
        vsd = [P.dsem() for _ in range(2)]
        vpieces = {id(v): [Buf(f"A_vp{i}_{k}", v.t) for k in range(16)] for i, v in enumerate(vs)}
        nums = [P.sbuf(f"A_num{i}", [128, S], F32) for i in range(2)]
        dens = [P.sbuf(f"A_den{i}", [128, S], F32) for i in range(2)]
        rden = P.sbuf("A_rden", [128, S], F32)
        y = P.sbuf("A_y", [128, S], BF16)
        yl, yh = Buf("A_yl", y.t), Buf("A_yh", y.t)
        yd = P.dsem()
        ebf = [P.sbuf(f"A_eb{i}", [128, 256], BF16) for i in range(3)]
        PT = [P.sbuf(f"A_PT{i}", [128, 256], BF16) for i in range(NPT)]
        psS = [P.psum(f"A_psS{i}", [128, 512], F32) for i in range(3)]
        psN = [P.psum(f"A_psN{i}", [128, 512], F32) for i in range(2)]
        psD = [P.psum(f"A_psD{i}", [128, 512], F32) for i in range(2)]
        vcount = 0
        gcount = 0
        iters = [(s, hd) for s in range(NS) for hd in range(NH)]

        def qk_load(n):
            s, hd = iters[n]
            sblk = [G.qkv_b[s * BPS + bb] for bb in range(BPS)]
            i = n % 2
            P.dma("sp", qT[i][:], G.qT.t[hd, :, s * S:(s + 1) * S], qkd[i], reads=sblk, writes=[qT[i]])
            P.dma("sp", kT[i][:], G.kT.t[hd, :, s * S:(s + 1) * S], qkd[i], reads=sblk, writes=[kT[i]])

        def build_per(n):
            nonlocal vcount
            s, hd = iters[n]
            sblk = [G.qkv_b[s * BPS + bb] for bb in range(BPS)]
            per = []
            for pi, (W, dil) in enumerate(DIL_PATTERNS):
                nb = S // dil // 128
                v_ = vs[vcount % 2]
                vdm = vsd[vcount % 2]
                vcount += 1
                lds, bks = [], []
                src4 = G.vd.t[s * S:(s + 1) * S, hd * 128:(hd + 1) * 128].rearrange("(kb p r) e -> p r kb e", p=128, r=dil)
                v4 = v_.t[:].rearrange("p (r kb) e -> p r kb e", r=dil)
                npc = dil if dil <= nb else nb
                pcs = vpieces[id(v_)][:npc]
                if dil <= nb:
                    for r in range(dil):
                        lds.append(("load", pcs[r], vdm, v4[:, r, :, :], src4[:, r, :, :], sblk))
                else:
                    for kb in range(nb):
                        lds.append(("load", pcs[kb], vdm, v4[:, :, kb, :], src4[:, :, kb, :], sblk))
                for r in range(dil):
                    for kb in range(nb):
                        bks.append(("blk", pi, dil, nb, r, kb, v_, pcs))
                per.append((lds, bks))
            return per

        pers = [build_per(n) for n in range(len(iters))]
        qk_load(0)
        for n, (s, hd) in enumerate(iters):
            if True:
                sblk = [G.qkv_b[s * BPS + bb] for bb in range(BPS)]
                i = n % 2
                q_, k_ = qT[i], kT[i]
                num, den = nums[i], dens[i]
                if n + 1 < len(iters):
                    qk_load(n + 1)
                per = pers[n]
                tasks = list(per[0][0]) if n == 0 else []
                for pi in range(len(per)):
                    bks = per[pi][1]
                    tasks += bks[:5]
                    if pi + 1 < len(per):
                        tasks += per[pi + 1][0]
                    elif n + 1 < len(iters):
                        tasks += pers[n + 1][0][0]
                    tasks += bks[5:]
                blks = [t for t in tasks if t[0] == "blk"]
                pending = []
                ptmap = {}
                bi = 0

                def stage2(idx):
                    nonlocal gcount
                    _, pi, dil, nb, r, qb, v_, pcs = blks[idx]
                    gi = qb % 4
                    N, Dn = psN[gcount % 2], psD[gcount % 2]
                    reg = slice(gi * 128, (gi + 1) * 128)
                    ptc = ptmap[idx]
                    if qb >= 1:
                        ptp = ptmap[idx - 1]
                        for dst, lw, lwB in ((N, v_, None), (Dn, None, ones)):
                            l0 = v_[:, r * nb + qb - 1, :] if lwB is None else ones[:]
                            l1 = v_[:, r * nb + qb, :] if lwB is None else ones[:]
                            rb = pcs if lwB is None else [ones]
                            P.op("pe", lambda e, dst=dst, l0=l0, ptp=ptp: e.matmul(dst[:, reg], lhsT=l0, rhs=ptp[:, 128:256], start=True, stop=False),
                                 reads=list(rb) + [ptp], writes=[dst])
                            P.op("pe", lambda e, dst=dst, l1=l1, ptc=ptc: e.matmul(dst[:, reg], lhsT=l1, rhs=ptc[:, 0:128], start=False, stop=True),
                                 reads=list(rb) + [ptc], writes=[dst])
                    else:
                        P.op("pe", lambda e: e.matmul(N[:, reg], lhsT=v_[:, r * nb + qb, :], rhs=ptc[:, 0:128], start=True, stop=True),
                             reads=list(pcs) + [ptc], writes=[N])
                        P.op("pe", lambda e: e.matmul(Dn[:, reg], lhsT=ones[:], rhs=ptc[:, 0:128], start=True, stop=True),
                             reads=[ones, ptc], writes=[Dn])
                    if gi == 3 or qb == nb - 1:
                        ng = gi + 1
                        qb0 = qb - gi
                        lo = r + dil * 128 * qb0
                        hi = lo + dil * (128 * ng - 1) + 1
                        nsl = num[:, lo:hi:dil]
                        dsl = den[:, lo:hi:dil]
                        if pi == 0:
                            P.op("act", lambda e: e.activation(out=nsl, in_=N[:, 0:ng * 128], func=AF.Copy), reads=[N], writes=[num])
                            P.op("dve", lambda e: e.tensor_copy(out=dsl, in_=Dn[:, 0:ng * 128]), reads=[Dn], writes=[den])
                        else:
                            P.op("dve", lambda e: e.tensor_tensor(out=nsl, in0=nsl, in1=N[:, 0:ng * 128], op=ALU.add), reads=[num, N], writes=[num])
                            P.op("dve", lambda e: e.tensor_tensor(out=dsl, in0=dsl, in1=Dn[:, 0:ng * 128], op=ALU.add), reads=[den, Dn], writes=[den])
                        gcount += 1

                for t in tasks:
                    if t[0] == "load":
                        _, v_, vdm, dst, src, sb_ = t
                        P.dma("sp", dst, src, vdm, reads=sb_, writes=[v_])
                        continue
                    _, pi, dil, nb, r, kb, v_, pcs = t
                    nq = 256 if kb < nb - 1 else 128
                    base = r + dil * 128 * kb
                    ksl = k_[:, base:base + dil * 127 + 1:dil]
                    qsl = q_[:, base:base + dil * (nq - 1) + 1:dil]
                    Sb = psS[bi % 3]
                    eb = ebf[bi % 3]
                    pt = PT[bi % NPT]
                    ptmap[bi] = pt
                    P.op("pe", lambda e, Sb=Sb, ksl=ksl, qsl=qsl, nq=nq: e.matmul(Sb[:, 0:nq], lhsT=ksl, rhs=qsl, start=True, stop=True),
                         reads=[k_, q_], writes=[Sb])
                    P.op("act", lambda e, Sb=Sb, eb=eb, nq=nq: e.activation(out=eb[:, 0:nq], in_=Sb[:, 0:nq], func=AF.Exp, scale=scale),
                         reads=[Sb], writes=[eb])
                    P.op("pool" if bi % 3 == 0 else "dve", lambda e, eb=eb, pt=pt, nq=nq: e.tensor_tensor(out=pt[:, 0:nq], in0=eb[:, 0:nq], in1=mask[:, 0:nq], op=ALU.mult),
                         reads=[eb, mask], writes=[pt])
                    pending.append(bi)
                    bi += 1
                    if len(pending) > SKEW:
                        stage2(pending.pop(0))
                while pending:
                    stage2(pending.pop(0))
                P.op("act", lambda e, den=den: e.activation(out=rden[:], in_=den[:], func=AF.Ln), reads=[den], writes=[rden])
                P.op("act", lambda e: e.activation(out=rden[:], in_=rden[:], func=AF.Exp, scale=-1.0), reads=[rden], writes=[rden])
                H2 = S // 4
                P.op("pool", lambda e, num=num: e.tensor_tensor(out=y[:, 0:H2], in0=num[:, 0:H2], in1=rden[:, 0:H2], op=ALU.mult), reads=[num, rden], writes=[yl])
                P.op("dve", lambda e, num=num: e.tensor_tensor(out=y[:, H2:S], in0=num[:, H2:S], in1=rden[:, H2:S], op=ALU.mult), reads=[num, rden], writes=[yh])
                P.dma("pool", G.catT.t[4 + hd, :, s * S:(s + 1) * S], y[:], yd, reads=[yl, yh], writes=[G.catT_b[s * BPS + bb] for bb in range(BPS)])
        P.end_phase()
    P.scope = P.es
```

```python
import contextlib
import math
import numpy as np
import concourse.bass as bass
import concourse.mybir as mybir
from concourse.bass_utils import run_bass_kernel_spmd

F32 = mybir.dt.float32
BF16 = mybir.dt.bfloat16
I32 = mybir.dt.int32
AF = mybir.ActivationFunctionType
ALU = mybir.AluOpType
AX = mybir.AxisListType

D = 1024
DFF = 2816
NH = 4
DH = 128
TB = 512
NJ = TB // 128
DEPTH = 4
ALPHA = float((2 * DEPTH) ** 0.25)
EPS = 1e-5
POOL_W = (2, 4, 8, 16)
DIL_PATTERNS = ((128, 1), (512, 4), (2048, 16))
EPOCH = 30000
CDEC = [float(math.exp(128.0 * math.log1p(-(2.0 ** (-5.0 - h))))) for h in range(NH)]

C_INVF, C_SGN, C_ID, C_DEC, C_KDEC, C_CDEC, C_QDEC, C_MASK, C_PCNT, C_RM, C_END = (
    0, 1, 2, 130, 642, 1154, 1666, 3714, 3970, 4034, 4162)


class Buf:
    __slots__ = ("name", "t", "excl", "lw", "rd", "rd_dma")

    def __init__(self, name, t=None, excl=False):
        self.name = name
        self.t = t
        self.excl = excl
        self.lw = None
        self.rd = {}
        self.rd_dma = []

    def __getitem__(self, k):
        return self.t[k]


class DSem:
    __slots__ = ("sem", "cnt")

    def __init__(self, sem):
        self.sem = sem
        self.cnt = 0


class Op:
    __slots__ = ("eng", "fn", "deps", "sig", "sem", "val", "isdma", "dsem")

    def __init__(self, eng, fn, isdma=False, dsem=None):
        self.eng = eng
        self.fn = fn
        self.deps = ()
        self.sig = False
        self.sem = None
        self.val = 0
        self.isdma = isdma
        self.dsem = dsem


class Prog:
    ENGS = ("pe", "act", "dve", "pool", "sp")

    def __init__(self, nc):
        self.nc = nc
        self.es = contextlib.ExitStack()
        self.scope = self.es
        self.ops = []
        self.eng = {"pe": nc.tensor, "act": nc.scalar, "dve": nc.vector, "pool": nc.gpsimd, "sp": nc.sync}
        self.nsem = 0
        self.nbuf = 0
        self.cnt = {}
        self.cursem = {}
        self.waited = {e: {} for e in self.ENGS}
        self.pending_dma = []
        self.nwait = 0
        self.ninst = 0
        self.dsem_pool = []
        self.phase_dsems = []
        self.bar_t = self.es.enter_context(self.nc.sbuf_tensor("bar_scr", [128, 8], F32))

    def sbuf(self, name, shape, dt):
        self.nbuf += 1
        t = self.scope.enter_context(self.nc.sbuf_tensor(f"{name}_{self.nbuf}", list(shape), dt))
        return Buf(name, t)

    def psum(self, name, shape, dt):
        self.nbuf += 1
        t = self.scope.enter_context(self.nc.psum_tensor(f"{name}_{self.nbuf}", list(shape), dt))
        return Buf(name, t, excl=True)

    def dram(self, name, shape, dt, kind="Internal"):
        t = self.nc.dram_tensor(name, list(shape), dt, kind=kind)
        return Buf(name, t.ap())

    def new_sem(self, name=None):
        self.nsem += 1
        return self.es.enter_context(self.nc.semaphore(name or f"s{self.nsem}"))

    def dsem(self):
        d = self.dsem_pool.pop() if self.dsem_pool else DSem(self.new_sem())
        self.phase_dsems.append(d)
        return d

    def end_phase(self):
        self.barrier()
        self.flush()
        self.dsem_pool.extend(self.phase_dsems)
        self.phase_dsems = []

    def _add(self, op, reads, writes):
        eng = op.eng
        deps = set()
        for b in reads:
            if b.lw is not None:
                deps.add(b.lw)
            if b.excl:
                for e, r in b.rd.items():
                    if e != eng:
                        deps.add(r)
        for b in writes:
            if b.lw is not None:
                deps.add(b.lw)
            for r in b.rd.values():
                deps.add(r)
            for r in b.rd_dma:
                deps.add(r)
        if eng == "pe" and not op.isdma:
            deps = {d for d in deps if d.isdma or d.eng != "pe"}
        op.deps = tuple(deps)
        for d in deps:
            d.sig = True
        for b in reads:
            if op.isdma:
                b.rd_dma.append(op)
            else:
                b.rd[eng] = op
        for b in writes:
            b.lw = op
            b.rd = {}
            b.rd_dma = []
        self.ops.append(op)
        return op

    def op(self, eng, fn, reads=(), writes=()):
        return self._add(Op(eng, fn), reads, writes)

    def dma(self, q, out, in_, dsem, reads=(), writes=(), **kw):
        def fn(e, out=out, in_=in_, kw=kw):
            return e.dma_start(out=out, in_=in_, **kw)
        o = Op(q, fn, isdma=True, dsem=dsem)
        o.sig = True
        self.pending_dma.append(o)
        return self._add(o, reads, writes)

    def barrier(self):
        if self.bar_t is None:
            t = self.es.enter_context(self.nc.sbuf_tensor("bar_scr", [128, 8], F32))
            self.bar_t = t
        t = self.bar_t
        first = []
        for i, e in enumerate(("act", "dve", "pool")):
            if e == "act":
                o = Op(e, (lambda en, i=i: en.memzero(t[:, i:i + 1])))
            else:
                o = Op(e, (lambda en, i=i: en.memset(t[:, i:i + 1], 0.0)))
            o.sig = True
            self.ops.append(o)
            first.append(o)
        last_pe = None
        for o in reversed(self.ops):
            if o.eng == "pe" and not o.isdma:
                last_pe = o
                break
        if last_pe is not None:
            last_pe.sig = True
            first.append(last_pe)
        deps = tuple(first) + tuple(self.pending_dma)
        self.pending_dma = []
        self.join_deps = deps

    def flush(self, with_join=True):
        self._emit()
        jd = getattr(self, "join_deps", None)
        if jd:
            for en in self.ENGS:
                self._waits(en, jd)
            self.join_deps = None

    def _assign(self, op):
        if op.isdma:
            op.dsem.cnt += 16
            op.sem = op.dsem.sem
            op.val = op.dsem.cnt
        elif op.sig:
            e = op.eng
            if e not in self.cursem or self.cnt[e] >= EPOCH:
                self.cursem[e] = self.new_sem(f"e_{e}_{self.nsem}")
                self.cnt[e] = 0
            self.cnt[e] += 1
            op.sem = self.cursem[e]
            op.val = self.cnt[e]

    def _waits(self, en, deps):
        e = self.eng[en]
        w = self.waited[en]
        need = {}
        for d in deps:
            k = id(d.sem)
            if w.get(k, 0) >= d.val:
                continue
            if k not in need or need[k][1] < d.val:
                need[k] = (d.sem, d.val)
        for k, (s, v) in need.items():
            e.wait_ge(s, v)
            w[k] = v
            self.nwait += 1

    def _emit(self):
        for op in self.ops:
            self._assign(op)
        for op in self.ops:
            self._waits(op.eng, op.deps)
            ins = op.fn(self.eng[op.eng])
            if op.isdma:
                ins.then_inc(op.sem, 16)
            elif op.sig:
                ins.then_inc(op.sem, 1)
            op.fn = None
            self.ninst += 1
        self.ops = []

    def finish(self, final_ops=()):
        self._emit()
        e = self.eng["sp"]
        for d in final_ops:
            e.wait_ge(d.sem, d.val)
        self.es.close()


def make_consts():
    c = np.zeros((128, C_END), np.float64)
    inv = 10000.0 ** (-(np.arange(0, DH, 2, dtype=np.float32) / np.float32(DH)).astype(np.float32))
    inv = inv.astype(np.float32)
    p = np.arange(128)
    c[:, C_INVF] = inv[p % 64]
    c[:, C_SGN] = np.where(p < 64, -1.0, 1.0)
    c[:, C_ID:C_ID + 128] = np.eye(128)
    lg = np.log1p(-(2.0 ** (-5.0 - np.arange(NH, dtype=np.float64))))
    i = np.arange(128)
    sc = DH ** -0.5
    for h in range(NH):
        rel = i[None, :] - i[:, None]
        c[:, C_DEC + h * 128:C_DEC + (h + 1) * 128] = np.where(rel >= 0, sc * np.exp(np.maximum(rel, 0) * lg[h]), 0.0)
        c[:, C_KDEC + h * 128:C_KDEC + (h + 1) * 128] = (sc * np.exp((127 - i) * lg[h]))[:, None]
        c[:, C_CDEC + h * 128:C_CDEC + (h + 1) * 128] = np.exp(128 * lg[h])
        t = np.arange(TB)
        c[:, C_QDEC + h * TB:C_QDEC + (h + 1) * TB] = np.exp(((t % 128) + 1) * lg[h])[None, :]
    cc = np.arange(256)
    c[:, C_MASK:C_MASK + 256] = ((cc[None, :] >= p[:, None]) & (cc[None, :] <= p[:, None] + 128)).astype(np.float64)
    for g, w in enumerate(POOL_W):
        t = np.arange(16)
        c[:, C_PCNT + g * 16:C_PCNT + (g + 1) * 16] = (1.0 / np.minimum(t + 1, w))[None, :]
    c[(p + 64) % 128, C_RM + p] = 1.0
    return c.astype(np.float32)


class Ctx:
    pass


def load_weight(P, stages, sidx, src_ap, srcBuf, dst_ap, dstBuf, ncols):
    st, ds = stages[sidx[0] % len(stages)]
    ce = ("act", "pool", "dve")[sidx[0] % 3]
    sidx[0] += 1
    P.dma("sp", st[:, 0:ncols], src_ap, ds, reads=[srcBuf], writes=[st])
    if ce == "act":
        P.op("act", lambda e: e.activation(out=dst_ap, in_=st[:, 0:ncols], func=AF.Copy), reads=[st], writes=[dstBuf])
    else:
        P.op(ce, lambda e: e.tensor_copy(out=dst_ap, in_=st[:, 0:ncols]), reads=[st], writes=[dstBuf])


def bulk_load(P, items, width, nstage=8):
    outer = P.scope
    with contextlib.ExitStack() as st:
        P.scope = st
        stages = [(P.sbuf(f"stg{i}", [128, width], F32), P.dsem()) for i in range(nstage)]
        for n_, (src_ap, srcBuf, dst_ap, dstBuf, ncols) in enumerate(items):
            stg, ds = stages[n_ % nstage]
            P.dma("sp" if n_ % 2 == 0 else "act", stg[:, 0:ncols], src_ap, ds, reads=[srcBuf], writes=[stg])
            if n_ % 2 == 0:
                P.op("dve", lambda e, stg=stg, dst_ap=dst_ap, ncols=ncols: e.tensor_copy(out=dst_ap, in_=stg[:, 0:ncols]), reads=[stg], writes=[dstBuf])
            else:
                P.op("act", lambda e, stg=stg, dst_ap=dst_ap, ncols=ncols: e.activation(out=dst_ap, in_=stg[:, 0:ncols], func=AF.Copy), reads=[stg], writes=[dstBuf])
        P.barrier()
        P.flush()
    P.scope = outer


def ln_A(P, r_ap, rBuf, sm, k):
    i = k % 3
    st, mv, ve, nm, rs, nb, nh = sm["st"][i], sm["mv"][i], sm["ve"][i], sm["nm"][i], sm["rs"][i], sm["nb"][i], sm["nh"]
    for h in range(2):
        P.op("dve", lambda e, h=h: e.bn_stats(out=st[:, h, :], in_=r_ap[:, h * 512:(h + 1) * 512]), reads=[rBuf], writes=[st])
    P.op("dve", lambda e: e.bn_aggr(out=mv[:], in_=st[:].rearrange("p a b -> p (a b)")), reads=[st], writes=[mv])
    P.op("dve", lambda e: e.tensor_scalar_add(out=ve[:], in0=mv[:, 1:2], scalar1=EPS), reads=[mv], writes=[ve])
    P.op("dve", lambda e: e.tensor_scalar(out=nm[:], in0=mv[:, 0:1], scalar1=-1.0, scalar2=None, op0=ALU.mult), reads=[mv], writes=[nm])
    P.op("pool", lambda e: e.tensor_tensor(out=rs[:], in0=ve[:], in1=nh[:, 0:1], op=ALU.pow), reads=[ve, nh], writes=[rs])
    P.op("pool", lambda e: e.tensor_tensor(out=nb[:], in0=nm[:], in1=rs[:], op=ALU.mult), reads=[nm, rs], writes=[nb])
    return rs, nb


def ln_A2(P, r_ap, rBuf, ssum, junk, sm, k):
    i = k % 3
    mv, ve, nm, rs, nb, nh = sm["mv"][i], sm["ve"][i], sm["nm"][i], sm["rs"][i], sm["nb"][i], sm["nh"]
    sq = sm["st"][i]
    P.op("act", lambda e: e.activation(out=junk[:], in_=r_ap, func=AF.Square, accum_out=sq[:, 0, 0:1]), reads=[rBuf], writes=[junk, sq])
    P.op("dve", lambda e: e.tensor_tensor(out=mv[:, 0:1], in0=ssum[:, 0:1], in1=ssum[:, 1:2], op=ALU.add), reads=[ssum], writes=[mv])
    P.op("dve", lambda e: e.tensor_scalar(out=nm[:], in0=mv[:, 0:1], scalar1=-1.0 / D, scalar2=None, op0=ALU.mult), reads=[mv], writes=[nm])
    P.op("dve", lambda e: e.tensor_tensor(out=mv[:, 1:2], in0=nm[:], in1=nm[:], op=ALU.mult), reads=[nm], writes=[mv])
    P.op("dve", lambda e: e.scalar_tensor_tensor(out=ve[:], in0=sq[:, 0, 0:1], scalar=1.0 / D, in1=mv[:, 1:2], op0=ALU.mult, op1=ALU.subtract),
         reads=[sq, mv], writes=[ve])
    P.op("dve", lambda e: e.tensor_scalar_add(out=ve[:], in0=ve[:], scalar1=EPS), reads=[ve], writes=[ve])
    P.op("pool", lambda e: e.tensor_tensor(out=rs[:], in0=ve[:], in1=nh[:, 0:1], op=ALU.pow), reads=[ve, nh], writes=[rs])
    P.op("pool", lambda e: e.tensor_tensor(out=nb[:], in0=nm[:], in1=rs[:], op=ALU.mult), reads=[nm, rs], writes=[nb])
    return rs, nb


def ln_B(P, r_ap, rBuf, o_ap, oBuf, g_tab, b_tab, gbBuf, rs, nb):
    P.op("act", lambda e: e.activation(out=o_ap, in_=r_ap, func=AF.Identity, bias=nb[:], scale=rs[:]), reads=[rBuf, nb, rs], writes=[oBuf])
    P.op("dve", lambda e: e.tensor_tensor(out=o_ap, in0=o_ap, in1=g_tab, op=ALU.mult), reads=[oBuf, gbBuf], writes=[oBuf])
    P.op("pool", lambda e: e.tensor_tensor(out=o_ap, in0=o_ap, in1=b_tab, op=ALU.add), reads=[oBuf, gbBuf], writes=[oBuf])


def ln_smalls(P, tag):
    sm = {}
    sm["st"] = [P.sbuf(f"{tag}_st{i}", [128, 2, 6], F32) for i in range(3)]
    sm["mv"] = [P.sbuf(f"{tag}_mv{i}", [128, 2], F32) for i in range(3)]
    for n in ("ve", "nm", "rs", "nb"):
        sm[n] = [P.sbuf(f"{tag}_{n}{i}", [128, 1], F32) for i in range(3)]
    sm["nh"] = P.sbuf(f"{tag}_nh", [128, 16], F32)
    P.op("pool", lambda e: e.memset(sm["nh"][:], -0.5), writes=[sm["nh"]])
    return sm


def blk_bufs(name, ap, nblk):
    return [Buf(f"{name}{b}", ap) for b in range(nblk)]


def phase_P(P, G, w_out_ap, wBuf, lng_ap, lnb_ap, lnBuf, hin, hin_b, hout, hout_b):
    NT, NB = G.NT, G.NB
    with contextlib.ExitStack() as sc:
        P.scope = sc
        wo = P.sbuf("P_wo", [128, 8, D], BF16)
        wo_k = [Buf(f"P_wo{k}", wo.t) for k in range(8)]
        bulk_load(P, [(w_out_ap[k * 128:(k + 1) * 128, :], wBuf, wo.t[:, k, :], wo_k[k], 1024) for k in range(8)], 1024)
        gb = P.sbuf("P_gb", [128, 2, D], F32)
        gds = P.dsem()
        P.dma("sp", gb[:, 0, :], lng_ap.partition_broadcast(128), gds, reads=[lnBuf], writes=[gb])
        P.dma("sp", gb[:, 1, :], lnb_ap.partition_broadcast(128), gds, reads=[lnBuf], writes=[gb])
        sm = ln_smalls(P, "P")
        ssums = [P.sbuf(f"P_ss{i}", [128, 2], F32) for i in range(3)]
        junk = P.sbuf("P_junk", [128, D], F32)
        ct = [P.sbuf(f"P_ct{i}", [128, 8, TB], BF16) for i in range(2)]
        ctd = [P.dsem() for _ in range(2)]
        hb = [[P.sbuf(f"P_h{i}_{j}", [128, D], F32) for j in range(NJ)] for i in range(3)]
        hd_ = [[P.dsem() for j in range(NJ)] for i in range(3)]
        ob = [P.sbuf(f"P_o{i}", [128, D], F32) for i in range(4)]
        od = [P.dsem() for _ in range(4)]
        ps = [P.psum(f"P_ps{i}", [128, 512], F32) for i in range(4)]

        def loads(b):
            i = b % 2
            h3 = b % 3
            t0 = b * TB
            P.dma("sp", ct[i][:], G.catT.t[:, :, t0:t0 + TB].rearrange("c p t -> p c t"), ctd[i], reads=[G.catT_b[b]], writes=[ct[i]])
            for j in range(NJ):
                P.dma("sp", hb[h3][j][:], hin.t[t0 + j * 128:t0 + (j + 1) * 128, :], hd_[h3][j], reads=[hin_b[b]], writes=[hb[h3][j]])

        loads(0)
        kk = 0
        pend = None
        st4 = {n: [P.sbuf(f"P4_{n}{i}", [128, 4], F32) for i in range(2)] for n in ("s", "q", "nm", "msq", "ve", "rs", "nb")}
        ss8 = [P.sbuf(f"P4_ss{i}", [128, 8], F32) for i in range(2)]
        nh4 = P.sbuf("P4_nh", [128, 4], F32)
        P.op("pool", lambda e: e.memset(nh4[:], -0.5), writes=[nh4])
        for b in range(NB):
            i = b % 2
            t0 = b * TB
            h3 = b % 3
            if b + 1 < NB:
                loads(b + 1)
            ss, sq4 = ss8[i], st4["q"][i]
            for j in range(NJ):
                for n in range(2):
                    pb = ps[(2 * j + n) % 4]
                    for k in range(8):
                        P.op("pe", lambda e, pb=pb, k=k, j=j, n=n, i=i: e.matmul(
                            pb[:, :], lhsT=ct[i][:, k, j * 128:(j + 1) * 128], rhs=wo.t[:, k, n * 512:(n + 1) * 512],
                            start=(k == 0), stop=(k == 7)), reads=[ct[i], wo_k[k]], writes=[pb])
                    P.op("dve", lambda e, pb=pb, j=j, n=n, h3=h3, ss=ss: e.scalar_tensor_tensor(
                        out=hb[h3][j][:, n * 512:(n + 1) * 512], in0=hb[h3][j][:, n * 512:(n + 1) * 512], scalar=ALPHA,
                        in1=pb[:, :], op0=ALU.mult, op1=ALU.add, accum_out=ss[:, 2 * j + n:2 * j + n + 1]), reads=[hb[h3][j], pb], writes=[hb[h3][j], ss])
                P.op("act", lambda e, j=j, h3=h3, sq4=sq4: e.activation(out=junk[:], in_=hb[h3][j][:], func=AF.Square, accum_out=sq4[:, j:j + 1]),
                     reads=[hb[h3][j]], writes=[junk, sq4])
                if pend is not None:
                    try:
                        next(pend)
                    except StopIteration:
                        pend = None

            def tail(b=b, i=i, h3=h3, t0=t0, ss=ss, sq4=sq4):
                nonlocal kk
                s_, nm, msq, ve, rs, nb_ = (st4[n][i] for n in ("s", "nm", "msq", "ve", "rs", "nb"))
                P.op("dve", lambda e: e.tensor_reduce(out=s_[:], in_=ss[:].rearrange("p (j n) -> p j n", n=2), axis=AX.X, op=ALU.add), reads=[ss], writes=[s_])
                P.op("dve", lambda e: e.tensor_scalar(out=nm[:], in0=s_[:], scalar1=-1.0 / D, scalar2=None, op0=ALU.mult), reads=[s_], writes=[nm])
                P.op("dve", lambda e: e.tensor_tensor(out=msq[:], in0=nm[:], in1=nm[:], op=ALU.mult), reads=[nm], writes=[msq])
                P.op("dve", lambda e: e.scalar_tensor_tensor(out=ve[:], in0=sq4[:], scalar=1.0 / D, in1=msq[:], op0=ALU.mult, op1=ALU.subtract),
                     reads=[sq4, msq], writes=[ve])
                P.op("dve", lambda e: e.tensor_scalar_add(out=ve[:], in0=ve[:], scalar1=EPS), reads=[ve], writes=[ve])
                P.op("pool", lambda e: e.tensor_tensor(out=rs[:], in0=ve[:], in1=nh4[:], op=ALU.pow), reads=[ve, nh4], writes=[rs])
                P.op("pool", lambda e: e.tensor_tensor(out=nb_[:], in0=nm[:], in1=rs[:], op=ALU.mult), reads=[nm, rs], writes=[nb_])
                for j in range(NJ):
                    o = ob[kk % 4]
                    P.op("act", lambda e, j=j, o=o: e.activation(out=o[:], in_=hb[h3][j][:], func=AF.Identity, bias=nb_[:, j:j + 1], scale=rs[:, j:j + 1]),
                         reads=[hb[h3][j], nb_, rs], writes=[o])
                    P.op("dve", lambda e, o=o: e.tensor_tensor(out=o[:], in0=o[:], in1=gb[:, 0, :], op=ALU.mult), reads=[o, gb], writes=[o])
                    P.op("pool", lambda e, o=o: e.tensor_tensor(out=o[:], in0=o[:], in1=gb[:, 1, :], op=ALU.add), reads=[o, gb], writes=[o])
                    P.dma("sp", hout.t[t0 + j * 128:t0 + (j + 1) * 128, :], o[:], od[kk % 4], reads=[o], writes=[hout_b[b]])
                    kk += 1
                    if j % 2 == 1:
                        yield
            if pend is not None:
                for _ in pend:
                    pass
            pend = tail()
        for _ in pend:
            pass
        P.end_phase()
    P.scope = P.es


def phase_F(P, G, wi_ap, wiBuf, wf_ap, wfBuf, lng_ap, lnb_ap, lnBuf, hin, hin_b, hout, hout_b):
    NT, NB = G.NT, G.NB
    NC = DFF // 128
    fin = []
    with contextlib.ExitStack() as sc:
        P.scope = sc
        wi = P.sbuf("F_wi", [128, 8, 2 * DFF], BF16)
        wi_k = [Buf(f"F_wi{k}", wi.t) for k in range(8)]
        wf = P.sbuf("F_wf", [128, NC, D], BF16)
        wf_k = [Buf(f"F_wf{k}", wf.t) for k in range(NC)]
        items = []
        for k in range(8):
            for q in range(4):
                items.append((wi_ap[k * 128:(k + 1) * 128, q * 1408:(q + 1) * 1408], wiBuf, wi.t[:, k, q * 1408:(q + 1) * 1408], wi_k[k], 1408))
        for k in range(NC):
            items.append((wf_ap[k * 128:(k + 1) * 128, :], wfBuf, wf.t[:, k, :], wf_k[k], 1024))
        bulk_load(P, items, 1408)
        gb = P.sbuf("F_gb", [128, 2, D], F32)
        gds = P.dsem()
        P.dma("sp", gb[:, 0, :], lng_ap.partition_broadcast(128), gds, reads=[lnBuf], writes=[gb])
        P.dma("sp", gb[:, 1, :], lnb_ap.partition_broadcast(128), gds, reads=[lnBuf], writes=[gb])
        ident = P.sbuf("F_id", [128, 128], F32)
        P.dma("sp", ident[:], G.consts.t[:, C_ID:C_ID + 128], P.dsem(), reads=[G.consts], writes=[ident])
        sm = ln_smalls(P, "F")
        NR = 6
        hbr = [P.sbuf(f"F_h{j}", [128, D], F32) for j in range(NR)]
        hdr = [P.dsem() for j in range(NR)]
        ob = [P.sbuf(f"F_o{i}", [128, D], F32) for i in range(2)]
        od = [P.dsem() for _ in range(2)]
        hT = P.sbuf("F_hT", [128, 8, TB], BF16)
        hT_k = [Buf(f"F_hT{k}", hT.t) for k in range(8)]
        act = P.sbuf("F_act", [128, NC, TB], BF16)
        act_k = [Buf(f"F_act{k}", act.t) for k in range(NC)]
        sg = [P.sbuf(f"F_sg{i}", [128, TB], F32) for i in range(2)]
        psO = [P.psum(f"F_psO{i}", [128, 512], F32) for i in range(2)]
        psG = [P.psum(f"F_psG{i}", [128, 512], F32) for i in range(2)]
        psU = [P.psum(f"F_psU{i}", [128, 512], F32) for i in range(2)]
        psT = [P.psum(f"F_psT{i}", [128, 512], F32) for i in range(2)]

        def load_tile(b, j):
            t0 = b * TB
            r = (4 * b + j) % NR
            P.dma("sp", hbr[r][:], hin.t[t0 + j * 128:t0 + (j + 1) * 128, :], hdr[r], reads=[hin_b[b]], writes=[hbr[r]])

        for j in range(NJ):
            load_tile(0, j)
        if NB > 1:
            load_tile(1, 0)
            load_tile(1, 1)
        kk = 0
        pend = None

        def do_transposes(b):
            hb = [hbr[(4 * b + j) % NR] for j in range(NJ)]
            for c in range(8):
                pt = psT[c % 2]
                for j in range(NJ):
                    P.op("pe", lambda e, pt=pt, j=j, c=c, hb=hb: e.transpose(out=pt[:, j * 128:(j + 1) * 128],
                                                                             in_=hb[j][:, c * 128:(c + 1) * 128], identity=ident[:]),
                         reads=[hb[j], ident], writes=[pt])
                if c % 2 == 0:
                    P.op("dve", lambda e, pt=pt, c=c: e.tensor_copy(out=hT.t[:, c, :], in_=pt[:, :]), reads=[pt], writes=[hT_k[c]])
                else:
                    P.op("act", lambda e, pt=pt, c=c: e.activation(out=hT.t[:, c, :], in_=pt[:, :], func=AF.Copy), reads=[pt], writes=[hT_k[c]])

        do_transposes(0)
        for b in range(NB):
            t0 = b * TB
            hb = [hbr[(4 * b + j) % NR] for j in range(NJ)]
            for c in range(NC):
                pg, pu, s = psG[c % 2], psU[c % 2], sg[c % 2]
                for k in range(8):
                    P.op("pe", lambda e, pg=pg, k=k, c=c: e.matmul(pg[:, :], lhsT=wi.t[:, k, c * 128:(c + 1) * 128], rhs=hT.t[:, k, :],
                                                                   start=(k == 0), stop=(k == 7)), reads=[wi_k[k], hT_k[k]], writes=[pg])
                for k in range(8):
                    P.op("pe", lambda e, pu=pu, k=k, c=c: e.matmul(pu[:, :], lhsT=wi.t[:, k, DFF + c * 128:DFF + (c + 1) * 128], rhs=hT.t[:, k, :],
                                                                   start=(k == 0), stop=(k == 7)), reads=[wi_k[k], hT_k[k]], writes=[pu])
                P.op("act", lambda e, pg=pg, s=s: e.activation(out=s[:], in_=pg[:, :], func=AF.Silu), reads=[pg], writes=[s])
                P.op("dve", lambda e, pu=pu, s=s, c=c: e.tensor_tensor(out=act.t[:, c, :], in0=s[:], in1=pu[:, :], op=ALU.mult),
                     reads=[s, pu], writes=[act_k[c]])
            for j in range(NJ):
                for n in range(2):
                    pb = psO[n]
                    for k in range(NC):
                        P.op("pe", lambda e, pb=pb, k=k, j=j, n=n: e.matmul(
                            pb[:, :], lhsT=act.t[:, k, j * 128:(j + 1) * 128], rhs=wf.t[:, k, n * 512:(n + 1) * 512],
                            start=(k == 0), stop=(k == NC - 1)), reads=[act_k[k], wf_k[k]], writes=[pb])
                    P.op("dve", lambda e, pb=pb, j=j, n=n, hb=hb: e.scalar_tensor_tensor(
                        out=hb[j][:, n * 512:(n + 1) * 512], in0=hb[j][:, n * 512:(n + 1) * 512], scalar=ALPHA,
                        in1=pb[:, :], op0=ALU.mult, op1=ALU.add), reads=[hb[j], pb], writes=[hb[j]])
                    if j == NJ - 1 and n == 0 and b + 1 < NB:
                        do_transposes(b + 1)
                rs_, nb_ = ln_A(P, hb[j][:], hb[j], sm, kk)

                def fin_tile(j=j, kk=kk, rs_=rs_, nb_=nb_, t0=t0, b=b, hb=hb):
                    o = ob[kk % 2]
                    ln_B(P, hb[j][:], hb[j], o[:], o, gb[:, 0, :], gb[:, 1, :], gb, rs_, nb_)
                    if j < 2 and b + 1 < NB:
                        load_tile(b + 1, j + 2)
                    if j >= 2 and b + 2 < NB:
                        load_tile(b + 2, j - 2)
                    fin.append(P.dma("sp", hout.t[t0 + j * 128:t0 + (j + 1) * 128, :], o[:], od[kk % 2], reads=[o], writes=[hout_b[b]]))
                if pend is not None:
                    pend()
                pend = fin_tile
                kk += 1
            pend()
            pend = None
        P.end_phase()
    P.scope = P.es
    return fin


def phase_R(P, G):
    S, NS = G.S, G.NS
    MAGIC = 12582912.0
    HI = 6.28125
    LO = 2.0 * math.pi - HI
    PIL = 3.1415925
    with contextlib.ExitStack() as sc:
        P.scope = sc
        cs = P.sbuf("R_cs", [128, 2], F32)
        P.dma("sp", cs[:], G.consts.t[:, 0:2], P.dsem(), reads=[G.consts], writes=[cs])
        pi = P.sbuf("R_pi", [128, S], I32)
        ang = P.sbuf("R_ang", [128, S], F32)
        kq = P.sbuf("R_k", [128, S], F32)
        r = P.sbuf("R_r", [128, S], F32)
        oc = P.sbuf("R_oc", [128, S], F32)
        os_ = P.sbuf("R_os", [128, S], F32)
        d1, d2, d3 = P.dsem(), P.dsem(), P.dsem()
        for s in range(NS):
            P.dma("sp", pi[:], G.pos.t[s:s + 1, :].partition_broadcast(128), d1, reads=[G.pos], writes=[pi])
            P.op("dve", lambda e: e.tensor_copy(out=ang[:], in_=pi[:]), reads=[pi], writes=[ang])
            P.op("dve", lambda e: e.tensor_scalar(out=ang[:], in0=ang[:], scalar1=cs[:, 0:1], scalar2=None, op0=ALU.mult),
                 reads=[ang, cs], writes=[ang])
            P.op("dve", lambda e: e.tensor_scalar(out=kq[:], in0=ang[:], scalar1=float(1.0 / (2.0 * math.pi)), scalar2=MAGIC,
                                                  op0=ALU.mult, op1=ALU.add), reads=[ang], writes=[kq])
            P.op("dve", lambda e: e.tensor_scalar_add(out=kq[:], in0=kq[:], scalar1=-MAGIC), reads=[kq], writes=[kq])
            P.op("dve", lambda e: e.scalar_tensor_tensor(out=r[:], in0=kq[:], scalar=-HI, in1=ang[:], op0=ALU.mult, op1=ALU.add),
                 reads=[kq, ang], writes=[r])
            P.op("dve", lambda e: e.scalar_tensor_tensor(out=r[:], in0=kq[:], scalar=-LO, in1=r[:], op0=ALU.mult, op1=ALU.add),
                 reads=[kq, r], writes=[r])
            P.op("dve", lambda e: e.tensor_scalar(out=r[:], in0=r[:], scalar1=-PIL, scalar2=PIL, op0=ALU.max, op1=ALU.min),
                 reads=[r], writes=[r])
            P.op("act", lambda e: e.activation(out=os_[:], in_=r[:], func=AF.Sin, scale=cs[:, 1:2]), reads=[r, cs], writes=[os_])
            P.op("dve", lambda e: e.scalar_tensor_tensor(out=kq[:], in0=r[:], scalar=-1.0, in1=r[:], op0=ALU.mult, op1=ALU.max), reads=[r], writes=[kq])
            P.op("act", lambda e: e.activation(out=oc[:], in_=kq[:], func=AF.Sin, scale=-1.0, bias=float(math.pi / 2)),
                 reads=[kq], writes=[oc])
            P.dma("sp", G.cosT.t[s, :, :], oc[:], d2, reads=[oc], writes=[G.cosT])
            P.dma("sp", G.sinT.t[s, :, :], os_[:], d3, reads=[os_], writes=[G.sinT])
        P.end_phase()
    P.scope = P.es


def transpose_block(P, hb, ident, psT, hT, hT_k):
    for c in range(8):
        pt = psT[c % len(psT)]
        for j in range(NJ):
            P.op("pe", lambda e, pt=pt, j=j, c=c: e.transpose(out=pt[:, j * 128:(j + 1) * 128],
                                                              in_=hb[j][:, c * 128:(c + 1) * 128], identity=ident[:]),
                 reads=[hb[j], ident], writes=[pt])
        if c % 2 == 0:
            P.op("dve", lambda e, pt=pt, c=c: e.tensor_copy(out=hT.t[:, c, :], in_=pt[:, :]), reads=[pt], writes=[hT_k[c]])
        else:
            P.op("act", lambda e, pt=pt, c=c: e.activation(out=hT.t[:, c, :], in_=pt[:, :], func=AF.Copy), reads=[pt], writes=[hT_k[c]])


def rope_head(P, X, Y, cs, sn, t1, t2):
    P.op("dve", lambda e: e.tensor_tensor(out=t1[:], in0=X[:, :], in1=cs[:], op=ALU.mult), reads=[X, cs], writes=[t1])
    P.op("dve", lambda e: e.tensor_tensor(out=t2[:], in0=Y[:, :], in1=sn[:], op=ALU.mult), reads=[Y, sn], writes=[t2])
    P.op("pool", lambda e: e.tensor_tensor(out=t1[:], in0=t1[:], in1=t2[:], op=ALU.add), reads=[t1, t2], writes=[t1])


def phase_E(P, G, l2, hin, hin_b):
    S, NS, NB = G.S, G.NS, G.NB
    BPS = S // TB
    WC = 2560
    with contextlib.ExitStack() as sc:
        P.scope = sc
        wi = P.sbuf("E_wi", [128, 8, WC], BF16)
        wi_k = [Buf(f"E_wi{k}", wi.t) for k in range(8)]
        bulk_load(P, [(G.ev_w_in.t[l2, k * 128:(k + 1) * 128, q * 1280:(q + 1) * 1280], G.ev_w_in,
                       wi.t[:, k, q * 1280:(q + 1) * 1280], wi_k[k], 1280) for k in range(8) for q in range(2)], 1280)
        pw = P.sbuf("E_pw", [128, 4, 128], BF16)
        pst, pds = P.sbuf("E_pst", [128, 512], F32), P.dsem()
        P.dma("sp", pst[:, 0:512].rearrange("p (g d) -> p g d", g=4), G.ev_pool_w.t[l2].rearrange("g c d -> c g d"), pds,
              reads=[G.ev_pool_w], writes=[pst])
        P.op("dve", lambda e: e.tensor_copy(out=pw[:].rearrange("p g d -> p (g d)"), in_=pst[:, 0:512]), reads=[pst], writes=[pw])
        psc = P.sbuf("E_psc", [128, 4], F32)
        P.dma("sp", psc[:], G.ev_pool_scale.t[l2], P.dsem(), reads=[G.ev_pool_scale], writes=[psc])
        gain = P.sbuf("E_gain", [128, 512], F32)
        P.dma("sp", gain[:], G.ev_ret_norm_g.t[l2:l2 + 1, :].partition_broadcast(128), P.dsem(), reads=[G.ev_ret_norm_g], writes=[gain])
        ct = P.sbuf("E_ct", [128, C_END - C_ID], F32)
        P.dma("sp", ct[:], G.consts.t[:, C_ID:C_END], P.dsem(), reads=[G.consts], writes=[ct])
        o_ = lambda c: c - C_ID
        ident = Buf("E_ident", ct.t[:, o_(C_ID):o_(C_ID) + 128])
        dec = ct.t[:, o_(C_DEC):o_(C_DEC) + 512]
        kdec = ct.t[:, o_(C_KDEC):o_(C_KDEC) + 512]
        cdec = ct.t[:, o_(C_CDEC):o_(C_CDEC) + 512]
        qdec = ct.t[:, o_(C_QDEC):o_(C_QDEC) + 2048]
        pcnt = ct.t[:, o_(C_PCNT):o_(C_PCNT) + 64]
        ident_bf = P.sbuf("E_idbf", [128, 128], BF16)
        P.op("dve", lambda e: e.tensor_copy(out=ident_bf[:], in_=ct.t[:, 0:128]), reads=[ct], writes=[ident_bf])
        Rm = P.sbuf("E_Rm", [128, 128], BF16)
        P.op("dve", lambda e: e.tensor_copy(out=Rm[:], in_=ct.t[:, C_RM - C_ID:C_RM - C_ID + 128]), reads=[ct], writes=[Rm])
        xqs = [P.sbuf(f"E_xq{i}", [128, TB], BF16) for i in range(2)]
        identF = Buf("E_identF", ct.t[:, 0:128])
        identF.lw = ct.lw
        nh = P.sbuf("E_nh", [128, 16], F32)
        P.op("pool", lambda e: e.memset(nh[:], -0.5), writes=[nh])

        cs = [P.sbuf(f"E_cs{i}", [128, TB], F32) for i in range(1)] * 2
        sn = [P.sbuf(f"E_sn{i}", [128, TB], F32) for i in range(1)] * 2
        csd = [P.dsem() for _ in range(1)] * 2
        snd = [P.dsem() for _ in range(1)] * 2
        hb = [P.sbuf(f"E_h{j}", [128, D], F32) for j in range(NJ)]
        hds = [P.dsem() for _ in range(NJ)]
        hT = P.sbuf("E_hT", [128, 8, TB], BF16)
        hT_k = [Buf(f"E_hT{k}", hT.t) for k in range(8)]
        qr = P.sbuf("E_qr", [128, NH, TB], BF16)
        kr = P.sbuf("E_kr", [128, NH, TB], BF16)
        qd = P.sbuf("E_qd", [128, NH, TB], BF16)
        t1 = [P.sbuf(f"E_t1{i}", [128, TB], F32) for i in range(2)]
        t2 = [P.sbuf(f"E_t2{i}", [128, TB], F32) for i in range(2)]
        vb = P.sbuf("E_vb", [128, NJ, 512], BF16)
        vdb = P.sbuf("E_vdb", [128, NJ, 512], BF16)
        gsg = P.sbuf("E_gsg", [128, NJ, 512], F32)
        xh = P.sbuf("E_xh", [128, 4, 16 + TB], F32)
        sa = P.sbuf("E_sa", [128, 16 + TB], F32)
        sb = P.sbuf("E_sb", [128, 16 + TB], F32)
        pooled = P.sbuf("E_pooled", [128, 4, TB], BF16)
        ws = P.sbuf("E_ws", [128, 4, 16 + TB], F32)
        state = P.sbuf("E_state", [128, 512], F32)
        stmp = P.sbuf("E_stmp", [128, 512], F32)
        sbf = [P.sbuf(f"E_sbf{i}", [128, 512], BF16) for i in range(6)]
        PT = [P.sbuf(f"E_PT{i}", [128, 512], BF16) for i in range(2)]
        ktok = [P.sbuf(f"E_ktok{i}", [128, 512], BF16) for i in range(2)]
        osb = P.sbuf("E_osb", [128, NJ, 512], F32)
        sq = P.sbuf("E_sq", [128, NJ, 512], F32)
        s1 = P.sbuf("E_s1", [128, 16], F32)
        s2 = P.sbuf("E_s2", [128, 16], F32)
        mean = P.sbuf("E_mean", [128, 16], F32)
        msq = P.sbuf("E_msq", [128, 16], F32)
        var = P.sbuf("E_var", [128, 16], F32)
        rstd = P.sbuf("E_rstd", [128, 16], F32)
        nb = P.sbuf("E_nb", [128, 16], F32)
        cat = [P.sbuf(f"E_cat{i}", [128, 8, TB], BF16) for i in range(1)] * 2
        catd = [P.dsem() for _ in range(1)] * 2
        psA = [P.psum(f"E_psA{i}", [128, 512], F32) for i in range(4)]
        psS = [P.psum(f"E_psS{i}", [128, 512], F32) for i in range(2)]
        psTB = P.psum("E_psTB", [128, 1024], BF16)
        psKV = P.psum("E_psKV", [128, 512], F32)
        psO = psS
        G.sbuf_left_E = P.nc.sbuf_bytes_remaining

        def loads(b):
            t0 = b * TB
            s = t0 // S
            ts = t0 - s * S
            i = b % 2
            P.dma("sp", cs[i][:], G.cosT.t[s, :, ts:ts + TB], csd[i], reads=[G.cosT], writes=[cs[i]])
            P.dma("sp", sn[i][:], G.sinT.t[s, :, ts:ts + TB], snd[i], reads=[G.sinT], writes=[sn[i]])
            for j in range(NJ):
                P.dma("sp", hb[j][:], hin.t[t0 + j * 128:t0 + (j + 1) * 128, :], hds[j], reads=[hin_b[b]], writes=[hb[j]])

        st_ = {"na": 0, "gch": 0}

        def nextA():
            X = psA[st_["na"] % 4]
            st_["na"] += 1
            return X

        def partA(b):
            i = b % 2
            for c0_ in (0, 4):
                for c in range(c0_, c0_ + 4):
                    pt = nextA()
                    for j in range(NJ):
                        P.op("pe", lambda e, pt=pt, j=j, c=c: e.transpose(out=pt[:, j * 128:(j + 1) * 128], in_=hb[j][:, c * 128:(c + 1) * 128],
                                                                          identity=identF[:]), reads=[hb[j], identF], writes=[pt])
                    if c % 2 == 0:
                        P.op("dve", lambda e, pt=pt, c=c: e.tensor_copy(out=hT.t[:, c, :], in_=pt[:, :]), reads=[pt], writes=[hT_k[c]])
                    else:
                        P.op("act", lambda e, pt=pt, c=c: e.activation(out=hT.t[:, c, :], in_=pt[:, :], func=AF.Copy), reads=[pt], writes=[hT_k[c]])
                yield
            items = [(which, col, hd) for which, col in (("q", 0), ("k", 512)) for hd in range(NH)]
            for p0 in range(0, len(items), 2):
                pair = []
                for nr, (which, col, hd) in enumerate(items[p0:p0 + 2]):
                    X = nextA()
                    for k in range(8):
                        P.op("pe", lambda e, X=X, k=k, c0=col + hd * 128: e.matmul(X[:, :], lhsT=wi.t[:, k, c0:c0 + 128], rhs=hT.t[:, k, :],
                                                                                  start=(k == 0), stop=(k == 7)), reads=[wi_k[k], hT_k[k]], writes=[X])
                    xq = xqs[nr]
                    P.op("act", lambda e, X=X, xq=xq: e.activation(out=xq[:], in_=X[:, :], func=AF.Copy), reads=[X], writes=[xq])
                    pair.append((which, hd, X, xq, nr))
                yield
                for which, hd, X, xq, nr in pair:
                    Y = nextA()
                    P.op("pe", lambda e, Y=Y, xq=xq: e.matmul(Y[:, :], lhsT=Rm[:], rhs=xq[:], start=True, stop=True), reads=[Rm, xq], writes=[Y])
                    a, bb = t1[nr], t2[nr]
                    rope_head(P, X, Y, cs[i], sn[i], a, bb)
                    if which == "q":
                        P.op("act", lambda e, a=a, hd=hd: e.activation(out=qr[:, hd, :], in_=a[:], func=AF.Copy), reads=[a], writes=[qr])
                        P.op("pool", lambda e, a=a, hd=hd: e.tensor_tensor(out=qd[:, hd, :], in0=a[:], in1=qdec[:, hd * TB:(hd + 1) * TB], op=ALU.mult),
                             reads=[a, ct], writes=[qd])
                    else:
                        P.op("act", lambda e, a=a, hd=hd: e.activation(out=kr[:, hd, :], in_=a[:], func=AF.Copy), reads=[a], writes=[kr])
                yield
            if b + 1 < NB:
                loads(b + 1)

        def partB(b):
            t0 = b * TB
            i = b % 2
            first = (t0 % S == 0)
            if first:
                P.op("pool", lambda e: e.memset(state[:], 0.0), writes=[state])
                sb0 = sbf[st_["gch"] % 6]
                P.op("pool", lambda e, sb0=sb0: e.memset(sb0[:], 0.0), writes=[sb0])
                P.op("pool", lambda e: e.memset(xh[:, :, 0:16], 0.0), writes=[xh])
            for j in range(NJ):
                X = nextA()
                for k in range(8):
                    P.op("pe", lambda e, X=X, k=k, j=j: e.matmul(X[:, :], lhsT=hT.t[:, k, j * 128:(j + 1) * 128], rhs=wi.t[:, k, 1024:1536],
                                                                 start=(k == 0), stop=(k == 7)), reads=[wi_k[k], hT_k[k]], writes=[X])
                P.op("act", lambda e, X=X, j=j: e.activation(out=vb[:, j, :], in_=X[:, :], func=AF.Copy), reads=[X], writes=[vb])
                P.op("dve", lambda e, X=X, j=j: e.tensor_tensor(out=vdb[:, j, :], in0=X[:, :], in1=kdec, op=ALU.mult), reads=[X, ct], writes=[vdb])
                X = nextA()
                for k in range(8):
                    P.op("pe", lambda e, X=X, k=k, j=j: e.matmul(X[:, :], lhsT=hT.t[:, k, j * 128:(j + 1) * 128], rhs=wi.t[:, k, 1536:2048],
                                                                 start=(k == 0), stop=(k == 7)), reads=[wi_k[k], hT_k[k]], writes=[X])
                P.op("act", lambda e, X=X, j=j: e.activation(out=gsg[:, j, :], in_=X[:, :], func=AF.Silu), reads=[X], writes=[gsg])
                P.op("pool", lambda e, j=j: e.tensor_tensor(out=gsg[:, j, :], in0=gsg[:, j, :], in1=gain[:], op=ALU.mult), reads=[gsg, gain], writes=[gsg])
            for j in range(NJ):
                gi, w = j, POOL_W[j]
                X = nextA()
                for k in range(8):
                    P.op("pe", lambda e, X=X, k=k, c0=2048 + gi * 128: e.matmul(X[:, :], lhsT=wi.t[:, k, c0:c0 + 128], rhs=hT.t[:, k, :],
                                                                               start=(k == 0), stop=(k == 7)), reads=[wi_k[k], hT_k[k]], writes=[X])
                P.op("act", lambda e, X=X, gi=gi: e.activation(out=xh[:, gi, 16:16 + TB], in_=X[:, :], func=AF.Copy), reads=[X], writes=[xh])
                cur, curBuf = xh.t[:, gi, :], xh
                sh = 1
                tgl = [sa, sb]
                ti = 0
                while sh < w:
                    last = (sh * 2 >= w)
                    dst = tgl[ti % 2]
                    dst_ap = ws.t[:, gi, :] if last else dst.t
                    dstBuf = ws if last else dst
                    lo = 2 * sh - 1
                    P.op("pool", lambda e, dst_ap=dst_ap, cur=cur, sh=sh, lo=lo: e.tensor_tensor(out=dst_ap[:, lo:16 + TB], in0=cur[:, lo:16 + TB],
                                                                                               in1=cur[:, lo - sh:16 + TB - sh], op=ALU.add),
                         reads=[curBuf], writes=[dstBuf])
                    cur, curBuf = dst_ap, dstBuf
                    sh *= 2
                    ti += 1
                def pool_fin(cur=cur, curBuf=curBuf, gi=gi, w=w, ti=ti, tgl=tgl, first=first):
                    P.op("dve", lambda e, cur=cur, gi=gi, w=w: e.scalar_tensor_tensor(out=pooled[:, gi, :], in0=cur[:, 16:16 + TB], scalar=float(1.0 / w),
                                                                                     in1=xh[:, gi, 16:16 + TB], op0=ALU.mult, op1=ALU.subtract),
                         reads=[curBuf, xh], writes=[pooled])
                    if first:
                        tmp = tgl[ti % 2]
                        P.op("pool", lambda e, cur=cur, gi=gi, tmp=tmp: e.tensor_tensor(out=tmp[:, 0:16], in0=cur[:, 16:32], in1=pcnt[:, gi * 16:(gi + 1) * 16], op=ALU.mult),
                             reads=[curBuf, ct], writes=[tmp])
                        P.op("pool", lambda e, gi=gi, tmp=tmp: e.tensor_tensor(out=pooled[:, gi, 0:16], in0=tmp[:, 0:16], in1=xh[:, gi, 16:32], op=ALU.subtract),
                             reads=[tmp, xh, pooled], writes=[pooled])
                def pool_mm(gi=gi, i=i, pool_fin=pool_fin):
                    pool_fin()
                    X = nextA()
                    P.op("pe", lambda e, X=X, gi=gi: e.matmul(X[:, :], lhsT=pw[:, gi, :], rhs=pooled[:, gi, :], start=True, stop=True),
                         reads=[pw, pooled], writes=[X])
                    P.op("act", lambda e, X=X, gi=gi, i=i: e.activation(out=cat[i][:, 4 + gi, :], in_=X[:, :], func=AF.Identity, scale=psc[:, gi:gi + 1]),
                         reads=[X, psc], writes=[cat[i]])
                st_.setdefault("pool_q", []).append(pool_mm)
            for j in range(NJ):
                Sb = psS[j % 2]
                for hd in range(NH):
                    P.op("pe", lambda e, Sb=Sb, hd=hd, j=j: e.matmul(Sb[:, hd * 128:(hd + 1) * 128], lhsT=kr[:, hd, j * 128:(j + 1) * 128],
                                                                     rhs=qr[:, hd, j * 128:(j + 1) * 128], start=True, stop=True),
                         reads=[kr, qr], writes=[Sb])
                pt = PT[j % 2]
                P.op("dve", lambda e, Sb=Sb, pt=pt: e.tensor_tensor(out=pt[:], in0=Sb[:, :], in1=dec, op=ALU.mult), reads=[Sb, ct], writes=[pt])
                for hd in range(NH):
                    P.op("pe", lambda e, hd=hd, j=j: e.transpose(out=psTB[:, (j % 2) * 512 + hd * 128:(j % 2) * 512 + (hd + 1) * 128],
                                                                 in_=kr[:, hd, j * 128:(j + 1) * 128], identity=ident_bf[:]),
                         reads=[kr, ident_bf], writes=[psTB])
                kt = ktok[j % 2]
                P.op("act", lambda e, kt=kt, j=j: e.activation(out=kt[:], in_=psTB[:, (j % 2) * 512:(j % 2 + 1) * 512], func=AF.Copy),
                     reads=[psTB], writes=[kt])
                for hd in range(NH):
                    P.op("pe", lambda e, kt=kt, hd=hd, j=j: e.matmul(psKV[:, hd * 128:(hd + 1) * 128], lhsT=kt[:, hd * 128:(hd + 1) * 128],
                                                                     rhs=vdb[:, j, hd * 128:(hd + 1) * 128], start=True, stop=True),
                         reads=[kt, vdb], writes=[psKV])
                O = psO[j % 2]
                sbc = sbf[st_["gch"] % 6]
                for hd in range(NH):
                    P.op("pe", lambda e, O=O, pt=pt, hd=hd, j=j: e.matmul(O[:, hd * 128:(hd + 1) * 128], lhsT=pt[:, hd * 128:(hd + 1) * 128],
                                                                          rhs=vb[:, j, hd * 128:(hd + 1) * 128], start=True, stop=False),
                         reads=[pt, vb], writes=[O])
                    P.op("pe", lambda e, O=O, sbc=sbc, hd=hd, j=j: e.matmul(O[:, hd * 128:(hd + 1) * 128], lhsT=qd[:, hd, j * 128:(j + 1) * 128],
                                                                            rhs=sbc[:, hd * 128:(hd + 1) * 128], start=False, stop=True),
                         reads=[qd, sbc], writes=[O])
                P.op("act", lambda e, O=O, j=j: e.activation(out=osb[:, j, :], in_=O[:, :], func=AF.Copy), reads=[O], writes=[osb])
                sbn = sbf[(st_["gch"] + 1) % 6]
                for hd in range(NH):
                    P.op("dve", lambda e, hd=hd: e.scalar_tensor_tensor(out=state[:, hd * 128:(hd + 1) * 128], in0=state[:, hd * 128:(hd + 1) * 128],
                                                                       scalar=CDEC[hd], in1=psKV[:, hd * 128:(hd + 1) * 128], op0=ALU.mult, op1=ALU.add),
                         reads=[state, psKV], writes=[state])
                P.op("act", lambda e, sbn=sbn: e.activation(out=sbn[:], in_=state[:], func=AF.Copy), reads=[state], writes=[sbn])
                st_["gch"] += 1
            P.op("act", lambda e: e.activation(out=xh[:, :, 0:16], in_=xh[:, :, TB:TB + 16], func=AF.Copy), reads=[xh], writes=[xh])

        def partT(b):
            t0 = b * TB
            i = b % 2
            o3 = osb.t[:].rearrange("p j (h e) -> p (j h) e", h=NH)
            q3 = sq.t[:].rearrange("p j (h e) -> p (j h) e", h=NH)
            P.op("dve", lambda e: e.tensor_reduce(out=s1[:], in_=o3, axis=AX.X, op=ALU.add), reads=[osb], writes=[s1])
            P.op("pool", lambda e: e.tensor_tensor(out=sq[:], in0=osb[:], in1=osb[:], op=ALU.mult), reads=[osb], writes=[sq])
            yield
            for f_ in st_.get("pool_q", []):
                f_()
                yield
            st_["pool_q"] = []
            P.op("dve", lambda e: e.tensor_reduce(out=s2[:], in_=q3, axis=AX.X, op=ALU.add), reads=[sq], writes=[s2])
            P.op("dve", lambda e: e.tensor_scalar(out=mean[:], in0=s1[:], scalar1=1.0 / DH, scalar2=None, op0=ALU.mult), reads=[s1], writes=[mean])
            P.op("dve", lambda e: e.tensor_tensor(out=msq[:], in0=mean[:], in1=mean[:], op=ALU.mult), reads=[mean], writes=[msq])
            P.op("dve", lambda e: e.scalar_tensor_tensor(out=var[:], in0=s2[:], scalar=1.0 / DH, in1=msq[:], op0=ALU.mult, op1=ALU.subtract),
                 reads=[s2, msq], writes=[var])
            P.op("dve", lambda e: e.tensor_scalar_add(out=var[:], in0=var[:], scalar1=EPS), reads=[var], writes=[var])
            P.op("dve", lambda e: e.tensor_scalar(out=msq[:], in0=mean[:], scalar1=-1.0, scalar2=None, op0=ALU.mult), reads=[mean], writes=[msq])
            P.op("pool", lambda e: e.tensor_tensor(out=rstd[:], in0=var[:], in1=nh[:], op=ALU.pow), reads=[var, nh], writes=[rstd])
            P.op("pool", lambda e: e.tensor_tensor(out=nb[:], in0=msq[:], in1=rstd[:], op=ALU.mult), reads=[msq, rstd], writes=[nb])
            yield
            P.op("pool", lambda e: e.tensor_tensor(out=q3, in0=o3, in1=rstd[:].unsqueeze(2).to_broadcast([128, 16, DH]), op=ALU.mult),
                 reads=[osb, rstd], writes=[sq])
            yield
            P.op("pool", lambda e: e.tensor_tensor(out=q3, in0=q3, in1=nb[:].unsqueeze(2).to_broadcast([128, 16, DH]), op=ALU.add),
                 reads=[sq, nb], writes=[sq])
            yield
            P.op("dve", lambda e: e.tensor_tensor(out=sq[:], in0=sq[:], in1=gsg[:], op=ALU.mult), reads=[sq, gsg], writes=[sq])
            for _ in range(12):
                yield
            for j in range(NJ):
                X = psS[j % 2]
                for hd in range(NH):
                    P.op("pe", lambda e, X=X, hd=hd, j=j: e.transpose(out=X[:, hd * 128:(hd + 1) * 128], in_=sq[:, j, hd * 128:(hd + 1) * 128],
                                                                      identity=identF[:]), reads=[sq, identF], writes=[X])
                outap = cat[i][:, 0:4, j * 128:(j + 1) * 128]
                inap = X[:, :].rearrange("p (h t) -> p h t", h=NH)
                if j % 2 == 0:
                    P.op("dve", lambda e, outap=outap, inap=inap: e.tensor_copy(out=outap, in_=inap), reads=[X], writes=[cat[i]])
                else:
                    P.op("act", lambda e, outap=outap, inap=inap: e.activation(out=outap, in_=inap, func=AF.Copy), reads=[X], writes=[cat[i]])
                yield
            P.dma("sp", G.catT.t[:, :, t0:t0 + TB].rearrange("c p t -> p c t"), cat[i][:], catd[i], reads=[cat[i]], writes=[G.catT_b[b]])

        def interleave(*gens):
            gens = [g for g in gens if g is not None]
            while gens:
                for g in list(gens):
                    try:
                        next(g)
                    except StopIteration:
                        gens.remove(g)

        loads(0)
        interleave(partA(0))
        for b in range(NB):
            partB(b)
            interleave(partT(b), partA(b + 1) if b + 1 < NB else None)
        P.end_phase()
    P.scope = P.es


def build_program(NS, S, layers, debug=False):
    nc = bass.Bass("TRN2", target_bir_lowering=False)
    P = Prog(nc)
    G = Ctx()
    G.NS, G.S = NS, S
    G.NT = NS * S
    G.NB = G.NT // TB
    NT, NB = G.NT, G.NB
    ext = lambda name, shape, dt=F32: P.dram(name, shape, dt, kind="ExternalInput")
    G.x = ext("x", [NT, D])
    G.pos = ext("pos", [NS, S], I32)
    G.consts = ext("consts", [128, C_END])
    G.ev_w_in = ext("ev_w_in", [2, D, 2560])
    G.ev_ret_norm_g = ext("ev_ret_norm_g", [2, 512])
    G.ev_pool_w = ext("ev_pool_w", [2, 4, 128, 128])
    G.ev_pool_scale = ext("ev_pool_scale", [2, 128, 4])
    G.ev_w_out = ext("ev_w_out", [2, D, D])
    G.od_w_in = ext("od_w_in", [2, D, 2560])
    G.od_small = ext("od_small", [2, 128, 36])
    G.od_gate_a_w = ext("od_gate_a_w", [2, 8, 64, 64])
    G.od_gate_x_w = ext("od_gate_x_w", [2, 8, 64, 64])
    G.od_w_out = ext("od_w_out", [2, D, D])
    G.ffn_w_in = ext("ffn_w_in", [DEPTH, D, 2 * DFF])
    G.ffn_w_out = ext("ffn_w_out", [DEPTH, DFF, D])
    G.ln_g = ext("ln_g", [DEPTH, 2, D])
    G.ln_b = ext("ln_b", [DEPTH, 2, D])
    G.out = P.dram("out", [NT, D], F32, kind="ExternalOutput")
    dk = "ExternalOutput" if debug else "Internal"
    G.hA = P.dram("hA", [NT, D], F32, kind=dk)
    G.hB = P.dram("hB", [NT, D], F32, kind=dk)
    G.catT = P.dram("catT", [8, 128, NT], BF16, kind=dk)
    G.cosT = P.dram("cosT", [NS, 128, S], F32, kind=dk)
    G.sinT = P.dram("sinT", [NS, 128, S], F32, kind=dk)
    G.qT = P.dram("qT", [NH, 128, NT], BF16, kind=dk)
    G.kT = P.dram("kT", [NH, 128, NT], BF16, kind=dk)
    G.vd = P.dram("vd", [NT, 512], BF16, kind=dk)
    G.x_b = blk_bufs("x_b", G.x.t, NB)
    G.hA_b = blk_bufs("hA_b", G.hA.t, NB)
    G.hB_b = blk_bufs("hB_b", G.hB.t, NB)
    G.out_b = blk_bufs("out_b", G.out.t, NB)
    G.catT_b = blk_bufs("catT_b", G.catT.t, NB)
    G.qkv_b = blk_bufs("qkv_b", G.qT.t, NB)

    phase_R(P, G)
    fin = []
    hin, hin_b = G.x, G.x_b
    for li, layer in enumerate(layers):
        l2 = layer // 2
        last = (li == len(layers) - 1)
        if layer % 2 == 0:
            phase_E(P, G, l2, hin, hin_b)
            w_out = G.ev_w_out
        else:
            phase_O1(P, G, l2, hin, hin_b)
            phase_O2(P, G)
            w_out = G.od_w_out
        phase_P(P, G, w_out.t[l2], w_out, G.ln_g.t[layer, 0:1, :], G.ln_b.t[layer, 0:1, :], G.ln_g, hin, hin_b, G.hB, G.hB_b)
        hout, hout_b = (G.out, G.out_b) if last else (G.hA, G.hA_b)
        fin = phase_F(P, G, G.ffn_w_in.t[layer], G.ffn_w_in, G.ffn_w_out.t[layer], G.ffn_w_out,
                      G.ln_g.t[layer, 1:2, :], G.ln_b.t[layer, 1:2, :], G.ln_g, G.hB, G.hB_b, hout, hout_b)
        hin, hin_b = G.hA, G.hA_b
    P.finish(fin)
    return nc, P


def host_prep(inp):
    f = lambda a: np.ascontiguousarray(np.asarray(a, dtype=np.float32))
    swap = np.concatenate([np.arange(h * 128 + 64, h * 128 + 128).tolist() + np.arange(h * 128, h * 128 + 64).tolist() for h in range(NH)]).astype(np.int64)
    ev = f(inp["ev_w_in"])
    ev_ext = ev
    od = f(inp["od_w_in"])
    od_ext = od
    psc = f(inp["ev_pool_scale"]).reshape(2, 4, 128).transpose(0, 2, 1)
    cw = f(inp["od_conv_w"])
    cols = [cw[:, k, :] for k in range(4)] + [f(inp["od_conv_b"]), f(inp["od_gate_a_b"]), f(inp["od_gate_x_b"]), f(inp["od_lru_lambda"]),
                                              f(inp["od_conv_b"])]
    sm = np.stack(cols, axis=-1)
    sm = sm.reshape(2, 4, 128, 9).transpose(0, 2, 1, 3).reshape(2, 128, 36)
    shared = {
        "consts": make_consts(),
        "ev_w_in": np.ascontiguousarray(ev_ext), "ev_ret_norm_g": f(inp["ev_ret_norm_g"]), "ev_pool_w": f(inp["ev_pool_w"]),
        "ev_pool_scale": np.ascontiguousarray(psc), "ev_w_out": f(inp["ev_w_out"]),
        "od_w_in": np.ascontiguousarray(od_ext), "od_small": np.ascontiguousarray(sm),
        "od_gate_a_w": f(inp["od_gate_a_w"]), "od_gate_x_w": f(inp["od_gate_x_w"]), "od_w_out": f(inp["od_w_out"]),
        "ffn_w_in": f(inp["ffn_w_in"]), "ffn_w_out": f(inp["ffn_w_out"]), "ln_g": f(inp["ln_g"]), "ln_b": f(inp["ln_b"]),
    }
    return shared


def kernel(**inp):
    x = np.asarray(inp["x"], dtype=np.float32)
    pos = np.asarray(inp["positions"], dtype=np.int32)
    B, S, _ = x.shape
    ncores = 8
    NS = B // ncores
    shared = host_prep(inp)
    nc, P = build_program(NS, S, list(range(DEPTH)))
    in_maps = []
    for c in range(ncores):
        m = dict(shared)
        m["x"] = np.ascontiguousarray(x[c * NS:(c + 1) * NS].reshape(NS * S, D))
        m["pos"] = np.ascontiguousarray(pos[c * NS:(c + 1) * NS])
        in_maps.append(m)
    res = run_bass_kernel_spmd(nc, in_maps, core_ids=list(range(ncores)))
    out = np.concatenate([np.asarray(r["out"], dtype=np.float32).reshape(NS, S, D) for r in res.results], axis=0)
    return out


def phase_O1(P, G, l2, hin, hin_b):
    S, NS, NB = G.S, G.NS, G.NB
    WC = 2560
    with contextlib.ExitStack() as sc:
        P.scope = sc
        wi = P.sbuf("O_wi", [128, 8, WC], BF16)
        wi_k = [Buf(f"O_wi{k}", wi.t) for k in range(8)]
        bulk_load(P, [(G.od_w_in.t[l2, k * 128:(k + 1) * 128, q * 1280:(q + 1) * 1280], G.od_w_in,
                       wi.t[:, k, q * 1280:(q + 1) * 1280], wi_k[k], 1280) for k in range(8) for q in range(2)], 1280)
        gst = P.sbuf("O_gst", [128, 4, 128], F32)
        gsd = P.dsem()
        ga = P.sbuf("O_ga", [128, 4, 128], BF16)
        gx = P.sbuf("O_gx", [128, 4, 128], BF16)
        P.op("pool", lambda e: e.memset(gst[:], 0.0), writes=[gst])
        for wsrc, dst in ((G.od_gate_a_w, ga), (G.od_gate_x_w, gx)):
            v4 = wsrc.t[l2].rearrange("(c two) i d -> two i c d", two=2)
            P.dma("sp", gst[0:64, :, 0:64], v4[0], gsd, reads=[wsrc], writes=[gst])
            P.dma("sp", gst[64:128, :, 64:128], v4[1], gsd, reads=[wsrc], writes=[gst])
            P.op("dve", lambda e, dst=dst: e.tensor_copy(out=dst[:], in_=gst[:]), reads=[gst], writes=[dst])
        spar = P.sbuf("O_spar", [128, 36], F32)
        P.dma("sp", spar[:], G.od_small.t[l2], P.dsem(), reads=[G.od_small], writes=[spar])
        sp3 = spar.t[:].rearrange("p (c k) -> p c k", k=9)
        sm_ = {n: P.sbuf(f"O_{n}", [128, 4], F32) for n in ("ex", "den", "z", "z2", "p1", "zp", "c8", "c16")}
        ex, den, z, z2, p1, zp, c8, c16 = (sm_[n] for n in ("ex", "den", "z", "z2", "p1", "zp", "c8", "c16"))
        P.op("act", lambda e: e.activation(out=ex[:], in_=sp3[:, :, 7], func=AF.Exp, scale=-1.0), reads=[spar], writes=[ex])
        P.op("dve", lambda e: e.tensor_scalar_add(out=den[:], in0=ex[:], scalar1=2.0), reads=[ex], writes=[den])
        P.op("dve", lambda e: e.reciprocal(out=den[:], in_=den[:]), reads=[den], writes=[den])
        P.op("dve", lambda e: e.tensor_tensor(out=z[:], in0=ex[:], in1=den[:], op=ALU.mult), reads=[ex, den], writes=[z])
        P.op("dve", lambda e: e.tensor_tensor(out=z2[:], in0=z[:], in1=z[:], op=ALU.mult), reads=[z], writes=[z2])
        P.op("dve", lambda e: e.tensor_scalar(out=p1[:], in0=z2[:], scalar1=1.0 / 7.0, scalar2=1.0 / 5.0, op0=ALU.mult, op1=ALU.add), reads=[z2], writes=[p1])
        P.op("dve", lambda e: e.tensor_tensor(out=p1[:], in0=p1[:], in1=z2[:], op=ALU.mult), reads=[p1, z2], writes=[p1])
        P.op("dve", lambda e: e.tensor_scalar_add(out=p1[:], in0=p1[:], scalar1=1.0 / 3.0), reads=[p1], writes=[p1])
        P.op("dve", lambda e: e.tensor_tensor(out=p1[:], in0=p1[:], in1=z2[:], op=ALU.mult), reads=[p1, z2], writes=[p1])
        P.op("dve", lambda e: e.scalar_tensor_tensor(out=zp[:], in0=p1[:], scalar=1.0, in1=z[:], op0=ALU.add, op1=ALU.mult), reads=[p1, z], writes=[zp])
        P.op("dve", lambda e: e.tensor_scalar(out=c8[:], in0=zp[:], scalar1=-16.0, scalar2=None, op0=ALU.mult), reads=[zp], writes=[c8])
        P.op("dve", lambda e: e.tensor_scalar(out=c16[:], in0=zp[:], scalar1=-32.0, scalar2=None, op0=ALU.mult), reads=[zp], writes=[c16])
        ident = P.sbuf("O_id", [128, 128], F32)
        P.dma("sp", ident[:], G.consts.t[:, C_ID:C_ID + 128], P.dsem(), reads=[G.consts], writes=[ident])
        Rm32 = P.sbuf("O_Rm32", [128, 128], F32)
        P.dma("sp", Rm32[:], G.consts.t[:, C_RM:C_RM + 128], P.dsem(), reads=[G.consts], writes=[Rm32])
        Rm = P.sbuf("O_Rm", [128, 128], BF16)
        P.op("dve", lambda e: e.tensor_copy(out=Rm[:], in_=Rm32[:]), reads=[Rm32], writes=[Rm])
        xqs = [P.sbuf(f"O_xq{i}", [128, TB], BF16) for i in range(2)]
        half = P.sbuf("O_half", [128, TB], F32)
        P.op("pool", lambda e: e.memset(half[:], 0.5), writes=[half])

        cs = P.sbuf("O_cs", [128, TB], F32)
        sn = P.sbuf("O_sn", [128, TB], F32)
        csd = P.dsem()
        snd = P.dsem()
        hb = [P.sbuf(f"O_h{j}", [128, D], F32) for j in range(NJ)]
        hds = [P.dsem() for _ in range(NJ)]
        hT = P.sbuf("O_hT", [128, 8, TB], BF16)
        hT_k = [Buf(f"O_hT{k}", hT.t) for k in range(8)]
        gl = P.sbuf("O_gl", [128, 4, TB], F32)
        uh = P.sbuf("O_uh", [128, 4, 3 + TB], F32)
        uc = P.sbuf("O_uc", [128, 4, TB], F32)
        ucb = P.sbuf("O_ucb", [128, 4, TB], BF16)
        uc_c = [Buf(f"O_uc{c}", uc.t) for c in range(4)]
        ucb_c = [Buf(f"O_ucb{c}", ucb.t) for c in range(4)]
        rr = P.sbuf("O_rr", [128, 4, TB], F32)
        ii = P.sbuf("O_ii", [128, 4, TB], F32)
        aa = [P.sbuf(f"O_aa{i}", [128, TB], F32) for i in range(4)]
        w1 = [P.sbuf(f"O_w1{i}", [128, TB], F32) for i in range(4)]
        hs = [P.sbuf(f"O_hs{i}", [128, TB], F32) for i in range(2)]
        carry = P.sbuf("O_carry", [128, 4], F32)
        catl = P.sbuf("O_catl", [128, 4, TB], BF16)
        catd = P.dsem()
        qr = P.sbuf("O_qr", [128, NH, TB], BF16)
        kr = P.sbuf("O_kr", [128, NH, TB], BF16)
        qd_, kd_ = P.dsem(), P.dsem()
        t1s = [P.sbuf(f"O_t1{i}", [128, TB], F32) for i in range(2)]
        t2s = [P.sbuf(f"O_t2{i}", [128, TB], F32) for i in range(2)]
        vbt = P.sbuf("O_vbt", [128, NJ, 512], BF16)
        vds = P.dsem()
        psA = [P.psum(f"O_psA{i}", [128, 512], F32) for i in range(4)]
        psR = [P.psum(f"O_psR{i}", [128, 512], F32) for i in range(2)]
        psI = [P.psum(f"O_psI{i}", [128, 512], F32) for i in range(2)]

        def loads(b):
            t0 = b * TB
            s = t0 // S
            ts = t0 - s * S
            P.dma("sp", cs[:], G.cosT.t[s, :, ts:ts + TB], csd, reads=[G.cosT], writes=[cs])
            P.dma("sp", sn[:], G.sinT.t[s, :, ts:ts + TB], snd, reads=[G.sinT], writes=[sn])
            for j in range(NJ):
                P.dma("sp", hb[j][:], hin.t[t0 + j * 128:t0 + (j + 1) * 128, :], hds[j], reads=[hin_b[b]], writes=[hb[j]])

        def proj(X, c0, tokmajor=False, j=0):
            for k in range(8):
                if tokmajor:
                    P.op("pe", lambda e, k=k: e.matmul(X[:, :], lhsT=hT.t[:, k, j * 128:(j + 1) * 128], rhs=wi.t[:, k, c0:c0 + 512],
                                                       start=(k == 0), stop=(k == 7)), reads=[wi_k[k], hT_k[k]], writes=[X])
                else:
                    P.op("pe", lambda e, k=k: e.matmul(X[:, :], lhsT=wi.t[:, k, c0:c0 + 128], rhs=hT.t[:, k, :],
                                                       start=(k == 0), stop=(k == 7)), reads=[wi_k[k], hT_k[k]], writes=[X])

        st_ = {"na": 0}

        def nextA():
            X = psA[st_["na"] % 4]
            st_["na"] += 1
            return X

        def partA(b):
            t0 = b * TB
            for c0_ in (0, 4):
                for c in range(c0_, c0_ + 4):
                    pt = nextA()
                    for j in range(NJ):
                        P.op("pe", lambda e, pt=pt, j=j, c=c: e.transpose(out=pt[:, j * 128:(j + 1) * 128], in_=hb[j][:, c * 128:(c + 1) * 128],
                                                                          identity=ident[:]), reads=[hb[j], ident], writes=[pt])
                    if c % 2 == 0:
                        P.op("dve", lambda e, pt=pt, c=c: e.tensor_copy(out=hT.t[:, c, :], in_=pt[:, :]), reads=[pt], writes=[hT_k[c]])
                    else:
                        P.op("act", lambda e, pt=pt, c=c: e.activation(out=hT.t[:, c, :], in_=pt[:, :], func=AF.Copy), reads=[pt], writes=[hT_k[c]])
                yield
            items = [(col, dstb, hd) for col, dstb in ((1024, qr), (1536, kr)) for hd in range(NH)]
            for p0 in range(0, len(items), 2):
                pair = []
                for nr, (col, dstb, hd) in enumerate(items[p0:p0 + 2]):
                    X = nextA()
                    proj(X, col + hd * 128)
                    xq = xqs[nr]
                    P.op("act", lambda e, X=X, xq=xq: e.activation(out=xq[:], in_=X[:, :], func=AF.Copy), reads=[X], writes=[xq])
                    pair.append((dstb, hd, X, xq, nr))
                yield
                for dstb, hd, X, xq, nr in pair:
                    Y = nextA()
                    P.op("pe", lambda e, Y=Y, xq=xq: e.matmul(Y[:, :], lhsT=Rm[:], rhs=xq[:], start=True, stop=True), reads=[Rm, xq], writes=[Y])
                    t1, t2 = t1s[nr], t2s[nr]
                    rope_head(P, X, Y, cs, sn, t1, t2)
                    P.op("act", lambda e, hd=hd, dstb=dstb, t1=t1: e.activation(out=dstb[:, hd, :], in_=t1[:], func=AF.Copy), reads=[t1], writes=[dstb])
                yield
            P.dma("sp", G.qT.t[:, :, t0:t0 + TB].rearrange("h p t -> p h t"), qr[:], qd_, reads=[qr], writes=[G.qkv_b[b]])
            P.dma("sp", G.kT.t[:, :, t0:t0 + TB].rearrange("h p t -> p h t"), kr[:], kd_, reads=[kr], writes=[G.qkv_b[b]])
            if b + 1 < NB:
                loads(b + 1)

        def partB(b):
            t0 = b * TB
            first = (t0 % S == 0)
            if first:
                P.op("pool", lambda e: e.memset(uh[:, :, 0:3], 0.0), writes=[uh])
                P.op("pool", lambda e: e.memset(carry[:], 0.0), writes=[carry])
            for c in range(4):
                X = nextA()
                proj(X, 512 + c * 128)
                P.op("act", lambda e, X=X, c=c: e.activation(out=uh[:, c, 3:3 + TB], in_=X[:, :], func=AF.Copy), reads=[X], writes=[uh])
            for c in range(4):
                X = nextA()
                proj(X, c * 128)
                P.op("act", lambda e, X=X, c=c: e.activation(out=gl[:, c, :], in_=X[:, :], func=AF.Gelu_apprx_tanh), reads=[X], writes=[gl])
                P.op("dve", lambda e, c=c: e.tensor_scalar(out=uc[:, c, :], in0=uh[:, c, 0:TB], scalar1=sp3[:, c, 0:1], scalar2=sp3[:, c, 4:5],
                                                          op0=ALU.mult, op1=ALU.add), reads=[uh, spar], writes=[uc_c[c]])
                for k in range(1, 4):
                    P.op("dve", lambda e, c=c, k=k: e.scalar_tensor_tensor(out=uc[:, c, :], in0=uh[:, c, k:k + TB], scalar=sp3[:, c, k:k + 1],
                                                                          in1=uc[:, c, :], op0=ALU.mult, op1=ALU.add), reads=[uh, spar, uc_c[c]], writes=[uc_c[c]])
                P.op("act", lambda e, c=c: e.activation(out=ucb[:, c, :], in_=uc[:, c, :], func=AF.Copy), reads=[uc_c[c]], writes=[ucb_c[c]])
            P.op("act", lambda e: e.activation(out=uh[:, :, 0:3], in_=uh[:, :, TB:TB + 3], func=AF.Copy), reads=[uh], writes=[uh])
            for j in range(NJ):
                X = nextA()
                proj(X, 2048, tokmajor=True, j=j)
                P.op("act", lambda e, X=X, j=j: e.activation(out=vbt[:, j, :], in_=X[:, :], func=AF.Copy), reads=[X], writes=[vbt])
            P.dma("sp", G.vd.t[t0:t0 + TB, :].rearrange("(j p) f -> p j f", p=128), vbt[:], vds, reads=[vbt], writes=[G.qkv_b[b]])
            for c in range(4):
                R_, I_ = psR[c % 2], psI[c % 2]
                P.op("pe", lambda e, R_=R_, c=c: e.matmul(R_[:, :], lhsT=ga[:, c, :], rhs=ucb[:, c, :], start=True, stop=True), reads=[ga, ucb_c[c]], writes=[R_])
                P.op("pe", lambda e, I_=I_, c=c: e.matmul(I_[:, :], lhsT=gx[:, c, :], rhs=ucb[:, c, :], start=True, stop=True), reads=[gx, ucb_c[c]], writes=[I_])
                P.op("act", lambda e, R_=R_, c=c: e.activation(out=rr[:, c, :], in_=R_[:, :], func=AF.Sigmoid, bias=sp3[:, c, 5:6]), reads=[R_, spar], writes=[rr])
                P.op("act", lambda e, I_=I_, c=c: e.activation(out=ii[:, c, :], in_=I_[:, :], func=AF.Sigmoid, bias=sp3[:, c, 6:7]), reads=[I_, spar], writes=[ii])

        def partT(b):
            t0 = b * TB
            for c in range(4):
                a_, w_ = aa[c], w1[c]
                P.op("act", lambda e, a_=a_, c=c: e.activation(out=a_[:], in_=rr[:, c, :], func=AF.Exp, scale=c8[:, c:c + 1]), reads=[rr, c8], writes=[a_])
                P.op("act", lambda e, w_=w_, c=c: e.activation(out=w_[:], in_=rr[:, c, :], func=AF.Exp, scale=c16[:, c:c + 1]), reads=[rr, c16], writes=[w_])
                P.op("act", lambda e, w_=w_: e.activation(out=w_[:], in_=w_[:], func=AF.Ln, scale=-1.0, bias=1.0), reads=[w_], writes=[w_])
                P.op("act", lambda e, w_=w_: e.activation(out=w_[:], in_=w_[:], func=AF.Exp, scale=0.5), reads=[w_], writes=[w_])
                P.op("pool", lambda e, c=c: e.tensor_tensor(out=ii[:, c, :], in0=ii[:, c, :], in1=uc[:, c, :], op=ALU.mult), reads=[ii, uc_c[c]], writes=[ii])
                if c % 2 == 1:
                    yield
            for c in range(4):
                a_, w_, h_ = aa[c], w1[c], hs[c % 2]
                P.op("dve", lambda e, w_=w_, c=c: e.tensor_tensor(out=w_[:], in0=w_[:], in1=ii[:, c, :], op=ALU.mult), reads=[w_, ii], writes=[w_])
                P.op("dve", lambda e, a_=a_, w_=w_, h_=h_, c=c: e.tensor_tensor_scan(out=h_[:], data0=a_[:], data1=w_[:], initial=carry[:, c:c + 1],
                                                                                   op0=ALU.mult, op1=ALU.add), reads=[a_, w_, carry], writes=[h_])
                P.op("act", lambda e, h_=h_, c=c: e.activation(out=carry[:, c:c + 1], in_=h_[:, TB - 1:TB], func=AF.Copy), reads=[h_], writes=[carry])
                P.op("pool", lambda e, h_=h_, c=c: e.tensor_tensor(out=catl[:, c, :], in0=h_[:], in1=gl[:, c, :], op=ALU.mult), reads=[h_, gl], writes=[catl])
                yield
            P.dma("sp", G.catT.t[0:4, :, t0:t0 + TB].rearrange("c p t -> p c t"), catl[:], catd, reads=[catl], writes=[G.catT_b[b]])

        def interleave(*gens):
            gens = [g for g in gens if g is not None]
            while gens:
                for g in list(gens):
                    try:
                        next(g)
                    except StopIteration:
                        gens.remove(g)

        loads(0)
        interleave(partA(0))
        for b in range(NB):
            partB(b)
            interleave(partT(b), partA(b + 1) if b + 1 < NB else None)
        P.end_phase()
    P.scope = P.es


def phase_O2(P, G):
    S, NS, NB = G.S, G.NS, G.NB
    BPS = S // TB
    SKEW = 3
    NPT = SKEW + 3
    scale = float(DH ** -0.5)
    with contextlib.ExitStack() as sc:
        P.scope = sc
        mk32 = P.sbuf("A_mk32", [128, 256], F32)
        P.dma("sp", mk32[:], G.consts.t[:, C_MASK:C_MASK + 256], P.dsem(), reads=[G.consts], writes=[mk32])
        mask = P.sbuf("A_mask", [128, 256], BF16)
        P.op("dve", lambda e: e.tensor_copy(out=mask[:], in_=mk32[:]), reads=[mk32], writes=[mask])
        ones = P.sbuf("A_ones", [128, 128], BF16)
        P.op("pool", lambda e: e.memset(ones[:], 1.0), writes=[ones])
        qT = [P.sbuf(f"A_qT{i}", [128, S], BF16) for i in range(2)]
        kT = [P.sbuf(f"A_kT{i}", [128, S], BF16) for i in range(2)]
        qkd = [P.dsem() for _ in range(2)]
        vs = [P.sbuf(f"A_vs{i}", [128, S // 128, 128], BF16) for i in range(2)]
        vsd = [P.dsem() for _ in range(2)]
        vpieces = {id(v): [Buf(f"A_vp{i}_{k}", v.t) for k in range(16)] for i, v in enumerate(vs)}
        nums = [P.sbuf(f"A_num{i}", [128, S], F32) for i in range(2)]
        dens = [P.sbuf(f"A_den{i}", [128, S], F32) for i in range(2)]
        rden = P.sbuf("A_rden", [128, S], F32)
        y = P.sbuf("A_y", [128, S], BF16)
        yl, yh = Buf("A_yl", y.t), Buf("A_yh", y.t)
        yd = P.dsem()
        ebf = [P.sbuf(f"A_eb{i}", [128, 256], BF16) for i in range(3)]
        PT = [P.sbuf(f"A_PT{i}", [128, 256], BF16) for i in range(NPT)]
        psS = [P.psum(f"A_psS{i}", [128, 512], F32) for i in range(3)]
        psN = [P.psum(f"A_psN{i}", [128, 512], F32) for i in range(2)]
        psD = [P.psum(f"A_psD{i}", [128, 512], F32) for i in range(2)]
        vcount = 0
        gcount = 0
        iters = [(s, hd) for s in range(NS) for hd in range(NH)]

        def qk_load(n):
            s, hd = iters[n]
            sblk = [G.qkv_b[s * BPS + bb] for bb in range(BPS)]
            i = n % 2
            P.dma("sp", qT[i][:], G.qT.t[hd, :, s * S:(s + 1) * S], qkd[i], reads=sblk, writes=[qT[i]])
            P.dma("sp", kT[i][:], G.kT.t[hd, :, s * S:(s + 1) * S], qkd[i], reads=sblk, writes=[kT[i]])

        def build_per(n):
            nonlocal vcount
            s, hd = iters[n]
            sblk = [G.qkv_b[s * BPS + bb] for bb in range(BPS)]
            per = []
            for pi, (W, dil) in enumerate(DIL_PATTERNS):
                nb = S // dil // 128
                v_ = vs[vcount % 2]
                vdm = vsd[vcount % 2]
                vcount += 1
                lds, bks = [], []
                src4 = G.vd.t[s * S:(s + 1) * S, hd * 128:(hd + 1) * 128].rearrange("(kb p r) e -> p r kb e", p=128, r=dil)
                v4 = v_.t[:].rearrange("p (r kb) e -> p r kb e", r=dil)
                npc = dil if dil <= nb else nb
                pcs = vpieces[id(v_)][:npc]
                if dil <= nb:
                    for r in range(dil):
                        lds.append(("load", pcs[r], vdm, v4[:, r, :, :], src4[:, r, :, :], sblk))
                else:
                    for kb in range(nb):
                        lds.append(("load", pcs[kb], vdm, v4[:, :, kb, :], src4[:, :, kb, :], sblk))
                for r in range(dil):
                    for kb in range(nb):
                        bks.append(("blk", pi, dil, nb, r, kb, v_, pcs))
                per.append((lds, bks))
            return per

        pers = [build_per(n) for n in range(len(iters))]
        qk_load(0)
        for n, (s, hd) in enumerate(iters):
            if True:
                sblk = [G.qkv_b[s * BPS + bb] for bb in range(BPS)]
                i = n % 2
                q_, k_ = qT[i], kT[i]
                num, den = nums[i], dens[i]
                if n + 1 < len(iters):
                    qk_load(n + 1)
                per = pers[n]
                tasks = list(per[0][0]) if n == 0 else []
                for pi in range(len(per)):
                    bks = per[pi][1]
                    tasks += bks[:5]
                    if pi + 1 < len(per):
                        tasks += per[pi + 1][0]
                    elif n + 1 < len(iters):
                        tasks += pers[n + 1][0][0]
                    tasks += bks[5:]
                blks = [t for t in tasks if t[0] == "blk"]
                pending = []
                ptmap = {}
                bi = 0

                def stage2(idx):
                    nonlocal gcount
                    _, pi, dil, nb, r, qb, v_, pcs = blks[idx]
                    gi = qb % 4
                    N, Dn = psN[gcount % 2], psD[gcount % 2]
                    reg = slice(gi * 128, (gi + 1) * 128)
                    ptc = ptmap[idx]
                    if qb >= 1:
                        ptp = ptmap[idx - 1]
                        for dst, lw, lwB in ((N, v_, None), (Dn, None, ones)):
                            l0 = v_[:, r * nb + qb - 1, :] if lwB is None else ones[:]
                            l1 = v_[:, r * nb + qb, :] if lwB is None else ones[:]
                            rb = pcs if lwB is None else [ones]
                            P.op("pe", lambda e, dst=dst, l0=l0, ptp=ptp: e.matmul(dst[:, reg], lhsT=l0, rhs=ptp[:, 128:256], start=True, stop=False),
                                 reads=list(rb) + [ptp], writes=[dst])
                            P.op("pe", lambda e, dst=dst, l1=l1, ptc=ptc: e.matmul(dst[:, reg], lhsT=l1, rhs=ptc[:, 0:128], start=False, stop=True),
                                 reads=list(rb) + [ptc], writes=[dst])
                    else:
                        P.op("pe", lambda e: e.matmul(N[:, reg], lhsT=v_[:, r * nb + qb, :], rhs=ptc[:, 0:128], start=True, stop=True),
                             reads=list(pcs) + [ptc], writes=[N])
                        P.op("pe", lambda e: e.matmul(Dn[:, reg], lhsT=ones[:], rhs=ptc[:, 0:128], start=True, stop=True),
                             reads=[ones, ptc], writes=[Dn])
                    if gi == 3 or qb == nb - 1:
                        ng = gi + 1
                        qb0 = qb - gi
                        lo = r + dil * 128 * qb0
                        hi = lo + dil * (128 * ng - 1) + 1
                        nsl = num[:, lo:hi:dil]
                        dsl = den[:, lo:hi:dil]
                        if pi == 0:
                            P.op("act", lambda e: e.activation(out=nsl, in_=N[:, 0:ng * 128], func=AF.Copy), reads=[N], writes=[num])
                            P.op("dve", lambda e: e.tensor_copy(out=dsl, in_=Dn[:, 0:ng * 128]), reads=[Dn], writes=[den])
                        else:
                            P.op("dve", lambda e: e.tensor_tensor(out=nsl, in0=nsl, in1=N[:, 0:ng * 128], op=ALU.add), reads=[num, N], writes=[num])
                            P.op("dve", lambda e: e.tensor_tensor(out=dsl, in0=dsl, in1=Dn[:, 0:ng * 128], op=ALU.add), reads=[den, Dn], writes=[den])
                        gcount += 1

                for t in tasks:
                    if t[0] == "load":
                        _, v_, vdm, dst, src, sb_ = t
                        P.dma("sp", dst, src, vdm, reads=sb_, writes=[v_])
                        continue
                    _, pi, dil, nb, r, kb, v_, pcs = t
                    nq = 256 if kb < nb - 1 else 128
                    base = r + dil * 128 * kb
                    ksl = k_[:, base:base + dil * 127 + 1:dil]
                    qsl = q_[:, base:base + dil * (nq - 1) + 1:dil]
                    Sb = psS[bi % 3]
                    eb = ebf[bi % 3]
                    pt = PT[bi % NPT]
                    ptmap[bi] = pt
                    P.op("pe", lambda e, Sb=Sb, ksl=ksl, qsl=qsl, nq=nq: e.matmul(Sb[:, 0:nq], lhsT=ksl, rhs=qsl, start=True, stop=True),
                         reads=[k_, q_], writes=[Sb])
                    P.op("act", lambda e, Sb=Sb, eb=eb, nq=nq: e.activation(out=eb[:, 0:nq], in_=Sb[:, 0:nq], func=AF.Exp, scale=scale),
                         reads=[Sb], writes=[eb])
                    P.op("pool" if bi % 3 == 0 else "dve", lambda e, eb=eb, pt=pt, nq=nq: e.tensor_tensor(out=pt[:, 0:nq], in0=eb[:, 0:nq], in1=mask[:, 0:nq], op=ALU.mult),
                         reads=[eb, mask], writes=[pt])
                    pending.append(bi)
                    bi += 1
                    if len(pending) > SKEW:
                        stage2(pending.pop(0))
                while pending:
                    stage2(pending.pop(0))
                P.op("act", lambda e, den=den: e.activation(out=rden[:], in_=den[:], func=AF.Ln), reads=[den], writes=[rden])
                P.op("act", lambda e: e.activation(out=rden[:], in_=rden[:], func=AF.Exp, scale=-1.0), reads=[rden], writes=[rden])
                H2 = S // 4
                P.op("pool", lambda e, num=num: e.tensor_tensor(out=y[:, 0:H2], in0=num[:, 0:H2], in1=rden[:, 0:H2], op=ALU.mult), reads=[num, rden], writes=[yl])
                P.op("dve", lambda e, num=num: e.tensor_tensor(out=y[:, H2:S], in0=num[:, H2:S], in1=rden[:, H2:S], op=ALU.mult), reads=[num, rden], writes=[yh])
                P.dma("pool", G.catT.t[4 + hd, :, s * S:(s + 1) * S], y[:], yd, reads=[yl, yh], writes=[G.catT_b[s * BPS + bb] for bb in range(BPS)])
        P.end_phase()
    P.scope = P.es
```
